# Optimizing a Trainium2 kernel written in Bass

```python
import math
import jax, jax.numpy as jnp
from jax import lax
import numpy as np

D_MODEL = 2048
BATCH = 8
SEQ = 2048
DEPTH = 1

M_HEADS = 4
M_DH = 256
M_W = M_HEADS * M_DH
CONV_K = 4
CHUNK = 64
A_HEADS = 8
A_NOPE = 128
A_ROPE = 64
A_DV = 128
A_DQK = A_NOPE + A_ROPE
A_W = A_HEADS * A_DV
Q_LORA = 512
KV_LORA = 512
ROPE_THETA = 10000.0
Q_BLOCK = 128
N_MEM = 256
C_HEADS = 4
C_DH = 256
C_W = C_HEADS * C_DH
N_BRANCH = 3
EPS = 1e-6

IN_SPLITS = (2 * M_W, M_W, M_W, M_W, M_HEADS, M_HEADS,
             Q_LORA, KV_LORA, A_ROPE, A_W,
             C_W, C_W,
             N_BRANCH * D_MODEL)
IN_DIM = sum(IN_SPLITS)

kernel_name = "hybrid_mlstm_mla_memory_gated"


def rms_norm(x, g):
    xf = x.astype(jnp.float32)
    y = xf * lax.rsqrt(jnp.mean(xf * xf, axis=-1, keepdims=True) + EPS)
    return (y * g.astype(jnp.float32)).astype(x.dtype)


def split_cols(z):
    idx = np.cumsum(IN_SPLITS)[:-1].tolist()
    return jnp.split(z, idx, axis=-1)


def causal_depthwise_conv(x, w, b):
    S = x.shape[1]
    xp = jnp.pad(x, ((0, 0), (CONV_K - 1, 0), (0, 0)))
    y = b
    for j in range(CONV_K):
        y = y + w[j] * xp[:, j:j + S]
    return y


def apply_rope(x, cos, sin):
    xf = x.astype(jnp.float32)
    half = xf.shape[-1] // 2
    x1, x2 = xf[..., :half], xf[..., half:]
    return jnp.concatenate([x1 * cos - x2 * sin, x2 * cos + x1 * sin], axis=-1).astype(x.dtype)


def mlstm_chunkwise(q, k, v, ig, lf):
    B, S, H, d = q.shape
    nc = S // CHUNK

    def chunks4(t):
        return t.reshape(B, nc, CHUNK, H, t.shape[-1]).transpose(1, 0, 3, 2, 4)

    def chunks3(t):
        return t.reshape(B, nc, CHUNK, H).transpose(1, 0, 3, 2)

    tril = jnp.tril(jnp.ones((CHUNK, CHUNK), dtype=bool))

    def step(carry, xs):
        C, n, m = carry
        qc, kc, vc, ic, fc = xs
        b = jnp.cumsum(fc, axis=-1)
        g = b[..., -1]
        dmat = jnp.where(tril, b[..., :, None] - b[..., None, :] + ic[..., None, :], -jnp.inf)
        inter = b + m[..., None]
        m_t = jnp.maximum(inter, jnp.max(dmat, axis=-1))
        w_intra = jnp.exp(dmat - m_t[..., None])
        w_inter = jnp.exp(inter - m_t)
        s = jnp.einsum('bhtd,bhsd->bhts', qc, kc) * w_intra
        num = (w_inter[..., None] * jnp.einsum('bhtk,bhkv->bhtv', qc, C)
               + jnp.einsum('bhts,bhsv->bhtv', s, vc))
        den = w_inter * jnp.einsum('bhtk,bhk->bht', qc, n) + jnp.sum(s, axis=-1)
        h = num / jnp.maximum(jnp.abs(den), jnp.exp(-m_t))[..., None]
        a = g[..., None] - b + ic
        m_new = jnp.maximum(g + m, jnp.max(a, axis=-1))
        w_s = jnp.exp(a - m_new[..., None])
        decay = jnp.exp(g + m - m_new)
        C_new = decay[..., None, None] * C + jnp.einsum('bhs,bhsk,bhsv->bhkv', w_s, kc, vc)
        n_new = decay[..., None] * n + jnp.einsum('bhs,bhsk->bhk', w_s, kc)
        return (C_new, n_new, m_new), h

    init = (jnp.zeros((B, H, d, d), jnp.float32),
            jnp.zeros((B, H, d), jnp.float32),
            jnp.zeros((B, H), jnp.float32))
    xs = (chunks4(q), chunks4(k), chunks4(v), chunks3(ig), chunks3(lf))
    _, h = lax.scan(step, init, xs)
    return h.transpose(1, 0, 3, 2, 4).reshape(B, S, H, d)


def causal_block_attention(q, k, v):
    B, S, H, dqk = q.shape
    dv = v.shape[-1]
    nb = S // Q_BLOCK
    scale = 1.0 / math.sqrt(dqk)
    qb = q.reshape(B, nb, Q_BLOCK, H, dqk).transpose(1, 0, 3, 2, 4)
    k_pos = jnp.arange(S)

    def attend(args):
        qi, blk = args
        s = jnp.einsum('bhqd,bshd->bhqs', qi, k).astype(jnp.float32) * scale
        q_pos = blk * Q_BLOCK + jnp.arange(Q_BLOCK)
        s = jnp.where(k_pos[None, :] <= q_pos[:, None], s, -jnp.inf)
        p = jax.nn.softmax(s, axis=-1)
        return jnp.einsum('bhqs,bshd->bqhd', p.astype(v.dtype), v)

    out = lax.map(attend, (qb, jnp.arange(nb)))
    return out.transpose(1, 0, 2, 3, 4).reshape(B, S, H * dv)


def memory_attention(q, k, v):
    B, S, H, d = q.shape
    s = jnp.einsum('bshd,bmhd->bhsm', q, k).astype(jnp.float32) * (1.0 / math.sqrt(d))
    p = jax.nn.softmax(s, axis=-1)
    return jnp.einsum('bhsm,bmhd->bshd', p.astype(v.dtype), v).reshape(B, S, H * d)


def setup_inputs(seed: int = 0) -> dict:
    key = jax.random.key(seed)
    ks = jax.random.split(key, 24)
    f32 = jnp.float32

    def nrm(k, shape, fan_in):
        return jax.random.normal(k, shape, f32) * (fan_in ** -0.5)

    def gain(k, shape):
        return 1.0 + 0.02 * jax.random.normal(k, shape, f32)

    x = jax.random.normal(ks[0], (BATCH, SEQ, D_MODEL), f32)
    mem = jax.random.normal(ks[1], (BATCH, N_MEM, D_MODEL), f32)
    start = jax.random.randint(ks[2], (BATCH, 1), 0, 4096, dtype=jnp.int32)
    positions = start + jnp.arange(SEQ, dtype=jnp.int32)[None, :]
    return {
        "x": x,
        "mem": mem,
        "positions": positions,
        "w_in": nrm(ks[3], (DEPTH, D_MODEL, IN_DIM), D_MODEL),
        "b_igate": 0.1 * jax.random.normal(ks[4], (DEPTH, M_HEADS), f32),
        "b_fgate": 3.0 + 3.0 * jax.random.uniform(ks[5], (DEPTH, M_HEADS), f32),
        "conv_w": nrm(ks[6], (DEPTH, CONV_K, 2 * M_W), CONV_K),
        "conv_b": 0.02 * jax.random.normal(ks[7], (DEPTH, 2 * M_W), f32),
        "mh_norm": gain(ks[8], (DEPTH, M_W)),
        "cq_norm": gain(ks[9], (DEPTH, Q_LORA)),
        "w_uq": nrm(ks[10], (DEPTH, Q_LORA, A_HEADS * A_DQK), Q_LORA),
        "ckv_norm": gain(ks[11], (DEPTH, KV_LORA)),
        "w_ukv": nrm(ks[12], (DEPTH, KV_LORA, A_HEADS * (A_NOPE + A_DV)), KV_LORA),
        "mem_norm": gain(ks[13], (DEPTH, D_MODEL)),
        "w_mem_kv": nrm(ks[14], (DEPTH, D_MODEL, 2 * C_W), D_MODEL),
        "w_br_m": nrm(ks[15], (DEPTH, M_W, D_MODEL), M_W),
        "w_br_a": nrm(ks[16], (DEPTH, A_W, D_MODEL), A_W),
        "w_br_c": nrm(ks[17], (DEPTH, C_W, D_MODEL), C_W),
        "w_out": nrm(ks[18], (DEPTH, D_MODEL, D_MODEL), D_MODEL),
        "norm": gain(ks[19], (DEPTH, D_MODEL)),
        "final_norm": gain(ks[20], (D_MODEL,)),
    }


def reference(x, mem, positions, w_in, b_igate, b_fgate, conv_w, conv_b, mh_norm, cq_norm, w_uq,
              ckv_norm, w_ukv, mem_norm, w_mem_kv, w_br_m, w_br_a, w_br_c, w_out, norm, final_norm):
    B, S, _ = x.shape
    f32 = jnp.float32
    inv_freq = ROPE_THETA ** (-jnp.arange(0, A_ROPE, 2, dtype=f32) / A_ROPE)
    ang = positions.astype(f32)[..., None] * inv_freq
    cos, sin = jnp.cos(ang), jnp.sin(ang)

    for l in range(DEPTH):
        h = rms_norm(x, norm[l])
        proj = h @ w_in[l]
        (m_qk, m_v, m_o, m_z, m_i, m_f,
         a_cq, a_ckv, a_kr, a_z,
         c_q, c_z, gates) = split_cols(proj)

        qk = jax.nn.silu(causal_depthwise_conv(m_qk, conv_w[l], conv_b[l]))
        mq, mk = jnp.split(qk, 2, axis=-1)
        mq = mq.reshape(B, S, M_HEADS, M_DH).astype(f32)
        mk = mk.reshape(B, S, M_HEADS, M_DH).astype(f32) * (M_DH ** -0.5)
        mv = m_v.reshape(B, S, M_HEADS, M_DH).astype(f32)
        ig = (m_i + b_igate[l]).astype(f32)
        lf = jax.nn.log_sigmoid((m_f + b_fgate[l]).astype(f32))
        hm = mlstm_chunkwise(mq, mk, mv, ig, lf)
        hm = rms_norm(hm, mh_norm[l].reshape(M_HEADS, M_DH)).reshape(B, S, M_W).astype(x.dtype)
        hm = hm * jax.nn.sigmoid(m_o) * jax.nn.silu(m_z)

        q_all = (rms_norm(a_cq, cq_norm[l]) @ w_uq[l]).reshape(B, S, A_HEADS, A_DQK)
        q_nope, q_rope = q_all[..., :A_NOPE], q_all[..., A_NOPE:]
        q_rope = apply_rope(q_rope, cos[:, :, None, :], sin[:, :, None, :])
        kv = (rms_norm(a_ckv, ckv_norm[l]) @ w_ukv[l]).reshape(B, S, A_HEADS, A_NOPE + A_DV)
        k_nope, v_a = kv[..., :A_NOPE], kv[..., A_NOPE:]
        k_rope = apply_rope(a_kr, cos, sin)
        qa = jnp.concatenate([q_nope, q_rope], axis=-1)
        ka = jnp.concatenate([k_nope, jnp.broadcast_to(k_rope[:, :, None, :], (B, S, A_HEADS, A_ROPE))], axis=-1)
        ha = causal_block_attention(qa, ka, v_a) * jax.nn.silu(a_z)

        mkv = (rms_norm(mem, mem_norm[l]) @ w_mem_kv[l]).reshape(B, N_MEM, 2, C_HEADS, C_DH)
        hc = memory_attention(c_q.reshape(B, S, C_HEADS, C_DH), mkv[:, :, 0], mkv[:, :, 1])
        hc = hc * jax.nn.silu(c_z)

        g_m = jax.nn.sigmoid(gates[..., 0:D_MODEL])
        g_a = jax.nn.sigmoid(gates[..., D_MODEL:2 * D_MODEL])
        g_c = jax.nn.sigmoid(gates[..., 2 * D_MODEL:3 * D_MODEL])
        merged = g_m * (hm @ w_br_m[l]) + g_a * (ha @ w_br_a[l]) + g_c * (hc @ w_br_c[l])
        x = x + merged @ w_out[l]

    return rms_norm(x, final_norm)
```

```python
import math
import contextlib
import numpy as np
import concourse.bass as bass
import concourse.mybir as mybir
from concourse.bass_utils import run_bass_kernel_spmd

F32 = mybir.dt.float32
BF16 = mybir.dt.bfloat16
I32 = mybir.dt.int32
AF = mybir.ActivationFunctionType
ALU = mybir.AluOpType

S = 2048
D = 2048
NT = 16
EPS = 1e-6
O_QK, O_V, O_O, O_Z, O_I, O_F, O_CQ, O_CKV, O_KR, O_AZ, O_CQC, O_CZ, O_G = (
    0, 2048, 3072, 4096, 5120, 5124, 5128, 5640, 6152, 6216, 7240, 8264, 9288)
LN16 = math.log(16.0)
PI = math.pi
PI_SAFE = 3.1415925
C1 = 6.28125
C2 = 2 * math.pi - 6.28125


class Res:
    __slots__ = ("w", "r")

    def __init__(self):
        self.w = None
        self.r = {}


class FW:
    NS = 6

    def __init__(self, nc, es):
        self.nc = nc
        self.eng = {"pe": nc.tensor, "act": nc.scalar, "dve": nc.vector, "pool": nc.gpsimd, "sp": nc.sync}
        self.sem = {k: es.enter_context(nc.semaphore("s_" + k)) for k in self.eng}
        self.cnt = {k: 0 for k in self.eng}
        self.seen = {k: {} for k in self.eng}
        self.dq = {}
        for q in ("sp", "pool", "act"):
            sems = [es.enter_context(nc.semaphore("d_%s%d" % (q, i))) for i in range(self.NS)]
            self.dq[q] = {"sems": sems, "vals": [0] * self.NS, "i": 0}

    def _semh(self, key):
        if isinstance(key, str):
            return self.sem[key]
        return self.dq[key[0]]["sems"][key[1]]

    def _wait(self, e, ev):
        if ev is None:
            return
        key, val = ev
        if e == "pe" and key == "pe":
            return
        if self.seen[e].get(key, 0) >= val:
            return
        self.seen[e][key] = val
        self.eng[e].wait_ge(self._semh(key), val)

    def _deps(self, e, reads, writes):
        for r in reads:
            self._wait(e, r.w)
        for w in writes:
            self._wait(e, w.w)
            for ev in list(w.r.values()):
                self._wait(e, ev)

    def _mark(self, me, reads, writes):
        for r in reads:
            r.r[me[0]] = me
        for w in writes:
            w.w = me
            w.r = {}

    def op(self, e, fn, reads=(), writes=()):
        self._deps(e, reads, writes)
        ins = fn(self.eng[e])
        self.cnt[e] += 1
        ins.then_inc(self.sem[e], 1)
        self._mark((e, self.cnt[e]), reads, writes)

    def dma(self, q, out, in_, reads=(), writes=(), **kw):
        self._deps(q, reads, writes)
        d = self.dq[q]
        i = d["i"]
        d["i"] = (i + 1) % self.NS
        key = (q, i)
        if d["vals"][i]:
            self._wait(q, (key, d["vals"][i]))
        ins = self.eng[q].dma_start(out=out, in_=in_, **kw)
        d["vals"][i] += 16
        ins.then_inc(d["sems"][i], 16)
        self._mark((key, d["vals"][i]), reads, writes)

    def all_events(self):
        evs = [(k, self.cnt[k]) for k in self.eng if self.cnt[k]]
        for q, d in self.dq.items():
            for i in range(self.NS):
                if d["vals"][i]:
                    evs.append(((q, i), d["vals"][i]))
        return evs

    def barrier(self):
        evs = self.all_events()
        for e in self.eng:
            for ev in evs:
                if ev[0] != e:
                    self._wait(e, ev)

    def final_wait(self, e="sp"):
        for ev in self.all_events():
            if ev[0] != e:
                self._wait(e, ev)


def build(debug=False, upto=99):
    nc = bass.Bass("TRN2", target_bir_lowering=False)

    def din(name, shape, dt=F32):
        return nc.dram_tensor(name, list(shape), dt, kind="ExternalInput").ap()

    skind = "ExternalOutput" if debug else "Internal"

    def dscr(name, shape, dt=BF16):
        return nc.dram_tensor(name, list(shape), dt, kind=skind).ap()

    x_d = din("x", [S, D])
    mem_d = din("mem", [256, D])
    pos_d = din("pos64", [64, S], I32)
    invf_d = din("invf", [64, 1])
    sgn_d = din("sgn", [64, 1])
    ident_d = din("ident", [128, 128])
    mask_d = din("mask", [128, 128])
    gn_d = din("g_norm", [128, D])
    gf_d = din("g_final", [128, D])
    gm_d = din("g_mem", [128, D])
    gcq_d = din("g_cq", [128, 512])
    gckv_d = din("g_ckv", [128, 512])
    gmh_d = din("g_mh", [128, 1024])
    convw_d = din("convw", [128, 16, 4])
    convb_d = din("convb", [128, 16])
    bif_d = din("bif", [128, 8])
    wA_d = din("wA", [18, 128, 16, 512])
    wKR_d = din("wKR", [128, 16, 128])
    wIF_d = din("wIF", [128, 16, 8])
    wUQ_d = din("wUQ", [4, 128, 4, 512])
    wUK_d = din("wUK", [2, 128, 4, 512])
    wUV_d = din("wUV", [2, 128, 4, 512])
    wMEM_d = din("wMEM", [4, 128, 16, 512])
    wG_d = din("wG", [16, 128, 16, 384])
    wBR_d = din("wBR", [16, 128, 8, 384])
    wOUT_d = din("wOUT", [4, 128, 16, 512])
    out_d = nc.dram_tensor("out", [S, D], F32, kind="ExternalOutput").ap()

    HT_d = dscr("HT", [D, S])
    MQT_d = dscr("MQT", [1024, S])
    MKT_d = dscr("MKT", [1024, S])
    MV_d = dscr("MV", [S, 1024])
    MG_d = dscr("MG", [S, 1024])
    KRT_d = dscr("KRT", [64, S])
    AZ_d = dscr("AZ", [S, 1024])
    CQT_d = dscr("CQT", [1024, S])
    CZ_d = dscr("CZ", [S, 1024])
    QNT_d = dscr("QNT", [1024, S])
    QRT_d = dscr("QRT", [512, S])
    KNT_d = dscr("KNT", [1024, S])
    VA_d = dscr("VA", [S, 1024])
    BH_d = dscr("BH", [64, 128], F32)
    HMT_d = dscr("HMT", [1024, S])
    HAT_d = dscr("HAT", [1024, S])
    HCT_d = dscr("HCT", [1024, S])
    GIF_d = dscr("GIF", [128, 128], F32)

    with contextlib.ExitStack() as es:
        fw = FW(nc, es)
        banks = [(es.enter_context(nc.psum_tensor("bk%d" % i, [128, 512], F32)), Res()) for i in range(8)]
        bi = [0]

        def nb():
            i = bi[0]
            bi[0] = (i + 1) % 8
            return banks[i][0], banks[i][1]

        bhi = [0]

        def nbhi():
            i = 4 + bhi[0]
            bhi[0] = (bhi[0] + 1) % 4
            return banks[i][0], banks[i][1]

        def b16(bk, k):
            return bk[:].bitcast(BF16).rearrange("p (k n) -> p k n", k=k)

        nsb = [0]

        def sbuf(stack, name, shape, dt):
            nsb[0] += 1
            return stack.enter_context(nc.sbuf_tensor("sb%d_%s" % (nsb[0], name), list(shape), dt))

        R0 = Res()

        identb = sbuf(es, "identb", [128, 128], BF16)
        identf = sbuf(es, "identf", [128, 128], F32)
        maskb = sbuf(es, "maskb", [128, 128], BF16)
        maskf = sbuf(es, "maskf", [128, 128], F32)
        onesf = sbuf(es, "onesf", [128, 128], F32)
        GI = sbuf(es, "GI", [128, 4, 16], F32)
        GF = sbuf(es, "GF", [128, 4, 16], F32)
        fw.dma("pool", identb[:], ident_d, writes=[R0])
        fw.dma("sp", identf[:], ident_d, writes=[R0])
        fw.dma("pool", maskb[:], mask_d, writes=[R0])
        fw.dma("sp", maskf[:], mask_d, writes=[R0])
        fw.op("dve", lambda e: e.memset(onesf[:], 1.0), writes=[R0])
        fw.barrier()

        def rmsnorm_T(st2, src_d, ntiles, g_d, dstT, hooks=None):
            xts = [(sbuf(st2, "rn_xt%d" % i, [128, D], F32), Res()) for i in range(4)]
            xbs = [(sbuf(st2, "rn_xb%d" % i, [128, D], BF16), Res()) for i in range(3)]
            junk = sbuf(st2, "rn_junk", [128, D], BF16)
            r_junk = Res()
            gbc = sbuf(st2, "rn_g", [128, D], F32)
            r_g = Res()
            stt = sbuf(st2, "rn_st", [128, 3 * ntiles], F32)
            r_sts = [Res() for _ in range(ntiles)]
            r_dst = Res()
            fw.dma("sp", gbc[:], g_d, writes=[r_g])
            def stats_part(tt):
                xt, r_xt = xts[tt % 4]
                r_st = r_sts[tt]
                c = 3 * tt
                fw.dma("sp", xt[:], src_d[tt * 128:(tt + 1) * 128, :], writes=[r_xt])
                fw.op("act", lambda e: e.activation(out=junk[:], in_=xt[:], func=AF.Square, accum_out=stt[:, c:c + 1]),
                      reads=[r_xt], writes=[r_junk, r_st])
                fw.op("act", lambda e: e.activation(out=stt[:, c + 1:c + 2], in_=stt[:, c:c + 1], func=AF.Ln, scale=1.0 / D, bias=EPS),
                      reads=[r_st], writes=[r_st])
                fw.op("act", lambda e: e.activation(out=stt[:, c + 2:c + 3], in_=stt[:, c + 1:c + 2], func=AF.Exp, scale=-0.5),
                      reads=[r_st], writes=[r_st])

            def write_part(tt):
                xt, r_xt = xts[tt % 4]
                xb, r_xb = xbs[tt % 3]
                r_st = r_sts[tt]
                c = 3 * tt
                fw.op("dve", lambda e: e.scalar_tensor_tensor(out=xb[:], in0=xt[:], scalar=stt[:, c + 2:c + 3], in1=gbc[:],
                                                              op0=ALU.mult, op1=ALU.mult),
                      reads=[r_xt, r_st, r_g], writes=[r_xb])

            def tr_part(tt):
                xb, r_xb = xbs[tt % 3]
                for half in range(2):
                    bk, r_bk = nb()
                    bkb = b16(bk, 8)
                    for k in range(8):
                        kk = half * 8 + k
                        fw.op("pe", lambda e: e.transpose(out=bkb[:, k, :], in_=xb[:, kk * 128:(kk + 1) * 128], identity=identb[:]),
                              reads=[r_xb], writes=[r_bk])
                    if half == 0:
                        fw.op("act", lambda e: e.copy(out=dstT[:, half * 8:(half + 1) * 8, tt * 128:(tt + 1) * 128], in_=bkb),
                              reads=[r_bk], writes=[r_dst])
                    else:
                        fw.op("dve", lambda e: e.tensor_copy(out=dstT[:, half * 8:(half + 1) * 8, tt * 128:(tt + 1) * 128], in_=bkb),
                              reads=[r_bk], writes=[r_dst])

            stats_part(0)
            if ntiles > 1:
                stats_part(1)
            write_part(0)
            for tt in range(ntiles):
                if tt + 2 < ntiles:
                    stats_part(tt + 2)
                    if hooks and (tt + 2) in hooks:
                        hooks[tt + 2](xts[(tt + 2) % 4][1])
                if tt + 1 < ntiles:
                    write_part(tt + 1)
                tr_part(tt)

        class WPool:
            def __init__(self, stack, name, nslots, elems):
                self.t = [sbuf(stack, "%s%d" % (name, i), [128, elems], BF16) for i in range(nslots)]
                self.r = [Res() for _ in range(nslots)]
                self.i = 0

            def load(self, dram_ap, KC, ncols):
                i = self.i
                self.i = (i + 1) % len(self.t)
                view = self.t[i][:, 0:KC * ncols].rearrange("p (k n) -> p k n", k=KC)
                tot = KC * ncols
                bsz = max(d for d in range(1, 1025) if tot % d == 0)
                fw.dma("pool", self.t[i][:, 0:tot].rearrange("p (a b) -> p a b", b=bsz),
                       dram_ap.rearrange("p k n -> p (k n)").rearrange("p (a b) -> p a b", b=bsz), writes=[self.r[i]])
                return view, self.r[i]

        def proj_fm(wv, wr, chunks, src, r_src, KC, ntok, evac):
            ntb = (ntok + 511) // 512
            for ci, (c0, M) in enumerate(chunks):
                bks = [nb() for _ in range(ntb)]
                for kc in range(KC):
                    for tb in range(ntb):
                        n = min(512, ntok - tb * 512)
                        bk, r_bk = bks[tb]
                        fw.op("pe", lambda e: e.matmul(bk[0:M, 0:n], lhsT=wv[:, kc, c0:c0 + M], rhs=src[:, kc, tb * 512:tb * 512 + n],
                                                       start=(kc == 0), stop=(kc == KC - 1)),
                              reads=[wr, r_src], writes=[r_bk])
                for tb in range(ntb):
                    n = min(512, ntok - tb * 512)
                    evac(ci, tb, bks[tb][0][0:M, 0:n], bks[tb][1])

        def proj_tm(wv, wr, ncols, src, r_src, KC, ntiles, evac):
            for tt in range(ntiles):
                bk, r_bk = nb()
                for kc in range(KC):
                    fw.op("pe", lambda e: e.matmul(bk[:, 0:ncols], lhsT=src[:, kc, tt * 128:(tt + 1) * 128], rhs=wv[:, kc, 0:ncols],
                                                   start=(kc == 0), stop=(kc == KC - 1)),
                          reads=[wr, r_src], writes=[r_bk])
                evac(tt, bk[:, 0:ncols], r_bk)

        with contextlib.ExitStack() as sA:
            cosT = sbuf(sA, "cosT", [64, S], F32)
            sinS = sbuf(sA, "sinS", [64, S], F32)
            hT = sbuf(sA, "hT", [128, 16, S], BF16)
            wpA = WPool(sA, "wpA", 3, 16 * 512)
            preA = []

            def pre_hook(lo, hi):
                def f(r_x):
                    fw._wait("pool", r_x.w)
                    for i in range(lo, hi):
                        preA.append(wpA.load(wA_d[i], 16, 512))
                return f
            if True:
                with contextlib.ExitStack() as s1:
                    rmsnorm_T(s1, x_d, NT, gn_d, hT, hooks={8: pre_hook(0, 1), 15: pre_hook(1, 3)})
                    fw.barrier()
            if upto >= 2:
              with contextlib.ExitStack() as s2:
                wp = wpA
                cqnT = sbuf(s2, "cqnT", [128, 4, S], BF16)
                ckvnT = sbuf(s2, "ckvnT", [128, 4, S], BF16)
                pre = sbuf(s2, "pre", [128, S + 4], F32)
                r_pre = Res()
                ybuf = sbuf(s2, "ybuf", [128, S], F32)
                r_y = Res()
                obs = [(sbuf(s2, "ob%d" % i, [128, S], BF16), Res()) for i in range(2)]
                t32s = [(sbuf(s2, "t32_%d" % i, [128, 512], F32), Res()) for i in range(3)]
                t16s = [(sbuf(s2, "t16_%d" % i, [128, 512], BF16), Res()) for i in range(4)]
                rot_i = {}

                def rot(lst):
                    i = rot_i.get(id(lst), 0)
                    rot_i[id(lst)] = (i + 1) % len(lst)
                    return lst[i]

                stA = sbuf(s2, "stA", [128, 16], F32)
                r_st = Res()
                cw = sbuf(s2, "cw", [128, 16, 4], F32)
                cb = sbuf(s2, "cb", [128, 16], F32)
                bif = sbuf(s2, "bif", [128, 8], F32)
                gcq = sbuf(s2, "gcq", [128, 512], F32)
                gckv = sbuf(s2, "gckv", [128, 512], F32)
                gmhA = sbuf(s2, "gmhA", [128, 1024], F32)
                r_k = Res()
                r_g = Res()
                r_cn = Res()
                fw.dma("sp", cw[:], convw_d, writes=[r_k])
                fw.dma("sp", cb[:], convb_d, writes=[r_k])
                fw.dma("sp", bif[:], bif_d, writes=[r_k])
                fw.dma("sp", gcq[:], gcq_d, writes=[r_k])
                fw.dma("sp", gckv[:], gckv_d, writes=[r_k])
                fw.dma("sp", gmhA[:], gmh_d, writes=[r_k])
                fw.op("dve", lambda e: e.memset(pre[:, 0:3], 0.0), writes=[r_pre])
                r_HT = Res()
                for k in range(16):
                    fw.dma("sp", HT_d[k * 128:(k + 1) * 128, :], hT[:, k, :], reads=[R0], writes=[r_HT])
                cur = {}

                for h in range(4):
                    wv, wr = preA[h] if h < 3 else wp.load(wA_d[h], 16, 512)

                    def ev_qk(ci, tb, ps, r_ps, h=h):
                        fw.op("act", lambda e: e.copy(out=pre[:, 3 + tb * 512:3 + (tb + 1) * 512], in_=ps), reads=[r_ps], writes=[r_pre])
                        if tb == 3:
                            j = 4 * h + ci
                            fw.op("dve", lambda e: e.tensor_scalar(out=ybuf[:], in0=pre[:, 3:3 + S], scalar1=cw[:, j, 3:4], scalar2=cb[:, j:j + 1],
                                                                   op0=ALU.mult, op1=ALU.add), reads=[r_pre, r_k], writes=[r_y])
                            for tap in (2, 1, 0):
                                fw.op("dve", lambda e: e.scalar_tensor_tensor(out=ybuf[:], in0=pre[:, tap:tap + S], scalar=cw[:, j, tap:tap + 1], in1=ybuf[:],
                                                                              op0=ALU.mult, op1=ALU.add), reads=[r_pre, r_y, r_k], writes=[r_y])
                            ob, r_ob = rot(obs)
                            fw.op("act", lambda e: e.activation(out=ob[:], in_=ybuf[:], func=AF.Silu), reads=[r_y], writes=[r_ob])
                            dst = MQT_d if ci < 2 else MKT_d
                            r0 = h * 256 + (ci % 2) * 128
                            fw.dma("sp", dst[r0:r0 + 128, :], ob[:], reads=[r_ob])

                    proj_fm(wv, wr, [(0, 128), (128, 128), (256, 128), (384, 128)], hT, R0, 16, S, ev_qk)

                QW = 512
                cst = sbuf(s2, "cst", [64, 2], F32)
                fw.dma("sp", cst[:, 0:1], invf_d, writes=[r_k])
                fw.dma("sp", cst[:, 1:2], sgn_d, writes=[r_k])
                posi = pre[0:64, 0:QW].bitcast(I32)
                ang = pre[0:64, QW:2 * QW]
                tq = pre[0:64, 2 * QW:3 * QW]
                ki = pre[0:64, 3 * QW:4 * QW].bitcast(I32)
                rr = ybuf[0:64, 0:QW]
                mm = ybuf[0:64, QW:2 * QW]
                r_rope = Res()

                def rope_quarter(qq):
                    qc = slice(qq * QW, (qq + 1) * QW)
                    RW = [r_pre, r_y]
                    V = lambda f: fw.op("dve", f, reads=[r_k], writes=RW)

                    def wrap(buf):
                        V(lambda e: e.tensor_scalar(out=mm, in0=buf, scalar1=PI, scalar2=-2 * PI, op0=ALU.is_gt, op1=ALU.mult))
                        V(lambda e: e.tensor_tensor(out=buf, in0=mm, in1=buf, op=ALU.add))
                        V(lambda e: e.tensor_scalar(out=mm, in0=buf, scalar1=-PI, scalar2=2 * PI, op0=ALU.is_lt, op1=ALU.mult))
                        V(lambda e: e.tensor_tensor(out=buf, in0=mm, in1=buf, op=ALU.add))
                        V(lambda e: e.tensor_scalar(out=buf, in0=buf, scalar1=PI_SAFE, scalar2=-PI_SAFE, op0=ALU.min, op1=ALU.max))

                    fw.dma("sp", posi, pos_d[:, qc], writes=RW)
                    V(lambda e: e.tensor_copy(out=ang, in_=posi))
                    V(lambda e: e.tensor_scalar(out=ang, in0=ang, scalar1=cst[:, 0:1], scalar2=None, op0=ALU.mult))
                    V(lambda e: e.tensor_scalar(out=tq, in0=ang, scalar1=1.0 / (2 * PI), scalar2=None, op0=ALU.mult))
                    V(lambda e: e.tensor_copy(out=ki, in_=tq))
                    V(lambda e: e.tensor_copy(out=tq, in_=ki))
                    V(lambda e: e.scalar_tensor_tensor(out=rr, in0=tq, scalar=-C1, in1=ang, op0=ALU.mult, op1=ALU.add))
                    V(lambda e: e.scalar_tensor_tensor(out=rr, in0=tq, scalar=-C2, in1=rr, op0=ALU.mult, op1=ALU.add))
                    wrap(rr)
                    fw.op("act", lambda e: e.activation(out=sinS[:, qc], in_=rr, func=AF.Sin), reads=RW, writes=[r_rope])
                    fw.op("dve", lambda e: e.tensor_scalar(out=sinS[:, qc], in0=sinS[:, qc], scalar1=cst[:, 1:2], scalar2=None, op0=ALU.mult),
                          reads=[r_k], writes=[r_rope])
                    V(lambda e: e.tensor_scalar(out=ang, in0=rr, scalar1=PI / 2, scalar2=None, op0=ALU.add))
                    wrap(ang)
                    fw.op("act", lambda e: e.activation(out=cosT[:, qc], in_=ang, func=AF.Sin), reads=RW, writes=[r_rope])

                for blk in range(2):
                    wv, wr = wp.load(wA_d[4 + blk], 16, 512)

                    def ev_v(tt, ps, r_ps, blk=blk):
                        t16, r16 = rot(t16s)
                        fw.op("act", lambda e: e.copy(out=t16[:], in_=ps), reads=[r_ps], writes=[r16])
                        fw.dma("sp", MV_d[tt * 128:(tt + 1) * 128, blk * 512:(blk + 1) * 512], t16[:], reads=[r16])
                        if tt in (2, 9):
                            rope_quarter(2 * blk + (0 if tt == 2 else 1))

                    proj_tm(wv, wr, 512, hT, R0, 16, NT, ev_v)

                for h in range(4):
                    wv, wr = wp.load(wA_d[6 + h], 16, 512)

                    def ev_oz(tt, ps, r_ps, h=h):
                        t32, r32 = rot(t32s)
                        fw.op("act", lambda e: e.activation(out=t32[:], in_=ps, func=AF.Sigmoid), reads=[r_ps], writes=[r32])
                        t16, r16 = rot(t16s)
                        fw.op("dve", lambda e: e.tensor_tensor(out=t32[:, 0:256], in0=t32[:, 0:256], in1=t32[:, 256:512], op=ALU.mult),
                              reads=[r32], writes=[r32])
                        fw.op("dve", lambda e: e.tensor_tensor(out=t32[:, 0:256], in0=ps[:, 256:512], in1=t32[:, 0:256], op=ALU.mult),
                              reads=[r32, r_ps], writes=[r32])
                        fw.op("dve", lambda e: e.tensor_tensor(out=t16[:, 0:256], in0=t32[:, 0:256], in1=gmhA[:, h * 256:(h + 1) * 256], op=ALU.mult),
                              reads=[r32, r_k], writes=[r16])
                        fw.dma("sp", MG_d[tt * 128:(tt + 1) * 128, h * 256:(h + 1) * 256], t16[:, 0:256], reads=[r16])

                    proj_tm(wv, wr, 512, hT, R0, 16, NT, ev_oz)

                wv, wr = wp.load(wIF_d, 16, 8)

                def ev_if(tt, ps, r_ps):
                    fw.op("dve", lambda e: e.tensor_tensor(out=stA[:, 0:8], in0=ps, in1=bif[:], op=ALU.add), reads=[r_ps, r_k], writes=[r_st])
                    fw.op("dve", lambda e: e.tensor_copy(out=GI[:, :, tt], in_=stA[:, 0:4]), reads=[r_st], writes=[r_g])
                    fw.op("act", lambda e: e.activation(out=stA[:, 8:12], in_=stA[:, 4:8], func=AF.Exp, scale=-1.0), reads=[r_st], writes=[r_st])
                    fw.op("act", lambda e: e.activation(out=stA[:, 12:16], in_=stA[:, 8:12], func=AF.Ln, bias=1.0), reads=[r_st], writes=[r_st])
                    fw.op("dve", lambda e: e.tensor_scalar(out=GF[:, :, tt], in0=stA[:, 12:16], scalar1=-1.0, scalar2=None, op0=ALU.mult),
                          reads=[r_st], writes=[r_g])

                proj_tm(wv, wr, 8, hT, R0, 16, NT, ev_if)

                for which, (gb, dstT) in enumerate([(gcq, cqnT), (gckv, ckvnT)]):
                    wv, wr = wp.load(wA_d[10 + which], 16, 512)

                    def ev_c(tt, ps, r_ps, gb=gb, dstT=dstT):
                        tj, rj = rot(t16s)
                        fw.op("act", lambda e: e.activation(out=tj[:], in_=ps, func=AF.Square, accum_out=stA[:, 0:1]), reads=[r_ps], writes=[rj, r_st])
                        fw.op("act", lambda e: e.activation(out=stA[:, 1:2], in_=stA[:, 0:1], func=AF.Ln, scale=1.0 / 512, bias=EPS), reads=[r_st], writes=[r_st])
                        fw.op("act", lambda e: e.activation(out=stA[:, 2:3], in_=stA[:, 1:2], func=AF.Exp, scale=-0.5), reads=[r_st], writes=[r_st])
                        t16, r16 = rot(t16s)
                        fw.op("dve", lambda e: e.scalar_tensor_tensor(out=t16[:], in0=ps, scalar=stA[:, 2:3], in1=gb[:], op0=ALU.mult, op1=ALU.mult),
                              reads=[r_ps, r_st, r_k], writes=[r16])
                        if cur.get("ctr") is not None:
                            cur["ctr"]()

                        def tr(t16=t16, r16=r16, tt=tt, dstT=dstT):
                            bk, r_bk = nb()
                            bkb = b16(bk, 8)
                            for k in range(4):
                                fw.op("pe", lambda e: e.transpose(out=bkb[:, k, :], in_=t16[:, k * 128:(k + 1) * 128], identity=identb[:]),
                                      reads=[r16], writes=[r_bk])
                            fw.op("act", lambda e: e.copy(out=dstT[:, 0:4, tt * 128:(tt + 1) * 128], in_=bkb[:, 0:4, :]), reads=[r_bk], writes=[r_cn])

                        cur["ctr"] = tr

                    proj_tm(wv, wr, 512, hT, R0, 16, NT, ev_c)
                    cur["ctr"]()
                    cur["ctr"] = None

                def rope_ev(kind, tb, ps, r_ps, dst_rows):
                    cols = slice(tb * 512, (tb + 1) * 512)
                    if kind == 1:
                        fw.op("dve", lambda e: e.tensor_tensor(out=ybuf[0:64, cols], in0=ps, in1=cosT[:, cols], op=ALU.mult), reads=[r_ps, r_rope], writes=[r_y])
                    else:
                        if tb == 0:
                            cur["rope"] = rot(obs)
                        ob, r_ob = cur["rope"]
                        t32, r32 = rot(t32s)
                        fw.op("dve", lambda e: e.tensor_tensor(out=t32[0:64, :], in0=ps, in1=sinS[:, cols], op=ALU.mult), reads=[r_ps, r_rope], writes=[r32])
                        fw.op("dve", lambda e: e.tensor_tensor(out=ob[0:64, cols], in0=ybuf[0:64, cols], in1=t32[0:64, :], op=ALU.add),
                              reads=[r_y, r32], writes=[r_ob])
                        if tb == 3:
                            fw.dma("sp", dst_rows, ob[0:64, :], reads=[r_ob])

                def copy_ev(tb, ps, r_ps, dst_rows, alt):
                    cols = slice(tb * 512, (tb + 1) * 512)
                    if tb == 0:
                        cur["cp"] = rot(obs)
                    ob, r_ob = cur["cp"]
                    if alt % 2 == 0:
                        fw.op("act", lambda e: e.copy(out=ob[:, cols], in_=ps), reads=[r_ps], writes=[r_ob])
                    else:
                        fw.op("dve", lambda e: e.tensor_copy(out=ob[:, cols], in_=ps), reads=[r_ps], writes=[r_ob])
                    if tb == 3:
                        fw.dma("sp", dst_rows, ob[:], reads=[r_ob])

                wv, wr = wp.load(wKR_d, 16, 128)
                proj_fm(wv, wr, [(0, 64), (64, 64)], hT, R0, 16, S,
                        lambda ci, tb, ps, r_ps: rope_ev(ci + 1, tb, ps, r_ps, KRT_d[0:64, :]))

                def silu_ev(dst_d, blk):
                    def ev(tt, ps, r_ps):
                        t16, r16 = rot(t16s)
                        fw.op("act", lambda e: e.activation(out=t16[:], in_=ps, func=AF.Silu), reads=[r_ps], writes=[r16])
                        fw.dma("sp", dst_d[tt * 128:(tt + 1) * 128, blk * 512:(blk + 1) * 512], t16[:], reads=[r16])
                    return ev

                for blk in range(2):
                    wv, wr = wp.load(wA_d[12 + blk], 16, 512)
                    proj_tm(wv, wr, 512, hT, R0, 16, NT, silu_ev(AZ_d, blk))
                for blk in range(2):
                    wv, wr = wp.load(wA_d[14 + blk], 16, 512)
                    proj_fm(wv, wr, [(0, 128), (128, 128), (256, 128), (384, 128)], hT, R0, 16, S,
                            lambda ci, tb, ps, r_ps, blk=blk: copy_ev(tb, ps, r_ps, CQT_d[(blk * 4 + ci) * 128:(blk * 4 + ci + 1) * 128, :], tb))
                for blk in range(2):
                    wv, wr = wp.load(wA_d[16 + blk], 16, 512)
                    proj_tm(wv, wr, 512, hT, R0, 16, NT, silu_ev(CZ_d, blk))

                for pr in range(4):
                    wv, wr = wp.load(wUQ_d[pr], 4, 512)

                    def ev_q(ci, tb, ps, r_ps, pr=pr):
                        hh = 2 * pr + ci // 3
                        kind = ci % 3
                        if kind == 0:
                            copy_ev(tb, ps, r_ps, QNT_d[hh * 128:(hh + 1) * 128, :], tb)
                        else:
                            rope_ev(kind, tb, ps, r_ps, QRT_d[hh * 64:(hh + 1) * 64, :])

                    proj_fm(wv, wr, [(0, 128), (128, 64), (192, 64), (256, 128), (384, 64), (448, 64)], cqnT, r_cn, 4, S, ev_q)
                for blk in range(2):
                    wv, wr = wp.load(wUK_d[blk], 4, 512)
                    proj_fm(wv, wr, [(0, 128), (128, 128), (256, 128), (384, 128)], ckvnT, r_cn, 4, S,
                            lambda ci, tb, ps, r_ps, blk=blk: copy_ev(tb, ps, r_ps, KNT_d[(blk * 4 + ci) * 128:(blk * 4 + ci + 1) * 128, :], tb))
                for blk in range(2):
                    wv, wr = wp.load(wUV_d[blk], 4, 512)

                    def ev_va(tt, ps, r_ps, blk=blk):
                        t16, r16 = rot(t16s)
                        fw.op("act", lambda e: e.copy(out=t16[:], in_=ps), reads=[r_ps], writes=[r16])
                        fw.dma("sp", VA_d[tt * 128:(tt + 1) * 128, blk * 512:(blk + 1) * 512], t16[:], reads=[r16])

                    proj_tm(wv, wr, 512, ckvnT, r_cn, 4, NT, ev_va)
                if debug:
                    fw.dma("sp", GIF_d[:, 0:64], GI[:].rearrange("p h t -> p (h t)"), reads=[r_g])
                    fw.dma("sp", GIF_d[:, 64:128], GF[:].rearrange("p h t -> p (h t)"), reads=[r_g])
                fw.barrier()
            fw.barrier()
        rot_i = {}

        def rot(lst):
            i = rot_i.get(id(lst), 0)
            rot_i[id(lst)] = (i + 1) % len(lst)
            return lst[i]

        def mk(stack, name, n, shape, dt):
            return [(sbuf(stack, "%s%d" % (name, i), shape, dt), Res()) for i in range(n)]

        if upto >= 3:
          with contextlib.ExitStack() as sC:
            memnT = sbuf(sC, "memnT", [128, 16, 256], BF16)
            kmT = sbuf(sC, "kmT", [128, 8, 256], BF16)
            vmx = sbuf(sC, "vmx", [128, 2, 4, 260], BF16)
            with contextlib.ExitStack() as s3:
                rmsnorm_T(s3, mem_d, 2, gm_d, memnT)
                fw.barrier()
            with contextlib.ExitStack() as s4:
                wp = WPool(s4, "wpM", 2, 16 * 512)
                r_km = Res()
                r_vm = Res()
                fw.op("dve", lambda e: e.memset(vmx[:, :, :, 256:257], 1.0), writes=[r_vm])
                for blk in range(2):
                    wv, wr = wp.load(wMEM_d[blk], 16, 512)
                    proj_fm(wv, wr, [(0, 128), (128, 128), (256, 128), (384, 128)], memnT, R0, 16, 256,
                            lambda ci, tb, ps, r_ps, blk=blk: fw.op("act", lambda e: e.copy(out=kmT[:, blk * 4 + ci, :], in_=ps), reads=[r_ps], writes=[r_km]))
                for blk in range(2):
                    wv, wr = wp.load(wMEM_d[2 + blk], 16, 512)
                    proj_tm(wv, wr, 512, memnT, R0, 16, 2,
                            lambda tt, ps, r_ps, blk=blk: fw.op("dve", lambda e: e.tensor_copy(out=vmx[:, tt, 2 * blk:2 * blk + 2, 0:256],
                                                                                             in_=ps.rearrange("p (h c) -> p h c", h=2)),
                                                               reads=[r_ps], writes=[r_vm]))
                fw.barrier()
            with contextlib.ExitStack() as s5:
                cqs = mk(s5, "cq", 2, [128, 2, S], BF16)
                czs = mk(s5, "cz", 2, [128, 16, 256], BF16)
                hcTs = mk(s5, "hcT", 2, [128, 2, S], BF16)
                Ps = mk(s5, "Pc", 4, [128, 512], BF16)
                hct = mk(s5, "hct", 3, [128, 256], BF16)
                stats = mk(s5, "stC", 4, [128, 8], F32)
                def c_loads(h):
                    fw.dma("sp", cqs[h % 2][0][:], CQT_d[h * 256:(h + 1) * 256, :].rearrange("(k p) t -> p k t", p=128), writes=[cqs[h % 2][1]])
                    fw.dma("sp", czs[h % 2][0][:], CZ_d[:, h * 256:(h + 1) * 256].rearrange("(t p) c -> p t c", p=128), writes=[czs[h % 2][1]])

                def c_scores(h, qb):
                    cq, r_cq = cqs[h % 2]
                    Pm = []
                    for mt in range(2):
                        bk, r_bk = nbhi()
                        for kc in range(2):
                            fw.op("pe", lambda e: e.matmul(bk[:, 0:512], lhsT=kmT[:, 2 * h + kc, mt * 128:(mt + 1) * 128],
                                                           rhs=cq[:, kc, qb * 512:(qb + 1) * 512], start=(kc == 0), stop=(kc == 1)),
                                  reads=[r_cq], writes=[r_bk])
                        P, r_P = rot(Ps)
                        fw.op("act", lambda e: e.activation(out=P[:], in_=bk[:, 0:512], func=AF.Exp, scale=1.0 / 16), reads=[r_bk], writes=[r_P])
                        Pm.append((P, r_P))
                    return Pm

                def c_tr(pend):
                    hc_, r_hc, hcT, r_hcT, tt = pend
                    bt, r_bt = nbhi()
                    btb = b16(bt, 8)
                    for kc in range(2):
                        fw.op("pe", lambda e: e.transpose(out=btb[:, kc, :], in_=hc_[:, kc * 128:(kc + 1) * 128], identity=identb[:]),
                              reads=[r_hc], writes=[r_bt])
                    fw.op("act", lambda e: e.copy(out=hcT[:, 0:2, tt * 128:(tt + 1) * 128], in_=btb[:, 0:2, :]), reads=[r_bt], writes=[r_hcT])

                steps = [(h, qb) for h in range(4) for qb in range(4)]
                c_loads(0)
                c_loads(1)
                nxtP = c_scores(0, 0)
                pend = None
                for si, (h, qb) in enumerate(steps):
                    Pm = nxtP
                    cz, r_cz = czs[h % 2]
                    hcT, r_hcT = hcTs[h % 2]
                    if si + 1 < len(steps):
                        nxtP = c_scores(*steps[si + 1])
                    for q4 in range(4):
                        tt = qb * 4 + q4
                        bo, r_bo = banks[q4]
                        for mt in range(2):
                            fw.op("pe", lambda e: e.matmul(bo[:, 0:257], lhsT=Pm[mt][0][:, q4 * 128:(q4 + 1) * 128], rhs=vmx[:, mt, h, 0:257],
                                                           start=(mt == 0), stop=(mt == 1)),
                                  reads=[Pm[mt][1]], writes=[r_bo])
                        if pend is not None:
                            c_tr(pend)
                        sc, r_sc = rot(stats)
                        fw.op("dve", lambda e: e.reciprocal(out=sc[:, 0:1], in_=bo[:, 256:257]), reads=[r_bo], writes=[r_sc])
                        hc_, r_hc = rot(hct)
                        fw.op("dve", lambda e: e.scalar_tensor_tensor(out=hc_[:], in0=bo[:, 0:256], scalar=sc[:, 0:1], in1=cz[:, tt, :],
                                                                      op0=ALU.mult, op1=ALU.mult),
                              reads=[r_bo, r_sc, r_cz], writes=[r_hc])
                        pend = (hc_, r_hc, hcT, r_hcT, tt)
                    if qb == 3:
                        c_tr(pend)
                        pend = None
                        fw.dma("sp", HCT_d[h * 256:(h + 1) * 256, :].rearrange("(k p) t -> p k t", p=128), hcT[:], reads=[r_hcT])
                        if h + 2 < 4:
                            c_loads(h + 2)
                fw.barrier()

        if upto >= 4:
          with contextlib.ExitStack() as s6:
            kr = sbuf(s6, "kr", [128, S], BF16)
            r_kr = Res()
            qns = mk(s6, "qn", 2, [128, S], BF16)
            qrs = mk(s6, "qr", 2, [128, S], BF16)
            kns = mk(s6, "kn", 2, [128, S], BF16)
            vxs = mk(s6, "vx", 2, [128, 16, 132], BF16)
            azs = mk(s6, "az", 2, [128, 16, 128], BF16)
            haTs = mk(s6, "haT", 2, [128, S], BF16)
            Ps = mk(s6, "Pa", 3, [128, 512], BF16)
            hat = mk(s6, "hat", 8, [128, 128], BF16)
            stats = mk(s6, "stB", 4, [128, 8], F32)
            fw.op("dve", lambda e: e.memset(kr[64:128, :], 0.0), writes=[r_kr])
            fw.dma("sp", kr[0:64, :], KRT_d[0:64, :], writes=[r_kr])
            for vx, r_vx in vxs:
                fw.op("dve", lambda e: e.memset(vx[:, :, 128:129], 1.0), writes=[r_vx])
            for qr, r_qr in qrs:
                fw.op("dve", lambda e: e.memset(qr[64:128, :], 0.0), writes=[r_qr])
            SC = 1.0 / math.sqrt(192.0)

            def mla_loads(h):
                fw.dma("sp", qns[h % 2][0][:], QNT_d[h * 128:(h + 1) * 128, :], writes=[qns[h % 2][1]])
                fw.dma("sp", qrs[h % 2][0][0:64, :], QRT_d[h * 64:(h + 1) * 64, :], writes=[qrs[h % 2][1]])
                fw.dma("sp", kns[h % 2][0][:], KNT_d[h * 128:(h + 1) * 128, :], writes=[kns[h % 2][1]])
                fw.dma("sp", vxs[h % 2][0][:, :, 0:128], VA_d[:, h * 128:(h + 1) * 128].rearrange("(t p) c -> p t c", p=128), writes=[vxs[h % 2][1]])
                fw.dma("sp", azs[h % 2][0][:], AZ_d[:, h * 128:(h + 1) * 128].rearrange("(t p) c -> p t c", p=128), writes=[azs[h % 2][1]])

            mla_loads(0)
            pend_ev = [None]
            for h in range(8):
                qn, r_qn = qns[h % 2]
                qr, r_qr = qrs[h % 2]
                kn, r_kn = kns[h % 2]
                vx, r_vx = vxs[h % 2]
                az, r_az = azs[h % 2]
                haT, r_haT = haTs[h % 2]
                if h + 1 < 8:
                    mla_loads(h + 1)
                for qb in range(4):
                    bos = [banks[i] for i in range(4)]
                    nkt = 4 * qb + 4

                    def emit_S(kt, qb=qb):
                        q_lo = max(kt, 4 * qb)
                        ncols = (4 * qb + 4 - q_lo) * 128
                        q0 = q_lo * 128
                        kc_ = slice(kt * 128, (kt + 1) * 128)
                        bs, r_bs = nbhi()
                        fw.op("pe", lambda e: e.matmul(bs[:, 0:ncols], lhsT=kn[:, kc_], rhs=qn[:, q0:q0 + ncols], start=True, stop=False),
                              reads=[r_kn, r_qn], writes=[r_bs])
                        fw.op("pe", lambda e: e.matmul(bs[:, 0:ncols], lhsT=kr[:, kc_], rhs=qr[:, q0:q0 + ncols], start=False, stop=True),
                              reads=[r_kr, r_qr], writes=[r_bs])
                        P, r_P = rot(Ps)
                        fw.op("act", lambda e: e.activation(out=P[:, 0:ncols], in_=bs[:, 0:ncols], func=AF.Exp, scale=SC), reads=[r_bs], writes=[r_P])
                        if kt >= 4 * qb:
                            fw.op("dve", lambda e: e.tensor_tensor(out=P[:, 0:128], in0=P[:, 0:128], in1=maskb[:], op=ALU.mult), reads=[r_P], writes=[r_P])
                        return P, r_P, q_lo

                    cur_s = emit_S(0)
                    if pend_ev[0] is not None:
                        pend_ev[0]()
                        pend_ev[0] = None
                    for kt in range(nkt):
                        nxt_s = emit_S(kt + 1) if kt + 1 < nkt else None
                        P, r_P, q_lo = cur_s
                        for qt in range(q_lo, 4 * qb + 4):
                            j = qt - q_lo
                            bo, r_bo = bos[qt - 4 * qb]
                            fw.op("pe", lambda e: e.matmul(bo[:, 0:129], lhsT=P[:, j * 128:(j + 1) * 128], rhs=vx[:, kt, 0:129],
                                                           start=(kt == 0), stop=(kt == qt)),
                                  reads=[r_P, r_vx], writes=[r_bo])
                        cur_s = nxt_s
                    has = []
                    for q4 in range(4):
                        qt = 4 * qb + q4
                        bo, r_bo = bos[q4]
                        sc, r_sc = rot(stats)
                        fw.op("dve", lambda e: e.reciprocal(out=sc[:, 0:1], in_=bo[:, 128:129]), reads=[r_bo], writes=[r_sc])
                        ha_, r_ha = rot(hat)
                        fw.op("dve", lambda e: e.scalar_tensor_tensor(out=ha_[:], in0=bo[:, 0:128], scalar=sc[:, 0:1], in1=az[:, qt, :],
                                                                      op0=ALU.mult, op1=ALU.mult),
                              reads=[r_bo, r_sc, r_az], writes=[r_ha])
                        has.append((ha_, r_ha, qt))

                    def ev_tail(has=has, haT=haT, r_haT=r_haT, h=h, last=(qb == 3)):
                        bt, r_bt = nbhi()
                        btb = b16(bt, 8)
                        for q4, (ha_, r_ha, qt) in enumerate(has):
                            fw.op("pe", lambda e: e.transpose(out=btb[:, q4, :], in_=ha_[:], identity=identb[:]), reads=[r_ha], writes=[r_bt])
                        q0 = has[0][2] * 128
                        fw.op("act", lambda e: e.copy(out=haT[:, q0:q0 + 512].rearrange("p (a b) -> p a b", a=4), in_=btb[:, 0:4, :]), reads=[r_bt], writes=[r_haT])
                        if last:
                            fw.dma("sp", HAT_d[h * 128:(h + 1) * 128, :], haT[:], reads=[r_haT])

                    pend_ev[0] = ev_tail
            pend_ev[0]()
            fw.barrier()

        hTh = sbuf(es, "hTh", [128, 16, 1024], BF16)
        r_in = Res()
        if upto >= 5:
          with contextlib.ExitStack() as s7:
            Bt = sbuf(s7, "Bt", [128, 64], F32)
            Gt = sbuf(s7, "Gt", [128, 64], F32)
            At = sbuf(s7, "At", [128, 64], F32)
            WS = sbuf(s7, "WS", [128, 64], F32)
            tmpg = sbuf(s7, "tmpg", [128, 64], F32)
            btT = sbuf(s7, "btT", [64, 128], F32)
            Mneg = sbuf(s7, "Mneg", [128, S], F32)
            r_gp = Res()
            r_bh = Res()
            for cc in range(16):
                fw.op("pool", lambda e: e.tensor_scalar(out=Mneg[:, cc * 128:(cc + 1) * 128], in0=maskf[:], scalar1=-1.0, scalar2=30000.0, op0=ALU.add, op1=ALU.mult),
                      writes=[r_gp])
            GFv = GF[:].rearrange("p h t -> p (h t)")
            GIv = GI[:].rearrange("p h t -> p (h t)")
            bk, r_bk = nb()
            fw.op("pe", lambda e: e.matmul(bk[:, 0:64], lhsT=maskf[:], rhs=GFv, start=True, stop=True), reads=[R0], writes=[r_bk])
            fw.op("act", lambda e: e.copy(out=Bt[:], in_=bk[:, 0:64]), reads=[r_bk], writes=[r_gp])
            bk2, r_bk2 = nb()
            fw.op("pe", lambda e: e.matmul(bk2[:, 0:64], lhsT=onesf[:], rhs=GFv, start=True, stop=True), reads=[R0], writes=[r_bk2])
            fw.op("dve", lambda e: e.tensor_copy(out=Gt[:], in_=bk2[:, 0:64]), reads=[r_bk2], writes=[r_gp])
            fw.op("dve", lambda e: e.scalar_tensor_tensor(out=At[:], in0=GIv, scalar=-LN16, in1=Bt[:], op0=ALU.add, op1=ALU.subtract),
                  reads=[r_gp], writes=[r_gp])
            fw.op("dve", lambda e: e.tensor_tensor(out=tmpg[:], in0=Gt[:], in1=At[:], op=ALU.add), reads=[r_gp], writes=[r_gp])
            fw.op("act", lambda e: e.activation(out=WS[:], in_=tmpg[:], func=AF.Exp), reads=[r_gp], writes=[r_gp])
            bk3, r_bk3 = nb()
            fw.op("pe", lambda e: e.transpose(out=bk3[0:64, 0:128], in_=Bt[:], identity=identf[:]), reads=[r_gp], writes=[r_bk3])
            fw.op("act", lambda e: e.copy(out=btT[:], in_=bk3[0:64, 0:128]), reads=[r_bk3], writes=[r_gp])
            fw.dma("sp", BH_d, btT[:], reads=[r_gp], writes=[r_bh])
            BHv = BH_d.rearrange("(h t) c -> h (t c)", h=4)

            NH = 2
            qTs = mk(s7, "mqT", NH, [128, 2, S], BF16)
            kTs = mk(s7, "mkT", NH, [128, 2, S], BF16)
            vxs = mk(s7, "mvx", NH, [128, 16, 260], BF16)
            mgs = mk(s7, "mmg", NH, [128, 16, 256], BF16)
            Brs = mk(s7, "Brow", NH, [128, S], F32)
            EBs = mk(s7, "EBrow", NH, [128, S], F32)
            BMs = mk(s7, "BrM", NH, [128, S], F32)
            hmTs = mk(s7, "hmT", NH, [128, 2, S], BF16)
            Cfs = mk(s7, "Cf", NH, [128, 2, 260], F32)
            Cb3 = [mk(s7, "Cb%d_" % i, 3, [128, 2, 260], BF16) for i in range(NH)]
            qeb2 = [mk(s7, "qeb%d_" % i, 3, [128, 2, 128], BF16) for i in range(NH)]
            DTs = mk(s7, "DT", NH, [128, 128], F32)
            PT2 = [mk(s7, "PT%d_" % i, 3, [128, 128], BF16) for i in range(NH)]
            kws = mk(s7, "kw", NH, [128, 2, 128], BF16)
            hmt2 = [mk(s7, "hmt%d_" % i, 2, [128, 256], BF16) for i in range(NH)]
            jks = mk(s7, "jk", NH, [128, 256], BF16)
            stats = mk(s7, "stM", 2, [128, 2, 8], F32)
            for vx, r_vx in vxs:
                fw.op("dve", lambda e: e.memset(vx[:, :, 256:257], 1.0), writes=[r_vx])
            for hg in ((0, 1), (2, 3)):
                if hg == (2, 3):
                    fw.dma("sp", hTh[:], HT_d[:, 0:1024].rearrange("(k p) t -> p k t", p=128), writes=[r_in])
                for i, h in enumerate(hg):
                    fw.dma("sp", qTs[i][0][:], MQT_d[h * 256:(h + 1) * 256, :].rearrange("(k p) t -> p k t", p=128), writes=[qTs[i][1]])
                    fw.dma("sp", kTs[i][0][:], MKT_d[h * 256:(h + 1) * 256, :].rearrange("(k p) t -> p k t", p=128), writes=[kTs[i][1]])
                    fw.dma("sp", vxs[i][0][:, :, 0:256], MV_d[:, h * 256:(h + 1) * 256].rearrange("(t p) c -> p t c", p=128), writes=[vxs[i][1]])
                    fw.dma("sp", mgs[i][0][:], MG_d[:, h * 256:(h + 1) * 256].rearrange("(t p) c -> p t c", p=128), writes=[mgs[i][1]])
                    fw.dma("sp", Brs[i][0][:].rearrange("p (o t) -> p o t", o=1), BHv[h:h + 1, :].partition_broadcast(128),
                           reads=[r_bh], writes=[Brs[i][1]])
                    fw.op("act", lambda e: e.activation(out=EBs[i][0][:], in_=Brs[i][0][:], func=AF.Exp), reads=[Brs[i][1]], writes=[EBs[i][1]])
                    fw.op("dve", lambda e: e.tensor_tensor(out=BMs[i][0][:], in0=Brs[i][0][:], in1=Mneg[:], op=ALU.add),
                          reads=[Brs[i][1], r_gp], writes=[BMs[i][1]])
                def P1a(c, hg=hg):
                    cols = slice(c * 128, (c + 1) * 128)
                    for i, h in enumerate(hg):
                        qT, r_qT = qTs[i]
                        kT, r_kT = kTs[i]
                        Br, r_Br = BMs[i]
                        EB, r_EB = EBs[i]
                        qeb, r_qeb = qeb2[i][c % 3]
                        DT, r_DT = DTs[i]
                        PT, r_PT = PT2[i][c % 3]
                        kw, r_kw = kws[i]
                        col = h * 16 + c
                        if c > 0:
                            for k in range(2):
                                fw.op("pool", lambda e: e.tensor_tensor(out=qeb[:, k, :], in0=qT[:, k, cols], in1=EB[:, cols], op=ALU.mult),
                                      reads=[r_qT, r_EB], writes=[r_qeb])
                        bs, r_bs = nb()
                        for k in range(2):
                            fw.op("pe", lambda e: e.matmul(bs[:, 0:128], lhsT=kT[:, k, cols], rhs=qT[:, k, cols], start=(k == 0), stop=(k == 1)),
                                  reads=[r_kT, r_qT], writes=[r_bs])
                        fw.op("act", lambda e: e.activation(out=DT[:], in_=Br[:, cols], func=AF.Exp, bias=At[:, col:col + 1], scale=1.0),
                              reads=[r_Br, r_gp], writes=[r_DT])
                        fw.op("dve", lambda e: e.tensor_tensor(out=PT[:], in0=bs[:, 0:128], in1=DT[:], op=ALU.mult), reads=[r_bs, r_DT], writes=[r_PT])
                        if c < 15:
                            bkt, r_bkt = nb()
                            bktb = b16(bkt, 8)
                            for k in range(2):
                                fw.op("pe", lambda e: e.transpose(out=bktb[:, k, :], in_=kT[:, k, cols], identity=identb[:]), reads=[r_kT], writes=[r_bkt])
                            fw.op("dve", lambda e: e.tensor_scalar(out=kw[:], in0=bktb[:, 0:2, :], scalar1=WS[:, col:col + 1], scalar2=None, op0=ALU.mult),
                                  reads=[r_bkt, r_gp], writes=[r_kw])

                def P1b(c, hg=hg):
                    if c >= 15:
                        return
                    for i, h in enumerate(hg):
                        vx, r_vx = vxs[i]
                        EB, r_EB = EBs[i]
                        Cf, r_Cf = Cfs[i]
                        Cb, r_Cb = Cb3[i][c % 3]
                        kw, r_kw = kws[i]
                        for kc in range(2):
                            bu, r_bu = nb()
                            fw.op("pe", lambda e: e.matmul(bu[:, 0:257], lhsT=kw[:, kc, :], rhs=vx[:, c, 0:257], start=True, stop=True),
                                  reads=[r_kw, r_vx], writes=[r_bu])
                            if c == 0:
                                fw.op("dve", lambda e: e.tensor_copy(out=Cf[:, kc, 0:257], in_=bu[:, 0:257]), reads=[r_bu], writes=[r_Cf])
                            else:
                                fw.op("dve", lambda e: e.scalar_tensor_tensor(out=Cf[:, kc, 0:257], in0=Cf[:, kc, 0:257],
                                                                              scalar=EB[:, c * 128 + 127:c * 128 + 128], in1=bu[:, 0:257],
                                                                              op0=ALU.mult, op1=ALU.add),
                                      reads=[r_bu, r_Cf, r_EB], writes=[r_Cf])
                        fw.op("act", lambda e: e.copy(out=Cb[:, :, 0:257], in_=Cf[:, :, 0:257]), reads=[r_Cf], writes=[r_Cb])

                def P2a(c, hg=hg):
                    sc, r_sc = stats[c % 2]
                    bos = []
                    for i, h in enumerate(hg):
                        vx, r_vx = vxs[i]
                        qeb, r_qeb = qeb2[i][c % 3]
                        PT, r_PT = PT2[i][c % 3]
                        jk, r_jk = jks[i]
                        bo, r_bo = nb()
                        bos.append((bo, r_bo))
                        if c > 0:
                            Cb, r_Cb = Cb3[i][(c - 1) % 3]
                            for k in range(2):
                                fw.op("pe", lambda e: e.matmul(bo[:, 0:257], lhsT=qeb[:, k, :], rhs=Cb[:, k, 0:257], start=(k == 0), stop=False),
                                      reads=[r_qeb, r_Cb], writes=[r_bo])
                        fw.op("pe", lambda e: e.matmul(bo[:, 0:257], lhsT=PT[:], rhs=vx[:, c, 0:257], start=(c == 0), stop=True),
                              reads=[r_PT, r_vx], writes=[r_bo])
                        fw.op("act", lambda e: e.activation(out=sc[:, i, 0:1], in_=bo[:, 256:257], func=AF.Square), reads=[r_bo], writes=[r_sc])
                        fw.op("act", lambda e: e.activation(out=jk[:], in_=bo[:, 0:256], func=AF.Square, accum_out=sc[:, i, 1:2]), reads=[r_bo], writes=[r_jk, r_sc])
                    return bos

                def P2b(c, bos, hg=hg):
                    sc, r_sc = stats[c % 2]
                    fw.op("dve", lambda e: e.tensor_scalar(out=sc[:, :, 2], in0=sc[:, :, 0], scalar1=1.0, scalar2=EPS, op0=ALU.max, op1=ALU.mult),
                          reads=[r_sc], writes=[r_sc])
                    fw.op("dve", lambda e: e.scalar_tensor_tensor(out=sc[:, :, 3], in0=sc[:, :, 1], scalar=1.0 / 256, in1=sc[:, :, 2], op0=ALU.mult, op1=ALU.add),
                          reads=[r_sc], writes=[r_sc])
                    fw.op("act", lambda e: e.activation(out=sc[:, :, 4], in_=sc[:, :, 3], func=AF.Ln), reads=[r_sc], writes=[r_sc])
                    fw.op("act", lambda e: e.activation(out=sc[:, :, 5], in_=sc[:, :, 4], func=AF.Exp, scale=-0.5), reads=[r_sc], writes=[r_sc])
                    for i, h in enumerate(hg):
                        mg, r_mg = mgs[i]
                        hmt, r_hmt = hmt2[i][c % 2]
                        bo, r_bo = bos[i]
                        fw.op("dve", lambda e: e.scalar_tensor_tensor(out=hmt[:], in0=bo[:, 0:256], scalar=sc[:, i, 5:6], in1=mg[:, c, :],
                                                                      op0=ALU.mult, op1=ALU.mult),
                              reads=[r_bo, r_sc, r_mg], writes=[r_hmt])

                def P2tail(c, hg=hg):
                    cols = slice(c * 128, (c + 1) * 128)
                    for i, h in enumerate(hg):
                        hmT, r_hmT = hmTs[i]
                        hmt, r_hmt = hmt2[i][c % 2]
                        bt, r_bt = nb()
                        btb = b16(bt, 8)
                        for kc in range(2):
                            fw.op("pe", lambda e: e.transpose(out=btb[:, kc, :], in_=hmt[:, kc * 128:(kc + 1) * 128], identity=identb[:]),
                                  reads=[r_hmt], writes=[r_bt])
                        fw.op("act", lambda e: e.copy(out=hmT[:, 0:2, cols], in_=btb[:, 0:2, :]), reads=[r_bt], writes=[r_hmT])

                P1a(0)
                P1b(0)
                P1a(1)
                P1b(1)
                for c in range(16):
                    if c + 2 < 16:
                        P1a(c + 2)
                    bos_c = P2a(c)
                    if c > 0:
                        P2tail(c - 1)
                    P2b(c, bos_c)
                    if c + 2 < 16:
                        P1b(c + 2)
                P2tail(15)
                for i, h in enumerate(hg):
                    fw.dma("sp", HMT_d[h * 256:(h + 1) * 256, :].rearrange("(k p) t -> p k t", p=128), hmTs[i][0][:], reads=[hmTs[i][1]])
            fw.barrier()

        if upto >= 6:
          with contextlib.ExitStack() as sD:
            gfin = sbuf(sD, "gfin", [128, D], F32)
            mergedT = sbuf(sD, "mergedT", [128, 16, 1024], BF16)
            wpg = WPool(sD, "wpg", 2, 16 * 384)
            wpb = WPool(sD, "wpb", 2, 8 * 384)
            wpoa = WPool(sD, "wpoa", 2, 16 * 512)
            r_gfin = Res()

            def c1_pre(th):
                if th > 0:
                    fw.dma("sp", hTh[:], HT_d[:, th * 1024:(th + 1) * 1024].rearrange("(k p) t -> p k t", p=128), writes=[r_in])
                return {0: (wpg.load(wG_d[0], 16, 384), wpb.load(wBR_d[0], 8, 384))}

            wl_next = c1_pre(0)
            fw.dma("sp", gfin[:], gf_d, writes=[r_gfin])
            for th in range(2):
                t0 = th * 1024
                with contextlib.ExitStack() as c1:
                    hbs = [sbuf(c1, "hb%d" % b, [128, 8, 1024], BF16) for b in range(3)]
                    r_hb = [Res() for _ in range(3)]
                    sgs = mk(c1, "sg", 2, [128, 512], F32)
                    tmps = mk(c1, "tmpc", 2, [128, 512], F32)
                    acc = sbuf(c1, "acc", [128, 1024], F32)
                    r_acc = Res()
                    r_mT = Res()
                    wl = wl_next
                    for b, src in enumerate((HMT_d, HAT_d, HCT_d)):
                        fw.dma("sp", hbs[b][:], src[:, t0:t0 + 1024].rearrange("(k p) t -> p k t", p=128), writes=[r_hb[b]])
                    for c in range(16):
                        (wgv, wgr), (wbv, wbr) = wl.pop(c)
                        for b in range(3):
                            if b == 1 and c + 1 < 16:
                                wl[c + 1] = (wpg.load(wG_d[c + 1], 16, 384), wpb.load(wBR_d[c + 1], 8, 384))
                            if b == 2 and c == 13:
                                wos_pre = [wpoa.load(wOUT_d[n4], 16, 512) for n4 in range(2)]
                            gb = [nb(), nb()]
                            rb = [nb(), nb()]
                            for kc in range(16):
                                for tb in range(2):
                                    fw.op("pe", lambda e: e.matmul(gb[tb][0][:, 0:512], lhsT=wgv[:, kc, b * 128:(b + 1) * 128],
                                                                   rhs=hTh[:, kc, tb * 512:(tb + 1) * 512], start=(kc == 0), stop=(kc == 15)),
                                          reads=[wgr, r_in], writes=[gb[tb][1]])
                            for kc in range(8):
                                for tb in range(2):
                                    fw.op("pe", lambda e: e.matmul(rb[tb][0][:, 0:512], lhsT=wbv[:, kc, b * 128:(b + 1) * 128],
                                                                   rhs=hbs[b][:, kc, tb * 512:(tb + 1) * 512], start=(kc == 0), stop=(kc == 7)),
                                          reads=[wbr, r_hb[b]], writes=[rb[tb][1]])
                            for tb in range(2):
                                tc_ = slice(tb * 512, (tb + 1) * 512)
                                sg, r_sg = rot(sgs)
                                fw.op("act", lambda e: e.activation(out=sg[:], in_=gb[tb][0][:, 0:512], func=AF.Sigmoid), reads=[gb[tb][1]], writes=[r_sg])
                                if b == 0:
                                    fw.op("dve", lambda e: e.tensor_tensor(out=acc[:, tc_], in0=sg[:], in1=rb[tb][0][:, 0:512], op=ALU.mult),
                                          reads=[r_sg, rb[tb][1]], writes=[r_acc])
                                else:
                                    tmp, r_tmp = rot(tmps)
                                    fw.op("dve", lambda e: e.tensor_tensor(out=tmp[:], in0=sg[:], in1=rb[tb][0][:, 0:512], op=ALU.mult),
                                          reads=[r_sg, rb[tb][1]], writes=[r_tmp])
                                    if b == 1:
                                        fw.op("pool", lambda e: e.tensor_tensor(out=acc[:, tc_], in0=acc[:, tc_], in1=tmp[:], op=ALU.add),
                                              reads=[r_tmp, r_acc], writes=[r_acc])
                                    else:
                                        fw.op("pool", lambda e: e.tensor_tensor(out=mergedT[:, c, tc_], in0=acc[:, tc_], in1=tmp[:], op=ALU.add),
                                              reads=[r_tmp, r_acc], writes=[r_mT])
                    fw.barrier()
                with contextlib.ExitStack() as c2:
                    wpo = WPool(c2, "wpo", 2, 16 * 512)
                    wos = wos_pre + [wpo.load(wOUT_d[n4], 16, 512) for n4 in range(2, 4)]
                    xts = mk(c2, "xo", 3, [128, D], F32)
                    fw.dma("sp", xts[0][0][:], x_d[t0:t0 + 128, :], writes=[xts[0][1]])
                    jk = sbuf(c2, "jko", [128, D], BF16)
                    r_jk = Res()
                    stats = mk(c2, "stO", 2, [128, 4], F32)
                    for t8 in range(8):
                        xt, r_xt = xts[t8 % 3]
                        sc, r_sc = rot(stats)
                        rows = slice(t0 + t8 * 128, t0 + (t8 + 1) * 128)
                        if t8 + 1 < 8:
                            fw.dma("sp", xts[(t8 + 1) % 3][0][:], x_d[t0 + (t8 + 1) * 128:t0 + (t8 + 2) * 128, :], writes=[xts[(t8 + 1) % 3][1]])
                        for n4 in range(4):
                            wov, wor = wos[n4]
                            bk, r_bk = nb()
                            for kc in range(16):
                                fw.op("pe", lambda e: e.matmul(bk[:, 0:512], lhsT=mergedT[:, kc, t8 * 128:(t8 + 1) * 128], rhs=wov[:, kc, 0:512],
                                                               start=(kc == 0), stop=(kc == 15)),
                                      reads=[wor], writes=[r_bk])
                            fw.op("dve", lambda e: e.tensor_tensor(out=xt[:, n4 * 512:(n4 + 1) * 512], in0=bk[:, 0:512], in1=xt[:, n4 * 512:(n4 + 1) * 512], op=ALU.add),
                                  reads=[r_bk, r_xt], writes=[r_xt])
                        fw.op("act", lambda e: e.activation(out=jk[:], in_=xt[:], func=AF.Square, accum_out=sc[:, 0:1]), reads=[r_xt], writes=[r_jk, r_sc])
                        fw.op("act", lambda e: e.activation(out=sc[:, 1:2], in_=sc[:, 0:1], func=AF.Ln, scale=1.0 / D, bias=EPS), reads=[r_sc], writes=[r_sc])
                        fw.op("act", lambda e: e.activation(out=sc[:, 2:3], in_=sc[:, 1:2], func=AF.Exp, scale=-0.5), reads=[r_sc], writes=[r_sc])
                        fw.op("dve", lambda e: e.scalar_tensor_tensor(out=xt[:], in0=xt[:], scalar=sc[:, 2:3], in1=gfin[:], op0=ALU.mult, op1=ALU.mult),
                              reads=[r_xt, r_sc, r_gfin], writes=[r_xt])
                        fw.dma("sp", out_d[rows, :], xt[:], reads=[r_xt])
                        if th == 0 and t8 == 3:
                            wl_next = c1_pre(1)
                    fw.barrier()
        fw.final_wait("sp")
    return nc


def _tile_w(W):
    K, n = W.shape
    return np.ascontiguousarray(W.reshape(K // 128, 128, n).transpose(1, 0, 2))


def prep_shared(inp):
    f = np.float32
    w_in = np.asarray(inp["w_in"], f)[0]
    ar = np.arange
    blocks = []
    for h in range(4):
        blocks.append(np.concatenate([O_QK + h * 256 + ar(256), O_QK + 1024 + h * 256 + ar(256)]))
    for b in range(2):
        blocks.append(O_V + b * 512 + ar(512))
    for h in range(4):
        blocks.append(np.concatenate([O_O + h * 256 + ar(256), O_Z + h * 256 + ar(256)]))
    blocks.append(O_CQ + ar(512))
    blocks.append(O_CKV + ar(512))
    for b in range(2):
        blocks.append(O_AZ + b * 512 + ar(512))
    for b in range(2):
        blocks.append(O_CQC + b * 512 + ar(512))
    for b in range(2):
        blocks.append(O_CZ + b * 512 + ar(512))
    sh = {}
    sh["wA"] = np.stack([_tile_w(w_in[:, c]) for c in blocks])
    swap = np.concatenate([32 + ar(32), ar(32)])
    sh["wKR"] = _tile_w(w_in[:, np.concatenate([O_KR + ar(64), O_KR + swap])])
    sh["wIF"] = _tile_w(w_in[:, np.concatenate([O_I + ar(4), O_F + ar(4)])])
    w_uq = np.asarray(inp["w_uq"], f)[0]
    uq = []
    for pr in range(4):
        cols = []
        for hh in (2 * pr, 2 * pr + 1):
            cols += [hh * 192 + ar(128), hh * 192 + 128 + ar(64), hh * 192 + 128 + swap]
        uq.append(_tile_w(w_uq[:, np.concatenate(cols)]))
    sh["wUQ"] = np.stack(uq)
    w_ukv = np.asarray(inp["w_ukv"], f)[0]
    sh["wUK"] = np.stack([_tile_w(w_ukv[:, np.concatenate([hh * 256 + ar(128) for hh in range(4 * b, 4 * b + 4)])]) for b in range(2)])
    sh["wUV"] = np.stack([_tile_w(w_ukv[:, np.concatenate([hh * 256 + 128 + ar(128) for hh in range(4 * b, 4 * b + 4)])]) for b in range(2)])
    w_mem = np.asarray(inp["w_mem_kv"], f)[0]
    sh["wMEM"] = np.stack([_tile_w(w_mem[:, b * 512:(b + 1) * 512]) for b in range(4)])
    wbm = np.asarray(inp["w_br_m"], f)[0]
    wba = np.asarray(inp["w_br_a"], f)[0]
    wbc = np.asarray(inp["w_br_c"], f)[0]
    sh["wG"] = np.stack([_tile_w(w_in[:, np.concatenate([O_G + b * 2048 + c * 128 + ar(128) for b in range(3)])]) for c in range(16)])
    sh["wBR"] = np.stack([_tile_w(np.concatenate([w[:, c * 128:(c + 1) * 128] for w in (wbm, wba, wbc)], axis=1)) for c in range(16)])
    w_out = np.asarray(inp["w_out"], f)[0]
    sh["wOUT"] = np.stack([_tile_w(w_out[:, b * 512:(b + 1) * 512]) for b in range(4)])
    bc = lambda v, n=128: np.ascontiguousarray(np.broadcast_to(np.asarray(v, f).reshape(1, -1), (n, np.asarray(v).size)))
    sh["g_norm"] = bc(inp["norm"][0])
    sh["g_final"] = bc(inp["final_norm"])
    sh["g_mem"] = bc(inp["mem_norm"][0])
    sh["g_cq"] = bc(inp["cq_norm"][0])
    sh["g_ckv"] = bc(inp["ckv_norm"][0])
    sh["g_mh"] = bc(inp["mh_norm"][0])
    conv_w = np.asarray(inp["conv_w"], f)[0]
    conv_b = np.asarray(inp["conv_b"], f)[0]
    cw = np.zeros((128, 16, 4), f)
    cb = np.zeros((128, 16), f)
    for h in range(4):
        for ci in range(4):
            base = (0 if ci < 2 else 1024) + h * 256 + (ci % 2) * 128
            cw[:, 4 * h + ci, :] = conv_w[:, base:base + 128].T
            cb[:, 4 * h + ci] = conv_b[base:base + 128]
    sh["convw"] = cw
    sh["convb"] = cb
    sh["bif"] = bc(np.concatenate([np.asarray(inp["b_igate"], f)[0], np.asarray(inp["b_fgate"], f)[0]]))
    inv_freq = (np.float32(10000.0) ** (-(np.arange(0, 64, 2, dtype=np.float32)) / np.float32(64))).astype(f)
    sh["invf"] = np.concatenate([inv_freq, inv_freq]).reshape(64, 1).astype(f)
    sh["sgn"] = np.concatenate([-np.ones(32, f), np.ones(32, f)]).reshape(64, 1)
    sh["ident"] = np.eye(128, dtype=f)
    sh["mask"] = np.triu(np.ones((128, 128), f))
    return sh


def prep_core(inp, b):
    return {
        "x": np.ascontiguousarray(np.asarray(inp["x"], np.float32)[b]),
        "mem": np.ascontiguousarray(np.asarray(inp["mem"], np.float32)[b]),
        "pos64": np.ascontiguousarray(np.broadcast_to(np.asarray(inp["positions"], np.int32)[b].reshape(1, S), (64, S))),
    }


def kernel(**inputs):
    sh = prep_shared(inputs)
    nc = build()
    in_maps = []
    for b in range(8):
        m = dict(sh)
        m.update(prep_core(inputs, b))
        in_maps.append(m)
    res = run_bass_kernel_spmd(nc, in_maps, core_ids=list(range(8)))
    return np.stack([np.asarray(r["out"], np.float32) for r in res.results], axis=0)
```

```python
import math
import contextlib
import numpy as np
import concourse.bass as bass
import concourse.mybir as mybir
from concourse.bass_utils import run_bass_kernel_spmd

F32 = mybir.dt.float32
BF16 = mybir.dt.bfloat16
I32 = mybir.dt.int32
AF = mybir.ActivationFunctionType
ALU = mybir.AluOpType

S = 2048
D = 2048
NT = 16
EPS = 1e-6
O_QK, O_V, O_O, O_Z, O_I, O_F, O_CQ, O_CKV, O_KR, O_AZ, O_CQC, O_CZ, O_G = (
    0, 2048, 3072, 4096, 5120, 5124, 5128, 5640, 6152, 6216, 7240, 8264, 9288)
LN16 = math.log(16.0)
PI = math.pi
PI_SAFE = 3.1415925
C1 = 6.28125
C2 = 2 * math.pi - 6.28125


class Res:
    __slots__ = ("w", "r")

    def __init__(self):
        self.w = None
        self.r = {}


class FW:
    NS = 6

    def __init__(self, nc, es):
        self.nc = nc
        self.eng = {"pe": nc.tensor, "act": nc.scalar, "dve": nc.vector, "pool": nc.gpsimd, "sp": nc.sync}
        self.sem = {k: es.enter_context(nc.semaphore("s_" + k)) for k in self.eng}
        self.cnt = {k: 0 for k in self.eng}
        self.seen = {k: {} for k in self.eng}
        self.dq = {}
        for q in ("sp", "pool", "act"):
            sems = [es.enter_context(nc.semaphore("d_%s%d" % (q, i))) for i in range(self.NS)]
            self.dq[q] = {"sems": sems, "vals": [0] * self.NS, "i": 0}

    def _semh(self, key):
        if isinstance(key, str):
            return self.sem[key]
        return self.dq[key[0]]["sems"][key[1]]

    def _wait(self, e, ev):
        if ev is None:
            return
        key, val = ev
        if e == "pe" and key == "pe":
            return
        if self.seen[e].get(key, 0) >= val:
            return
        self.seen[e][key] = val
        self.eng[e].wait_ge(self._semh(key), val)

    def _deps(self, e, reads, writes):
        for r in reads:
            self._wait(e, r.w)
        for w in writes:
            self._wait(e, w.w)
            for ev in list(w.r.values()):
                self._wait(e, ev)

    def _mark(self, me, reads, writes):
        for r in reads:
            r.r[me[0]] = me
        for w in writes:
            w.w = me
            w.r = {}

    def op(self, e, fn, reads=(), writes=()):
        self._deps(e, reads, writes)
        ins = fn(self.eng[e])
        self.cnt[e] += 1
        ins.then_inc(self.sem[e], 1)
        self._mark((e, self.cnt[e]), reads, writes)

    def dma(self, q, out, in_, reads=(), writes=(), **kw):
        self._deps(q, reads, writes)
        d = self.dq[q]
        i = d["i"]
        d["i"] = (i + 1) % self.NS
        key = (q, i)
        if d["vals"][i]:
            self._wait(q, (key, d["vals"][i]))
        ins = self.eng[q].dma_start(out=out, in_=in_, **kw)
        d["vals"][i] += 16
        ins.then_inc(d["sems"][i], 16)
        self._mark((key, d["vals"][i]), reads, writes)

    def all_events(self):
        evs = [(k, self.cnt[k]) for k in self.eng if self.cnt[k]]
        for q, d in self.dq.items():
            for i in range(self.NS):
                if d["vals"][i]:
                    evs.append(((q, i), d["vals"][i]))
        return evs

    def barrier(self):
        evs = self.all_events()
        for e in self.eng:
            for ev in evs:
                if ev[0] != e:
                    self._wait(e, ev)

    def final_wait(self, e="sp"):
        for ev in self.all_events():
            if ev[0] != e:
                self._wait(e, ev)


def build(debug=False, upto=99):
    nc = bass.Bass("TRN2", target_bir_lowering=False)

    def din(name, shape, dt=F32):
        return nc.dram_tensor(name, list(shape), dt, kind="ExternalInput").ap()

    skind = "ExternalOutput" if debug else "Internal"

    def dscr(name, shape, dt=BF16):
        return nc.dram_tensor(name, list(shape), dt, kind=skind).ap()

    x_d = din("x", [S, D])
    mem_d = din("mem", [256, D])
    pos_d = din("pos64", [64, S], I32)
    invf_d = din("invf", [64, 1])
    sgn_d = din("sgn", [64, 1])
    ident_d = din("ident", [128, 128])
    mask_d = din("mask", [128, 128])
    gn_d = din("g_norm", [128, D])
    gf_d = din("g_final", [128, D])
    gm_d = din("g_mem", [128, D])
    gcq_d = din("g_cq", [128, 512])
    gckv_d = din("g_ckv", [128, 512])
    gmh_d = din("g_mh", [128, 1024])
    convw_d = din("convw", [128, 16, 4])
    convb_d = din("convb", [128, 16])
    bif_d = din("bif", [128, 8])
    wA_d = din("wA", [18, 128, 16, 512])
    wKR_d = din("wKR", [128, 16, 128])
    wIF_d = din("wIF", [128, 16, 8])
    wUQ_d = din("wUQ", [4, 128, 4, 512])
    wUK_d = din("wUK", [2, 128, 4, 512])
    wUV_d = din("wUV", [2, 128, 4, 512])
    wMEM_d = din("wMEM", [4, 128, 16, 512])
    wG_d = din("wG", [16, 128, 16, 384])
    wBR_d = din("wBR", [16, 128, 8, 384])
    wOUT_d = din("wOUT", [4, 128, 16, 512])
    out_d = nc.dram_tensor("out", [S, D], F32, kind="ExternalOutput").ap()

    HT_d = dscr("HT", [D, S])
    MQT_d = dscr("MQT", [1024, S])
    MKT_d = dscr("MKT", [1024, S])
    MV_d = dscr("MV", [S, 1024])
    MG_d = dscr("MG", [S, 1024])
    KRT_d = dscr("KRT", [64, S])
    AZ_d = dscr("AZ", [S, 1024])
    CQT_d = dscr("CQT", [1024, S])
    CZ_d = dscr("CZ", [S, 1024])
    QNT_d = dscr("QNT", [1024, S])
    QRT_d = dscr("QRT", [512, S])
    KNT_d = dscr("KNT", [1024, S])
    VA_d = dscr("VA", [S, 1024])
    BH_d = dscr("BH", [64, 128], F32)
    HMT_d = dscr("HMT", [1024, S])
    HAT_d = dscr("HAT", [1024, S])
    HCT_d = dscr("HCT", [1024, S])
    GIF_d = dscr("GIF", [128, 128], F32)

    with contextlib.ExitStack() as es:
        fw = FW(nc, es)
        banks = [(es.enter_context(nc.psum_tensor("bk%d" % i, [128, 512], F32)), Res()) for i in range(8)]
        bi = [0]

        def nb():
            i = bi[0]
            bi[0] = (i + 1) % 8
            return banks[i][0], banks[i][1]

        bhi = [0]

        def nbhi():
            i = 4 + bhi[0]
            bhi[0] = (bhi[0] + 1) % 4
            return banks[i][0], banks[i][1]

        def b16(bk, k):
            return bk[:].bitcast(BF16).rearrange("p (k n) -> p k n", k=k)

        nsb = [0]

        def sbuf(stack, name, shape, dt):
            nsb[0] += 1
            return stack.enter_context(nc.sbuf_tensor("sb%d_%s" % (nsb[0], name), list(shape), dt))

        R0 = Res()

        identb = sbuf(es, "identb", [128, 128], BF16)
        identf = sbuf(es, "identf", [128, 128], F32)
        maskb = sbuf(es, "maskb", [128, 128], BF16)
        maskf = sbuf(es, "maskf", [128, 128], F32)
        onesf = sbuf(es, "onesf", [128, 128], F32)
        GI = sbuf(es, "GI", [128, 4, 16], F32)
        GF = sbuf(es, "GF", [128, 4, 16], F32)
        fw.dma("pool", identb[:], ident_d, writes=[R0])
        fw.dma("sp", identf[:], ident_d, writes=[R0])
        fw.dma("pool", maskb[:], mask_d, writes=[R0])
        fw.dma("sp", maskf[:], mask_d, writes=[R0])
        fw.op("dve", lambda e: e.memset(onesf[:], 1.0), writes=[R0])
        fw.barrier()

        def rmsnorm_T(st2, src_d, ntiles, g_d, dstT, hooks=None):
            xts = [(sbuf(st2, "rn_xt%d" % i, [128, D], F32), Res()) for i in range(4)]
            xbs = [(sbuf(st2, "rn_xb%d" % i, [128, D], BF16), Res()) for i in range(3)]
            junk = sbuf(st2, "rn_junk", [128, D], BF16)
            r_junk = Res()
            gbc = sbuf(st2, "rn_g", [128, D], F32)
            r_g = Res()
            stt = sbuf(st2, "rn_st", [128, 3 * ntiles], F32)
            r_sts = [Res() for _ in range(ntiles)]
            r_dst = Res()
            fw.dma("sp", gbc[:], g_d, writes=[r_g])
            def stats_part(tt):
                xt, r_xt = xts[tt % 4]
                r_st = r_sts[tt]
                c = 3 * tt
                fw.dma("sp", xt[:], src_d[tt * 128:(tt + 1) * 128, :], writes=[r_xt])
                fw.op("act", lambda e: e.activation(out=junk[:], in_=xt[:], func=AF.Square, accum_out=stt[:, c:c + 1]),
                      reads=[r_xt], writes=[r_junk, r_st])
                fw.op("act", lambda e: e.activation(out=stt[:, c + 1:c + 2], in_=stt[:, c:c + 1], func=AF.Ln, scale=1.0 / D, bias=EPS),
                      reads=[r_st], writes=[r_st])
                fw.op("act", lambda e: e.activation(out=stt[:, c + 2:c + 3], in_=stt[:, c + 1:c + 2], func=AF.Exp, scale=-0.5),
                      reads=[r_st], writes=[r_st])

            def write_part(tt):
                xt, r_xt = xts[tt % 4]
                xb, r_xb = xbs[tt % 3]
                r_st = r_sts[tt]
                c = 3 * tt
                fw.op("dve", lambda e: e.scalar_tensor_tensor(out=xb[:], in0=xt[:], scalar=stt[:, c + 2:c + 3], in1=gbc[:],
                                                              op0=ALU.mult, op1=ALU.mult),
                      reads=[r_xt, r_st, r_g], writes=[r_xb])

            def tr_part(tt):
                xb, r_xb = xbs[tt % 3]
                for half in range(2):
                    bk, r_bk = nb()
                    bkb = b16(bk, 8)
                    for k in range(8):
                        kk = half * 8 + k
                        fw.op("pe", lambda e: e.transpose(out=bkb[:, k, :], in_=xb[:, kk * 128:(kk + 1) * 128], identity=identb[:]),
                              reads=[r_xb], writes=[r_bk])
                    if half == 0:
                        fw.op("act", lambda e: e.copy(out=dstT[:, half * 8:(half + 1) * 8, tt * 128:(tt + 1) * 128], in_=bkb),
                              reads=[r_bk], writes=[r_dst])
                    else:
                        fw.op("dve", lambda e: e.tensor_copy(out=dstT[:, half * 8:(half + 1) * 8, tt * 128:(tt + 1) * 128], in_=bkb),
                              reads=[r_bk], writes=[r_dst])

            stats_part(0)
            if ntiles > 1:
                stats_part(1)
            write_part(0)
            for tt in range(ntiles):
                if tt + 2 < ntiles:
                    stats_part(tt + 2)
                    if hooks and (tt + 2) in hooks:
                        hooks[tt + 2](xts[(tt + 2) % 4][1])
                if tt + 1 < ntiles:
                    write_part(tt + 1)
                tr_part(tt)
            return r_dst

        class WPool:
            def __init__(self, stack, name, nslots, elems):
                self.t = [sbuf(stack, "%s%d" % (name, i), [128, elems], BF16) for i in range(nslots)]
                self.r = [Res() for _ in range(nslots)]
                self.i = 0

            def load(self, dram_ap, KC, ncols):
                i = self.i
                self.i = (i + 1) % len(self.t)
                view = self.t[i][:, 0:KC * ncols].rearrange("p (k n) -> p k n", k=KC)
                tot = KC * ncols
                bsz = max(d for d in range(1, 1025) if tot % d == 0)
                fw.dma("pool", self.t[i][:, 0:tot].rearrange("p (a b) -> p a b", b=bsz),
                       dram_ap.rearrange("p k n -> p (k n)").rearrange("p (a b) -> p a b", b=bsz), writes=[self.r[i]])
                return view, self.r[i]

        def proj_fm(wv, wr, chunks, src, r_src, KC, ntok, evac):
            ntb = (ntok + 511) // 512
            for ci, (c0, M) in enumerate(chunks):
                bks = [nb() for _ in range(ntb)]
                for kc in range(KC):
                    for tb in range(ntb):
                        n = min(512, ntok - tb * 512)
                        bk, r_bk = bks[tb]
                        fw.op("pe", lambda e: e.matmul(bk[0:M, 0:n], lhsT=wv[:, kc, c0:c0 + M], rhs=src[:, kc, tb * 512:tb * 512 + n],
                                                       start=(kc == 0), stop=(kc == KC - 1)),
                              reads=[wr, r_src], writes=[r_bk])
                for tb in range(ntb):
                    n = min(512, ntok - tb * 512)
                    evac(ci, tb, bks[tb][0][0:M, 0:n], bks[tb][1])

        def proj_tm(wv, wr, ncols, src, r_src, KC, ntiles, evac):
            for tt in range(ntiles):
                bk, r_bk = nb()
                for kc in range(KC):
                    fw.op("pe", lambda e: e.matmul(bk[:, 0:ncols], lhsT=src[:, kc, tt * 128:(tt + 1) * 128], rhs=wv[:, kc, 0:ncols],
                                                   start=(kc == 0), stop=(kc == KC - 1)),
                          reads=[wr, r_src], writes=[r_bk])
                evac(tt, bk[:, 0:ncols], r_bk)

        with contextlib.ExitStack() as sA:
            cosT = sbuf(sA, "cosT", [64, S], F32)
            sinS = sbuf(sA, "sinS", [64, S], F32)
            hT = sbuf(sA, "hT", [128, 16, S], BF16)
            wpA = WPool(sA, "wpA", 3, 16 * 512)
            preA = [wpA.load(wA_d[i], 16, 512) for i in range(3)]

            def pre_hook(lo, hi):
                def f(r_x):
                    fw._wait("pool", r_x.w)
                    for i in range(lo, hi):
                        preA.append(wpA.load(wA_d[i], 16, 512))
                return f
            if True:
                with contextlib.ExitStack() as s1:
                    rmsnorm_T(s1, x_d, NT, gn_d, hT)
                    fw.barrier()
            if upto >= 2:
              with contextlib.ExitStack() as s2:
                wp = wpA
                cqnT = sbuf(s2, "cqnT", [128, 4, S], BF16)
                ckvnT = sbuf(s2, "ckvnT", [128, 4, S], BF16)
                pre = sbuf(s2, "pre", [128, S + 4], F32)
                r_pre = Res()
                ybuf = sbuf(s2, "ybuf", [128, S], F32)
                r_y = Res()
                obs = [(sbuf(s2, "ob%d" % i, [128, S], BF16), Res()) for i in range(2)]
                t32s = [(sbuf(s2, "t32_%d" % i, [128, 512], F32), Res()) for i in range(3)]
                t16s = [(sbuf(s2, "t16_%d" % i, [128, 512], BF16), Res()) for i in range(4)]
                rot_i = {}

                def rot(lst):
                    i = rot_i.get(id(lst), 0)
                    rot_i[id(lst)] = (i + 1) % len(lst)
                    return lst[i]

                stA = sbuf(s2, "stA", [128, 16], F32)
                r_st = Res()
                cw = sbuf(s2, "cw", [128, 16, 4], F32)
                cb = sbuf(s2, "cb", [128, 16], F32)
                bif = sbuf(s2, "bif", [128, 8], F32)
                gcq = sbuf(s2, "gcq", [128, 512], F32)
                gckv = sbuf(s2, "gckv", [128, 512], F32)
                gmhA = sbuf(s2, "gmhA", [128, 1024], F32)
                r_k = Res()
                r_g = Res()
                r_cn = Res()
                fw.dma("sp", cw[:], convw_d, writes=[r_k])
                fw.dma("sp", cb[:], convb_d, writes=[r_k])
                fw.dma("sp", bif[:], bif_d, writes=[r_k])
                fw.dma("sp", gcq[:], gcq_d, writes=[r_k])
                fw.dma("sp", gckv[:], gckv_d, writes=[r_k])
                fw.dma("sp", gmhA[:], gmh_d, writes=[r_k])
                fw.op("dve", lambda e: e.memset(pre[:, 0:3], 0.0), writes=[r_pre])
                r_HT = Res()
                for k in range(16):
                    fw.dma("sp", HT_d[k * 128:(k + 1) * 128, :], hT[:, k, :], reads=[R0], writes=[r_HT])
                cur = {}

                for h in range(4):
                    wv, wr = preA[h] if h < 3 else wp.load(wA_d[h], 16, 512)

                    def ev_qk(ci, tb, ps, r_ps, h=h):
                        fw.op("act", lambda e: e.copy(out=pre[:, 3 + tb * 512:3 + (tb + 1) * 512], in_=ps), reads=[r_ps], writes=[r_pre])
                        if tb == 3:
                            j = 4 * h + ci
                            fw.op("dve", lambda e: e.tensor_scalar(out=ybuf[:], in0=pre[:, 3:3 + S], scalar1=cw[:, j, 3:4], scalar2=cb[:, j:j + 1],
                                                                   op0=ALU.mult, op1=ALU.add), reads=[r_pre, r_k], writes=[r_y])
                            for tap in (2, 1, 0):
                                fw.op("dve", lambda e: e.scalar_tensor_tensor(out=ybuf[:], in0=pre[:, tap:tap + S], scalar=cw[:, j, tap:tap + 1], in1=ybuf[:],
                                                                              op0=ALU.mult, op1=ALU.add), reads=[r_pre, r_y, r_k], writes=[r_y])
                            ob, r_ob = rot(obs)
                            fw.op("act", lambda e: e.activation(out=ob[:], in_=ybuf[:], func=AF.Silu), reads=[r_y], writes=[r_ob])
                            dst = MQT_d if ci < 2 else MKT_d
                            r0 = h * 256 + (ci % 2) * 128
                            fw.dma("sp", dst[r0:r0 + 128, :], ob[:], reads=[r_ob])

                    proj_fm(wv, wr, [(0, 128), (128, 128), (256, 128), (384, 128)], hT, R0, 16, S, ev_qk)

                QW = 512
                cst = sbuf(s2, "cst", [64, 2], F32)
                fw.dma("sp", cst[:, 0:1], invf_d, writes=[r_k])
                fw.dma("sp", cst[:, 1:2], sgn_d, writes=[r_k])
                posi = pre[0:64, 0:QW].bitcast(I32)
                ang = pre[0:64, QW:2 * QW]
                tq = pre[0:64, 2 * QW:3 * QW]
                ki = pre[0:64, 3 * QW:4 * QW].bitcast(I32)
                rr = ybuf[0:64, 0:QW]
                mm = ybuf[0:64, QW:2 * QW]
                r_rope = Res()

                def rope_quarter(qq):
                    qc = slice(qq * QW, (qq + 1) * QW)
                    RW = [r_pre, r_y]
                    V = lambda f: fw.op("dve", f, reads=[r_k], writes=RW)

                    def wrap(buf):
                        V(lambda e: e.tensor_scalar(out=mm, in0=buf, scalar1=PI, scalar2=-2 * PI, op0=ALU.is_gt, op1=ALU.mult))
                        V(lambda e: e.tensor_tensor(out=buf, in0=mm, in1=buf, op=ALU.add))
                        V(lambda e: e.tensor_scalar(out=mm, in0=buf, scalar1=-PI, scalar2=2 * PI, op0=ALU.is_lt, op1=ALU.mult))
                        V(lambda e: e.tensor_tensor(out=buf, in0=mm, in1=buf, op=ALU.add))
                        V(lambda e: e.tensor_scalar(out=buf, in0=buf, scalar1=PI_SAFE, scalar2=-PI_SAFE, op0=ALU.min, op1=ALU.max))

                    fw.dma("sp", posi, pos_d[:, qc], writes=RW)
                    V(lambda e: e.tensor_copy(out=ang, in_=posi))
                    V(lambda e: e.tensor_scalar(out=ang, in0=ang, scalar1=cst[:, 0:1], scalar2=None, op0=ALU.mult))
                    V(lambda e: e.tensor_scalar(out=tq, in0=ang, scalar1=1.0 / (2 * PI), scalar2=None, op0=ALU.mult))
                    V(lambda e: e.tensor_copy(out=ki, in_=tq))
                    V(lambda e: e.tensor_copy(out=tq, in_=ki))
                    V(lambda e: e.scalar_tensor_tensor(out=rr, in0=tq, scalar=-C1, in1=ang, op0=ALU.mult, op1=ALU.add))
                    V(lambda e: e.scalar_tensor_tensor(out=rr, in0=tq, scalar=-C2, in1=rr, op0=ALU.mult, op1=ALU.add))
                    wrap(rr)
                    fw.op("act", lambda e: e.activation(out=sinS[:, qc], in_=rr, func=AF.Sin), reads=RW, writes=[r_rope])
                    fw.op("dve", lambda e: e.tensor_scalar(out=sinS[:, qc], in0=sinS[:, qc], scalar1=cst[:, 1:2], scalar2=None, op0=ALU.mult),
                          reads=[r_k], writes=[r_rope])
                    V(lambda e: e.tensor_scalar(out=ang, in0=rr, scalar1=PI / 2, scalar2=None, op0=ALU.add))
                    wrap(ang)
                    fw.op("act", lambda e: e.activation(out=cosT[:, qc], in_=ang, func=AF.Sin), reads=RW, writes=[r_rope])

                for blk in range(2):
                    wv, wr = wp.load(wA_d[4 + blk], 16, 512)

                    def ev_v(tt, ps, r_ps, blk=blk):
                        t16, r16 = rot(t16s)
                        fw.op("act", lambda e: e.copy(out=t16[:], in_=ps), reads=[r_ps], writes=[r16])
                        fw.dma("sp", MV_d[tt * 128:(tt + 1) * 128, blk * 512:(blk + 1) * 512], t16[:], reads=[r16])
                        if tt in (2, 9):
                            rope_quarter(2 * blk + (0 if tt == 2 else 1))

                    proj_tm(wv, wr, 512, hT, R0, 16, NT, ev_v)

                for h in range(4):
                    wv, wr = wp.load(wA_d[6 + h], 16, 512)

                    def ev_oz(tt, ps, r_ps, h=h):
                        t32, r32 = rot(t32s)
                        fw.op("act", lambda e: e.activation(out=t32[:], in_=ps, func=AF.Sigmoid), reads=[r_ps], writes=[r32])
                        t16, r16 = rot(t16s)
                        fw.op("dve", lambda e: e.tensor_tensor(out=t32[:, 0:256], in0=t32[:, 0:256], in1=t32[:, 256:512], op=ALU.mult),
                              reads=[r32], writes=[r32])
                        fw.op("dve", lambda e: e.tensor_tensor(out=t32[:, 0:256], in0=ps[:, 256:512], in1=t32[:, 0:256], op=ALU.mult),
                              reads=[r32, r_ps], writes=[r32])
                        fw.op("dve", lambda e: e.tensor_tensor(out=t16[:, 0:256], in0=t32[:, 0:256], in1=gmhA[:, h * 256:(h + 1) * 256], op=ALU.mult),
                              reads=[r32, r_k], writes=[r16])
                        fw.dma("sp", MG_d[tt * 128:(tt + 1) * 128, h * 256:(h + 1) * 256], t16[:, 0:256], reads=[r16])

                    proj_tm(wv, wr, 512, hT, R0, 16, NT, ev_oz)

                wv, wr = wp.load(wIF_d, 16, 8)

                def ev_if(tt, ps, r_ps):
                    fw.op("dve", lambda e: e.tensor_tensor(out=stA[:, 0:8], in0=ps, in1=bif[:], op=ALU.add), reads=[r_ps, r_k], writes=[r_st])
                    fw.op("dve", lambda e: e.tensor_copy(out=GI[:, :, tt], in_=stA[:, 0:4]), reads=[r_st], writes=[r_g])
                    fw.op("act", lambda e: e.activation(out=stA[:, 8:12], in_=stA[:, 4:8], func=AF.Exp, scale=-1.0), reads=[r_st], writes=[r_st])
                    fw.op("act", lambda e: e.activation(out=stA[:, 12:16], in_=stA[:, 8:12], func=AF.Ln, bias=1.0), reads=[r_st], writes=[r_st])
                    fw.op("dve", lambda e: e.tensor_scalar(out=GF[:, :, tt], in0=stA[:, 12:16], scalar1=-1.0, scalar2=None, op0=ALU.mult),
                          reads=[r_st], writes=[r_g])

                proj_tm(wv, wr, 8, hT, R0, 16, NT, ev_if)

                for which, (gb, dstT) in enumerate([(gcq, cqnT), (gckv, ckvnT)]):
                    wv, wr = wp.load(wA_d[10 + which], 16, 512)

                    def ev_c(tt, ps, r_ps, gb=gb, dstT=dstT):
                        tj, rj = rot(t16s)
                        fw.op("act", lambda e: e.activation(out=tj[:], in_=ps, func=AF.Square, accum_out=stA[:, 0:1]), reads=[r_ps], writes=[rj, r_st])
                        fw.op("act", lambda e: e.activation(out=stA[:, 1:2], in_=stA[:, 0:1], func=AF.Ln, scale=1.0 / 512, bias=EPS), reads=[r_st], writes=[r_st])
                        fw.op("act", lambda e: e.activation(out=stA[:, 2:3], in_=stA[:, 1:2], func=AF.Exp, scale=-0.5), reads=[r_st], writes=[r_st])
                        t16, r16 = rot(t16s)
                        fw.op("dve", lambda e: e.scalar_tensor_tensor(out=t16[:], in0=ps, scalar=stA[:, 2:3], in1=gb[:], op0=ALU.mult, op1=ALU.mult),
                              reads=[r_ps, r_st, r_k], writes=[r16])
                        if cur.get("ctr") is not None:
                            cur["ctr"]()

                        def tr(t16=t16, r16=r16, tt=tt, dstT=dstT):
                            bk, r_bk = nb()
                            bkb = b16(bk, 8)
                            for k in range(4):
                                fw.op("pe", lambda e: e.transpose(out=bkb[:, k, :], in_=t16[:, k * 128:(k + 1) * 128], identity=identb[:]),
                                      reads=[r16], writes=[r_bk])
                            fw.op("act", lambda e: e.copy(out=dstT[:, 0:4, tt * 128:(tt + 1) * 128], in_=bkb[:, 0:4, :]), reads=[r_bk], writes=[r_cn])

                        cur["ctr"] = tr

                    proj_tm(wv, wr, 512, hT, R0, 16, NT, ev_c)
                    cur["ctr"]()
                    cur["ctr"] = None

                def rope_ev(kind, tb, ps, r_ps, dst_rows):
                    cols = slice(tb * 512, (tb + 1) * 512)
                    if kind == 1:
                        fw.op("dve", lambda e: e.tensor_tensor(out=ybuf[0:64, cols], in0=ps, in1=cosT[:, cols], op=ALU.mult), reads=[r_ps, r_rope], writes=[r_y])
                    else:
                        if tb == 0:
                            cur["rope"] = rot(obs)
                        ob, r_ob = cur["rope"]
                        t32, r32 = rot(t32s)
                        fw.op("dve", lambda e: e.tensor_tensor(out=t32[0:64, :], in0=ps, in1=sinS[:, cols], op=ALU.mult), reads=[r_ps, r_rope], writes=[r32])
                        fw.op("dve", lambda e: e.tensor_tensor(out=ob[0:64, cols], in0=ybuf[0:64, cols], in1=t32[0:64, :], op=ALU.add),
                              reads=[r_y, r32], writes=[r_ob])
                        if tb == 3:
                            fw.dma("sp", dst_rows, ob[0:64, :], reads=[r_ob])

                def copy_ev(tb, ps, r_ps, dst_rows, alt):
                    cols = slice(tb * 512, (tb + 1) * 512)
                    if tb == 0:
                        cur["cp"] = rot(obs)
                    ob, r_ob = cur["cp"]
                    if alt % 2 == 0:
                        fw.op("act", lambda e: e.copy(out=ob[:, cols], in_=ps), reads=[r_ps], writes=[r_ob])
                    else:
                        fw.op("dve", lambda e: e.tensor_copy(out=ob[:, cols], in_=ps), reads=[r_ps], writes=[r_ob])
                    if tb == 3:
                        fw.dma("sp", dst_rows, ob[:], reads=[r_ob])

                wv, wr = wp.load(wKR_d, 16, 128)
                proj_fm(wv, wr, [(0, 64), (64, 64)], hT, R0, 16, S,
                        lambda ci, tb, ps, r_ps: rope_ev(ci + 1, tb, ps, r_ps, KRT_d[0:64, :]))

                def silu_ev(dst_d, blk):
                    def ev(tt, ps, r_ps):
                        t16, r16 = rot(t16s)
                        fw.op("act", lambda e: e.activation(out=t16[:], in_=ps, func=AF.Silu), reads=[r_ps], writes=[r16])
                        fw.dma("sp", dst_d[tt * 128:(tt + 1) * 128, blk * 512:(blk + 1) * 512], t16[:], reads=[r16])
                    return ev

                for blk in range(2):
                    wv, wr = wp.load(wA_d[12 + blk], 16, 512)
                    proj_tm(wv, wr, 512, hT, R0, 16, NT, silu_ev(AZ_d, blk))
                for blk in range(2):
                    wv, wr = wp.load(wA_d[14 + blk], 16, 512)
                    proj_fm(wv, wr, [(0, 128), (128, 128), (256, 128), (384, 128)], hT, R0, 16, S,
                            lambda ci, tb, ps, r_ps, blk=blk: copy_ev(tb, ps, r_ps, CQT_d[(blk * 4 + ci) * 128:(blk * 4 + ci + 1) * 128, :], tb))
                for blk in range(2):
                    wv, wr = wp.load(wA_d[16 + blk], 16, 512)
                    proj_tm(wv, wr, 512, hT, R0, 16, NT, silu_ev(CZ_d, blk))

                for pr in range(4):
                    wv, wr = wp.load(wUQ_d[pr], 4, 512)

                    def ev_q(ci, tb, ps, r_ps, pr=pr):
                        hh = 2 * pr + ci // 3
                        kind = ci % 3
                        if kind == 0:
                            copy_ev(tb, ps, r_ps, QNT_d[hh * 128:(hh + 1) * 128, :], tb)
                        else:
                            rope_ev(kind, tb, ps, r_ps, QRT_d[hh * 64:(hh + 1) * 64, :])

                    proj_fm(wv, wr, [(0, 128), (128, 64), (192, 64), (256, 128), (384, 64), (448, 64)], cqnT, r_cn, 4, S, ev_q)
                for blk in range(2):
                    wv, wr = wp.load(wUK_d[blk], 4, 512)
                    proj_fm(wv, wr, [(0, 128), (128, 128), (256, 128), (384, 128)], ckvnT, r_cn, 4, S,
                            lambda ci, tb, ps, r_ps, blk=blk: copy_ev(tb, ps, r_ps, KNT_d[(blk * 4 + ci) * 128:(blk * 4 + ci + 1) * 128, :], tb))
                for blk in range(2):
                    wv, wr = wp.load(wUV_d[blk], 4, 512)

                    def ev_va(tt, ps, r_ps, blk=blk):
                        t16, r16 = rot(t16s)
                        fw.op("act", lambda e: e.copy(out=t16[:], in_=ps), reads=[r_ps], writes=[r16])
                        fw.dma("sp", VA_d[tt * 128:(tt + 1) * 128, blk * 512:(blk + 1) * 512], t16[:], reads=[r16])

                    proj_tm(wv, wr, 512, ckvnT, r_cn, 4, NT, ev_va)
                if debug:
                    fw.dma("sp", GIF_d[:, 0:64], GI[:].rearrange("p h t -> p (h t)"), reads=[r_g])
                    fw.dma("sp", GIF_d[:, 64:128], GF[:].rearrange("p h t -> p (h t)"), reads=[r_g])
                fw.barrier()
            fw.barrier()
        rot_i = {}

        def rot(lst):
            i = rot_i.get(id(lst), 0)
            rot_i[id(lst)] = (i + 1) % len(lst)
            return lst[i]

        def mk(stack, name, n, shape, dt):
            return [(sbuf(stack, "%s%d" % (name, i), shape, dt), Res()) for i in range(n)]

        if upto >= 3:
          with contextlib.ExitStack() as sC:
            memnT = sbuf(sC, "memnT", [128, 16, 256], BF16)
            kmT = sbuf(sC, "kmT", [128, 8, 256], BF16)
            vmx = sbuf(sC, "vmx", [128, 2, 4, 260], BF16)
            if True:
                s5 = sC
                wp = WPool(sC, "wpM", 2, 16 * 512)
                cqs = mk(s5, "cq", 2, [128, 2, S], BF16)
                czs = mk(s5, "cz", 2, [128, 16, 256], BF16)
                hcTs = mk(s5, "hcT", 2, [128, 2, S], BF16)
                Ps = mk(s5, "Pc", 4, [128, 512], BF16)
                hct = mk(s5, "hct", 3, [128, 256], BF16)
                stats = mk(s5, "stC", 4, [128, 8], F32)
                r_km = Res()
                r_vm = Res()
                wm = [wp.load(wMEM_d[0], 16, 512), wp.load(wMEM_d[1], 16, 512)]

                def c_loads(h):
                    fw.dma("sp", cqs[h % 2][0][:], CQT_d[h * 256:(h + 1) * 256, :].rearrange("(k p) t -> p k t", p=128), writes=[cqs[h % 2][1]])
                    fw.dma("sp", czs[h % 2][0][:], CZ_d[:, h * 256:(h + 1) * 256].rearrange("(t p) c -> p t c", p=128), writes=[czs[h % 2][1]])

                def c_scores(h, qb):
                    cq, r_cq = cqs[h % 2]
                    Pm = []
                    for mt in range(2):
                        bk, r_bk = nbhi()
                        for kc in range(2):
                            fw.op("pe", lambda e: e.matmul(bk[:, 0:512], lhsT=kmT[:, 2 * h + kc, mt * 128:(mt + 1) * 128],
                                                           rhs=cq[:, kc, qb * 512:(qb + 1) * 512], start=(kc == 0), stop=(kc == 1)),
                                  reads=[r_cq, r_km], writes=[r_bk])
                        P, r_P = rot(Ps)
                        fw.op("act", lambda e: e.activation(out=P[:], in_=bk[:, 0:512], func=AF.Exp, scale=1.0 / 16), reads=[r_bk], writes=[r_P])
                        Pm.append((P, r_P))
                    return Pm

                def c_tr(pend):
                    hc_, r_hc, hcT, r_hcT, tt = pend
                    bt, r_bt = nbhi()
                    btb = b16(bt, 8)
                    for kc in range(2):
                        fw.op("pe", lambda e: e.transpose(out=btb[:, kc, :], in_=hc_[:, kc * 128:(kc + 1) * 128], identity=identb[:]),
                              reads=[r_hc], writes=[r_bt])
                    fw.op("act", lambda e: e.copy(out=hcT[:, 0:2, tt * 128:(tt + 1) * 128], in_=btb[:, 0:2, :]), reads=[r_bt], writes=[r_hcT])

                steps = [(h, qb) for h in range(4) for qb in range(4)]
                c_loads(0)
                c_loads(1)
                r_memn = rmsnorm_T(sC, mem_d, 2, gm_d, memnT)
                fw.op("dve", lambda e: e.memset(vmx[:, :, :, 256:257], 1.0), writes=[r_vm])
                for blk in range(2):
                    wv, wr = wm[blk]
                    proj_fm(wv, wr, [(0, 128), (128, 128), (256, 128), (384, 128)], memnT, r_memn, 16, 256,
                            lambda ci, tb, ps, r_ps, blk=blk: fw.op("act", lambda e: e.copy(out=kmT[:, blk * 4 + ci, :], in_=ps), reads=[r_ps], writes=[r_km]))
                    if blk == 0:
                        wm.append(wp.load(wMEM_d[2], 16, 512))
                wm.append(wp.load(wMEM_d[3], 16, 512))
                for blk in range(2):
                    wv, wr = wm[2 + blk]
                    proj_tm(wv, wr, 512, memnT, r_memn, 16, 2,
                            lambda tt, ps, r_ps, blk=blk: fw.op("dve", lambda e: e.tensor_copy(out=vmx[:, tt, 2 * blk:2 * blk + 2, 0:256],
                                                                                             in_=ps.rearrange("p (h c) -> p h c", h=2)),
                                                               reads=[r_ps], writes=[r_vm]))
                nxtP = c_scores(0, 0)
                pend = None
                for si, (h, qb) in enumerate(steps):
                    Pm = nxtP
                    cz, r_cz = czs[h % 2]
                    hcT, r_hcT = hcTs[h % 2]
                    if si + 1 < len(steps):
                        nxtP = c_scores(*steps[si + 1])
                    for q4 in range(4):
                        tt = qb * 4 + q4
                        bo, r_bo = banks[q4]
                        for mt in range(2):
                            fw.op("pe", lambda e: e.matmul(bo[:, 0:257], lhsT=Pm[mt][0][:, q4 * 128:(q4 + 1) * 128], rhs=vmx[:, mt, h, 0:257],
                                                           start=(mt == 0), stop=(mt == 1)),
                                  reads=[Pm[mt][1], r_vm], writes=[r_bo])
                        if pend is not None:
                            c_tr(pend)
                        sc, r_sc = rot(stats)
                        fw.op("dve", lambda e: e.reciprocal(out=sc[:, 0:1], in_=bo[:, 256:257]), reads=[r_bo], writes=[r_sc])
                        hc_, r_hc = rot(hct)
                        fw.op("dve", lambda e: e.scalar_tensor_tensor(out=hc_[:], in0=bo[:, 0:256], scalar=sc[:, 0:1], in1=cz[:, tt, :],
                                                                      op0=ALU.mult, op1=ALU.mult),
                              reads=[r_bo, r_sc, r_cz], writes=[r_hc])
                        pend = (hc_, r_hc, hcT, r_hcT, tt)
                    if qb == 3:
                        c_tr(pend)
                        pend = None
                        fw.dma("sp", HCT_d[h * 256:(h + 1) * 256, :].rearrange("(k p) t -> p k t", p=128), hcT[:], reads=[r_hcT])
                        if h + 2 < 4:
                            c_loads(h + 2)
                fw.barrier()

        if upto >= 4:
          with contextlib.ExitStack() as s6:
            kr = sbuf(s6, "kr", [128, S], BF16)
            r_kr = Res()
            qns = mk(s6, "qn", 2, [128, S], BF16)
            qrs = mk(s6, "qr", 2, [128, S], BF16)
            kns = mk(s6, "kn", 2, [128, S], BF16)
            vxs = mk(s6, "vx", 2, [128, 16, 132], BF16)
            azs = mk(s6, "az", 2, [128, 16, 128], BF16)
            haTs = mk(s6, "haT", 2, [128, S], BF16)
            Ps = mk(s6, "Pa", 3, [128, 512], BF16)
            hat = mk(s6, "hat", 8, [128, 128], BF16)
            stats = mk(s6, "stB", 4, [128, 8], F32)
            fw.op("dve", lambda e: e.memset(kr[64:128, :], 0.0), writes=[r_kr])
            fw.dma("sp", kr[0:64, :], KRT_d[0:64, :], writes=[r_kr])
            for vx, r_vx in vxs:
                fw.op("dve", lambda e: e.memset(vx[:, :, 128:129], 1.0), writes=[r_vx])
            for qr, r_qr in qrs:
                fw.op("dve", lambda e: e.memset(qr[64:128, :], 0.0), writes=[r_qr])
            SC = 1.0 / math.sqrt(192.0)

            def mla_loads(h):
                fw.dma("sp", qns[h % 2][0][:], QNT_d[h * 128:(h + 1) * 128, :], writes=[qns[h % 2][1]])
                fw.dma("sp", qrs[h % 2][0][0:64, :], QRT_d[h * 64:(h + 1) * 64, :], writes=[qrs[h % 2][1]])
                fw.dma("sp", kns[h % 2][0][:], KNT_d[h * 128:(h + 1) * 128, :], writes=[kns[h % 2][1]])
                fw.dma("sp", vxs[h % 2][0][:, :, 0:128], VA_d[:, h * 128:(h + 1) * 128].rearrange("(t p) c -> p t c", p=128), writes=[vxs[h % 2][1]])
                fw.dma("sp", azs[h % 2][0][:], AZ_d[:, h * 128:(h + 1) * 128].rearrange("(t p) c -> p t c", p=128), writes=[azs[h % 2][1]])

            mla_loads(0)
            pend_ev = [None]
            for h in range(8):
                qn, r_qn = qns[h % 2]
                qr, r_qr = qrs[h % 2]
                kn, r_kn = kns[h % 2]
                vx, r_vx = vxs[h % 2]
                az, r_az = azs[h % 2]
                haT, r_haT = haTs[h % 2]
                if h + 1 < 8:
                    mla_loads(h + 1)
                for qb in range(4):
                    bos = [banks[i] for i in range(4)]
                    nkt = 4 * qb + 4

                    def emit_S(kt, qb=qb):
                        q_lo = max(kt, 4 * qb)
                        ncols = (4 * qb + 4 - q_lo) * 128
                        q0 = q_lo * 128
                        kc_ = slice(kt * 128, (kt + 1) * 128)
                        bs, r_bs = nbhi()
                        fw.op("pe", lambda e: e.matmul(bs[:, 0:ncols], lhsT=kn[:, kc_], rhs=qn[:, q0:q0 + ncols], start=True, stop=False),
                              reads=[r_kn, r_qn], writes=[r_bs])
                        fw.op("pe", lambda e: e.matmul(bs[:, 0:ncols], lhsT=kr[:, kc_], rhs=qr[:, q0:q0 + ncols], start=False, stop=True),
                              reads=[r_kr, r_qr], writes=[r_bs])
                        P, r_P = rot(Ps)
                        fw.op("act", lambda e: e.activation(out=P[:, 0:ncols], in_=bs[:, 0:ncols], func=AF.Exp, scale=SC), reads=[r_bs], writes=[r_P])
                        if kt >= 4 * qb:
                            fw.op("dve", lambda e: e.tensor_tensor(out=P[:, 0:128], in0=P[:, 0:128], in1=maskb[:], op=ALU.mult), reads=[r_P], writes=[r_P])
                        return P, r_P, q_lo

                    cur_s = emit_S(0)
                    if pend_ev[0] is not None:
                        pend_ev[0]()
                        pend_ev[0] = None
                    for kt in range(nkt):
                        nxt_s = emit_S(kt + 1) if kt + 1 < nkt else None
                        P, r_P, q_lo = cur_s
                        for qt in range(q_lo, 4 * qb + 4):
                            j = qt - q_lo
                            bo, r_bo = bos[qt - 4 * qb]
                            fw.op("pe", lambda e: e.matmul(bo[:, 0:129], lhsT=P[:, j * 128:(j + 1) * 128], rhs=vx[:, kt, 0:129],
                                                           start=(kt == 0), stop=(kt == qt)),
                                  reads=[r_P, r_vx], writes=[r_bo])
                        cur_s = nxt_s
                    has = []
                    for q4 in range(4):
                        qt = 4 * qb + q4
                        bo, r_bo = bos[q4]
                        sc, r_sc = rot(stats)
                        fw.op("dve", lambda e: e.reciprocal(out=sc[:, 0:1], in_=bo[:, 128:129]), reads=[r_bo], writes=[r_sc])
                        ha_, r_ha = rot(hat)
                        fw.op("dve", lambda e: e.scalar_tensor_tensor(out=ha_[:], in0=bo[:, 0:128], scalar=sc[:, 0:1], in1=az[:, qt, :],
                                                                      op0=ALU.mult, op1=ALU.mult),
                              reads=[r_bo, r_sc, r_az], writes=[r_ha])
                        has.append((ha_, r_ha, qt))

                    def ev_tail(has=has, haT=haT, r_haT=r_haT, h=h, last=(qb == 3)):
                        bt, r_bt = nbhi()
                        btb = b16(bt, 8)
                        for q4, (ha_, r_ha, qt) in enumerate(has):
                            fw.op("pe", lambda e: e.transpose(out=btb[:, q4, :], in_=ha_[:], identity=identb[:]), reads=[r_ha], writes=[r_bt])
                        q0 = has[0][2] * 128
                        fw.op("act", lambda e: e.copy(out=haT[:, q0:q0 + 512].rearrange("p (a b) -> p a b", a=4), in_=btb[:, 0:4, :]), reads=[r_bt], writes=[r_haT])
                        if last:
                            fw.dma("sp", HAT_d[h * 128:(h + 1) * 128, :], haT[:], reads=[r_haT])

                    pend_ev[0] = ev_tail
            pend_ev[0]()
            fw.barrier()

        hTh = sbuf(es, "hTh", [128, 16, 1024], BF16)
        r_in = Res()
        if upto >= 5:
          with contextlib.ExitStack() as s7:
            Bt = sbuf(s7, "Bt", [128, 64], F32)
            Gt = sbuf(s7, "Gt", [128, 64], F32)
            At = sbuf(s7, "At", [128, 64], F32)
            WS = sbuf(s7, "WS", [128, 64], F32)
            tmpg = sbuf(s7, "tmpg", [128, 64], F32)
            btT = sbuf(s7, "btT", [64, 128], F32)
            Mneg = sbuf(s7, "Mneg", [128, S], F32)
            r_gp = Res()
            r_bh = Res()
            for cc in range(16):
                fw.op("pool", lambda e: e.tensor_scalar(out=Mneg[:, cc * 128:(cc + 1) * 128], in0=maskf[:], scalar1=-1.0, scalar2=30000.0, op0=ALU.add, op1=ALU.mult),
                      writes=[r_gp])
            GFv = GF[:].rearrange("p h t -> p (h t)")
            GIv = GI[:].rearrange("p h t -> p (h t)")
            bk, r_bk = nb()
            fw.op("pe", lambda e: e.matmul(bk[:, 0:64], lhsT=maskf[:], rhs=GFv, start=True, stop=True), reads=[R0], writes=[r_bk])
            fw.op("act", lambda e: e.copy(out=Bt[:], in_=bk[:, 0:64]), reads=[r_bk], writes=[r_gp])
            bk2, r_bk2 = nb()
            fw.op("pe", lambda e: e.matmul(bk2[:, 0:64], lhsT=onesf[:], rhs=GFv, start=True, stop=True), reads=[R0], writes=[r_bk2])
            fw.op("dve", lambda e: e.tensor_copy(out=Gt[:], in_=bk2[:, 0:64]), reads=[r_bk2], writes=[r_gp])
            fw.op("dve", lambda e: e.scalar_tensor_tensor(out=At[:], in0=GIv, scalar=-LN16, in1=Bt[:], op0=ALU.add, op1=ALU.subtract),
                  reads=[r_gp], writes=[r_gp])
            fw.op("dve", lambda e: e.tensor_tensor(out=tmpg[:], in0=Gt[:], in1=At[:], op=ALU.add), reads=[r_gp], writes=[r_gp])
            fw.op("act", lambda e: e.activation(out=WS[:], in_=tmpg[:], func=AF.Exp), reads=[r_gp], writes=[r_gp])
            bk3, r_bk3 = nb()
            fw.op("pe", lambda e: e.transpose(out=bk3[0:64, 0:128], in_=Bt[:], identity=identf[:]), reads=[r_gp], writes=[r_bk3])
            fw.op("act", lambda e: e.copy(out=btT[:], in_=bk3[0:64, 0:128]), reads=[r_bk3], writes=[r_gp])
            fw.dma("sp", BH_d, btT[:], reads=[r_gp], writes=[r_bh])
            BHv = BH_d.rearrange("(h t) c -> h (t c)", h=4)

            NH = 2
            qTs = mk(s7, "mqT", NH, [128, 2, S], BF16)
            kTs = mk(s7, "mkT", NH, [128, 2, S], BF16)
            vxs = mk(s7, "mvx", NH, [128, 16, 260], BF16)
            mgs = mk(s7, "mmg", NH, [128, 16, 256], BF16)
            Brs = mk(s7, "Brow", NH, [128, S], F32)
            EBs = mk(s7, "EBrow", NH, [128, S], F32)
            BMs = mk(s7, "BrM", NH, [128, S], F32)
            hmTs = mk(s7, "hmT", NH, [128, 2, S], BF16)
            Cfs = mk(s7, "Cf", NH, [128, 2, 260], F32)
            Cb3 = [mk(s7, "Cb%d_" % i, 3, [128, 2, 260], BF16) for i in range(NH)]
            qeb2 = [mk(s7, "qeb%d_" % i, 3, [128, 2, 128], BF16) for i in range(NH)]
            DTs = mk(s7, "DT", NH, [128, 128], F32)
            PT2 = [mk(s7, "PT%d_" % i, 3, [128, 128], BF16) for i in range(NH)]
            kws = mk(s7, "kw", NH, [128, 2, 128], BF16)
            hmt2 = [mk(s7, "hmt%d_" % i, 2, [128, 256], BF16) for i in range(NH)]
            jks = mk(s7, "jk", NH, [128, 256], BF16)
            stats = mk(s7, "stM", 2, [128, 2, 8], F32)
            for vx, r_vx in vxs:
                fw.op("dve", lambda e: e.memset(vx[:, :, 256:257], 1.0), writes=[r_vx])
            for hg in ((0, 1), (2, 3)):
                if hg == (2, 3):
                    fw.dma("sp", hTh[:], HT_d[:, 0:1024].rearrange("(k p) t -> p k t", p=128), writes=[r_in])
                for i, h in enumerate(hg):
                    fw.dma("sp", qTs[i][0][:], MQT_d[h * 256:(h + 1) * 256, :].rearrange("(k p) t -> p k t", p=128), writes=[qTs[i][1]])
                    fw.dma("sp", kTs[i][0][:], MKT_d[h * 256:(h + 1) * 256, :].rearrange("(k p) t -> p k t", p=128), writes=[kTs[i][1]])
                    fw.dma("sp", vxs[i][0][:, :, 0:256], MV_d[:, h * 256:(h + 1) * 256].rearrange("(t p) c -> p t c", p=128), writes=[vxs[i][1]])
                    fw.dma("sp", mgs[i][0][:], MG_d[:, h * 256:(h + 1) * 256].rearrange("(t p) c -> p t c", p=128), writes=[mgs[i][1]])
                    fw.dma("sp", Brs[i][0][:].rearrange("p (o t) -> p o t", o=1), BHv[h:h + 1, :].partition_broadcast(128),
                           reads=[r_bh], writes=[Brs[i][1]])
                    fw.op("act", lambda e: e.activation(out=EBs[i][0][:], in_=Brs[i][0][:], func=AF.Exp), reads=[Brs[i][1]], writes=[EBs[i][1]])
                    fw.op("dve", lambda e: e.tensor_tensor(out=BMs[i][0][:], in0=Brs[i][0][:], in1=Mneg[:], op=ALU.add),
                          reads=[Brs[i][1], r_gp], writes=[BMs[i][1]])
                def P1a(c, hg=hg):
                    cols = slice(c * 128, (c + 1) * 128)
                    for i, h in enumerate(hg):
                        qT, r_qT = qTs[i]
                        kT, r_kT = kTs[i]
                        Br, r_Br = BMs[i]
                        EB, r_EB = EBs[i]
                        qeb, r_qeb = qeb2[i][c % 3]
                        DT, r_DT = DTs[i]
                        PT, r_PT = PT2[i][c % 3]
                        kw, r_kw = kws[i]
                        col = h * 16 + c
                        if c > 0:
                            for k in range(2):
                                fw.op("pool", lambda e: e.tensor_tensor(out=qeb[:, k, :], in0=qT[:, k, cols], in1=EB[:, cols], op=ALU.mult),
                                      reads=[r_qT, r_EB], writes=[r_qeb])
                        bs, r_bs = nb()
                        for k in range(2):
                            fw.op("pe", lambda e: e.matmul(bs[:, 0:128], lhsT=kT[:, k, cols], rhs=qT[:, k, cols], start=(k == 0), stop=(k == 1)),
                                  reads=[r_kT, r_qT], writes=[r_bs])
                        fw.op("act", lambda e: e.activation(out=DT[:], in_=Br[:, cols], func=AF.Exp, bias=At[:, col:col + 1], scale=1.0),
                              reads=[r_Br, r_gp], writes=[r_DT])
                        fw.op("dve", lambda e: e.tensor_tensor(out=PT[:], in0=bs[:, 0:128], in1=DT[:], op=ALU.mult), reads=[r_bs, r_DT], writes=[r_PT])
                        if c < 15:
                            bkt, r_bkt = nb()
                            bktb = b16(bkt, 8)
                            for k in range(2):
                                fw.op("pe", lambda e: e.transpose(out=bktb[:, k, :], in_=kT[:, k, cols], identity=identb[:]), reads=[r_kT], writes=[r_bkt])
                            fw.op("dve", lambda e: e.tensor_scalar(out=kw[:], in0=bktb[:, 0:2, :], scalar1=WS[:, col:col + 1], scalar2=None, op0=ALU.mult),
                                  reads=[r_bkt, r_gp], writes=[r_kw])

                def P1b(c, hg=hg):
                    if c >= 15:
                        return
                    for i, h in enumerate(hg):
                        vx, r_vx = vxs[i]
                        EB, r_EB = EBs[i]
                        Cf, r_Cf = Cfs[i]
                        Cb, r_Cb = Cb3[i][c % 3]
                        kw, r_kw = kws[i]
                        for kc in range(2):
                            bu, r_bu = nb()
                            fw.op("pe", lambda e: e.matmul(bu[:, 0:257], lhsT=kw[:, kc, :], rhs=vx[:, c, 0:257], start=True, stop=True),
                                  reads=[r_kw, r_vx], writes=[r_bu])
                            if c == 0:
                                fw.op("dve", lambda e: e.tensor_copy(out=Cf[:, kc, 0:257], in_=bu[:, 0:257]), reads=[r_bu], writes=[r_Cf])
                            else:
                                fw.op("dve", lambda e: e.scalar_tensor_tensor(out=Cf[:, kc, 0:257], in0=Cf[:, kc, 0:257],
                                                                              scalar=EB[:, c * 128 + 127:c * 128 + 128], in1=bu[:, 0:257],
                                                                              op0=ALU.mult, op1=ALU.add),
                                      reads=[r_bu, r_Cf, r_EB], writes=[r_Cf])
                        fw.op("act", lambda e: e.copy(out=Cb[:, :, 0:257], in_=Cf[:, :, 0:257]), reads=[r_Cf], writes=[r_Cb])

                def P2a(c, hg=hg):
                    sc, r_sc = stats[c % 2]
                    bos = []
                    for i, h in enumerate(hg):
                        vx, r_vx = vxs[i]
                        qeb, r_qeb = qeb2[i][c % 3]
                        PT, r_PT = PT2[i][c % 3]
                        jk, r_jk = jks[i]
                        bo, r_bo = nb()
                        bos.append((bo, r_bo))
                        if c > 0:
                            Cb, r_Cb = Cb3[i][(c - 1) % 3]
                            for k in range(2):
                                fw.op("pe", lambda e: e.matmul(bo[:, 0:257], lhsT=qeb[:, k, :], rhs=Cb[:, k, 0:257], start=(k == 0), stop=False),
                                      reads=[r_qeb, r_Cb], writes=[r_bo])
                        fw.op("pe", lambda e: e.matmul(bo[:, 0:257], lhsT=PT[:], rhs=vx[:, c, 0:257], start=(c == 0), stop=True),
                              reads=[r_PT, r_vx], writes=[r_bo])
                        fw.op("act", lambda e: e.activation(out=sc[:, i, 0:1], in_=bo[:, 256:257], func=AF.Square), reads=[r_bo], writes=[r_sc])
                        fw.op("act", lambda e: e.activation(out=jk[:], in_=bo[:, 0:256], func=AF.Square, accum_out=sc[:, i, 1:2]), reads=[r_bo], writes=[r_jk, r_sc])
                    return bos

                def P2b(c, bos, hg=hg):
                    sc, r_sc = stats[c % 2]
                    fw.op("dve", lambda e: e.tensor_scalar(out=sc[:, :, 2], in0=sc[:, :, 0], scalar1=1.0, scalar2=EPS, op0=ALU.max, op1=ALU.mult),
                          reads=[r_sc], writes=[r_sc])
                    fw.op("dve", lambda e: e.scalar_tensor_tensor(out=sc[:, :, 3], in0=sc[:, :, 1], scalar=1.0 / 256, in1=sc[:, :, 2], op0=ALU.mult, op1=ALU.add),
                          reads=[r_sc], writes=[r_sc])
                    fw.op("act", lambda e: e.activation(out=sc[:, :, 4], in_=sc[:, :, 3], func=AF.Ln), reads=[r_sc], writes=[r_sc])
                    fw.op("act", lambda e: e.activation(out=sc[:, :, 5], in_=sc[:, :, 4], func=AF.Exp, scale=-0.5), reads=[r_sc], writes=[r_sc])
                    for i, h in enumerate(hg):
                        mg, r_mg = mgs[i]
                        hmt, r_hmt = hmt2[i][c % 2]
                        bo, r_bo = bos[i]
                        fw.op("dve", lambda e: e.scalar_tensor_tensor(out=hmt[:], in0=bo[:, 0:256], scalar=sc[:, i, 5:6], in1=mg[:, c, :],
                                                                      op0=ALU.mult, op1=ALU.mult),
                              reads=[r_bo, r_sc, r_mg], writes=[r_hmt])

                def P2tail(c, hg=hg):
                    cols = slice(c * 128, (c + 1) * 128)
                    for i, h in enumerate(hg):
                        hmT, r_hmT = hmTs[i]
                        hmt, r_hmt = hmt2[i][c % 2]
                        bt, r_bt = nb()
                        btb = b16(bt, 8)
                        for kc in range(2):
                            fw.op("pe", lambda e: e.transpose(out=btb[:, kc, :], in_=hmt[:, kc * 128:(kc + 1) * 128], identity=identb[:]),
                                  reads=[r_hmt], writes=[r_bt])
                        fw.op("act", lambda e: e.copy(out=hmT[:, 0:2, cols], in_=btb[:, 0:2, :]), reads=[r_bt], writes=[r_hmT])

                P1a(0)
                P1b(0)
                P1a(1)
                P1b(1)
                for c in range(16):
                    if c + 2 < 16:
                        P1a(c + 2)
                    bos_c = P2a(c)
                    if c > 0:
                        P2tail(c - 1)
                    P2b(c, bos_c)
                    if c + 2 < 16:
                        P1b(c + 2)
                P2tail(15)
                for i, h in enumerate(hg):
                    fw.dma("sp", HMT_d[h * 256:(h + 1) * 256, :].rearrange("(k p) t -> p k t", p=128), hmTs[i][0][:], reads=[hmTs[i][1]])
            fw.barrier()

        if upto >= 6:
          with contextlib.ExitStack() as sD:
            gfin = sbuf(sD, "gfin", [128, D], F32)
            mergedT = sbuf(sD, "mergedT", [128, 16, 1024], BF16)
            wpg = WPool(sD, "wpg", 2, 16 * 384)
            wpb = WPool(sD, "wpb", 2, 8 * 384)
            wpoa = WPool(sD, "wpoa", 2, 16 * 512)
            r_gfin = Res()

            def c1_pre(th):
                if th > 0:
                    fw.dma("sp", hTh[:], HT_d[:, th * 1024:(th + 1) * 1024].rearrange("(k p) t -> p k t", p=128), writes=[r_in])
                return {0: (wpg.load(wG_d[0], 16, 384), wpb.load(wBR_d[0], 8, 384))}

            wl_next = c1_pre(0)
            fw.dma("sp", gfin[:], gf_d, writes=[r_gfin])
            for th in range(2):
                t0 = th * 1024
                with contextlib.ExitStack() as c1:
                    hbs = [sbuf(c1, "hb%d" % b, [128, 8, 1024], BF16) for b in range(3)]
                    r_hb = [Res() for _ in range(3)]
                    sgs = mk(c1, "sg", 2, [128, 512], F32)
                    tmps = mk(c1, "tmpc", 2, [128, 512], F32)
                    acc = sbuf(c1, "acc", [128, 1024], F32)
                    r_acc = Res()
                    r_mT = Res()
                    wl = wl_next
                    for b, src in enumerate((HMT_d, HAT_d, HCT_d)):
                        fw.dma("sp", hbs[b][:], src[:, t0:t0 + 1024].rearrange("(k p) t -> p k t", p=128), writes=[r_hb[b]])
                    for c in range(16):
                        (wgv, wgr), (wbv, wbr) = wl.pop(c)
                        for b in range(3):
                            if b == 1 and c + 1 < 16:
                                wl[c + 1] = (wpg.load(wG_d[c + 1], 16, 384), wpb.load(wBR_d[c + 1], 8, 384))
                            if b == 2 and c == 13:
                                wos_pre = [wpoa.load(wOUT_d[n4], 16, 512) for n4 in range(2)]
                            gb = [nb(), nb()]
                            rb = [nb(), nb()]
                            for kc in range(16):
                                for tb in range(2):
                                    fw.op("pe", lambda e: e.matmul(gb[tb][0][:, 0:512], lhsT=wgv[:, kc, b * 128:(b + 1) * 128],
                                                                   rhs=hTh[:, kc, tb * 512:(tb + 1) * 512], start=(kc == 0), stop=(kc == 15)),
                                          reads=[wgr, r_in], writes=[gb[tb][1]])
                            for kc in range(8):
                                for tb in range(2):
                                    fw.op("pe", lambda e: e.matmul(rb[tb][0][:, 0:512], lhsT=wbv[:, kc, b * 128:(b + 1) * 128],
                                                                   rhs=hbs[b][:, kc, tb * 512:(tb + 1) * 512], start=(kc == 0), stop=(kc == 7)),
                                          reads=[wbr, r_hb[b]], writes=[rb[tb][1]])
                            for tb in range(2):
                                tc_ = slice(tb * 512, (tb + 1) * 512)
                                sg, r_sg = rot(sgs)
                                fw.op("act", lambda e: e.activation(out=sg[:], in_=gb[tb][0][:, 0:512], func=AF.Sigmoid), reads=[gb[tb][1]], writes=[r_sg])
                                if b == 0:
                                    fw.op("dve", lambda e: e.tensor_tensor(out=acc[:, tc_], in0=sg[:], in1=rb[tb][0][:, 0:512], op=ALU.mult),
                                          reads=[r_sg, rb[tb][1]], writes=[r_acc])
                                else:
                                    tmp, r_tmp = rot(tmps)
                                    fw.op("dve", lambda e: e.tensor_tensor(out=tmp[:], in0=sg[:], in1=rb[tb][0][:, 0:512], op=ALU.mult),
                                          reads=[r_sg, rb[tb][1]], writes=[r_tmp])
                                    if b == 1:
                                        fw.op("pool", lambda e: e.tensor_tensor(out=acc[:, tc_], in0=acc[:, tc_], in1=tmp[:], op=ALU.add),
                                              reads=[r_tmp, r_acc], writes=[r_acc])
                                    else:
                                        fw.op("pool", lambda e: e.tensor_tensor(out=mergedT[:, c, tc_], in0=acc[:, tc_], in1=tmp[:], op=ALU.add),
                                              reads=[r_tmp, r_acc], writes=[r_mT])
                    fw.barrier()
                with contextlib.ExitStack() as c2:
                    wpo = WPool(c2, "wpo", 2, 16 * 512)
                    wos = wos_pre + [wpo.load(wOUT_d[n4], 16, 512) for n4 in range(2, 4)]
                    xts = mk(c2, "xo", 3, [128, D], F32)
                    fw.dma("sp", xts[0][0][:], x_d[t0:t0 + 128, :], writes=[xts[0][1]])
                    jk = sbuf(c2, "jko", [128, D], BF16)
                    r_jk = Res()
                    stats = mk(c2, "stO", 2, [128, 4], F32)
                    for t8 in range(8):
                        xt, r_xt = xts[t8 % 3]
                        sc, r_sc = rot(stats)
                        rows = slice(t0 + t8 * 128, t0 + (t8 + 1) * 128)
                        if t8 + 1 < 8:
                            fw.dma("sp", xts[(t8 + 1) % 3][0][:], x_d[t0 + (t8 + 1) * 128:t0 + (t8 + 2) * 128, :], writes=[xts[(t8 + 1) % 3][1]])
                        for n4 in range(4):
                            wov, wor = wos[n4]
                            bk, r_bk = nb()
                            for kc in range(16):
                                fw.op("pe", lambda e: e.matmul(bk[:, 0:512], lhsT=mergedT[:, kc, t8 * 128:(t8 + 1) * 128], rhs=wov[:, kc, 0:512],
                                                               start=(kc == 0), stop=(kc == 15)),
                                      reads=[wor], writes=[r_bk])
                            fw.op("dve", lambda e: e.tensor_tensor(out=xt[:, n4 * 512:(n4 + 1) * 512], in0=bk[:, 0:512], in1=xt[:, n4 * 512:(n4 + 1) * 512], op=ALU.add),
                                  reads=[r_bk, r_xt], writes=[r_xt])
                        fw.op("act", lambda e: e.activation(out=jk[:], in_=xt[:], func=AF.Square, accum_out=sc[:, 0:1]), reads=[r_xt], writes=[r_jk, r_sc])
                        fw.op("act", lambda e: e.activation(out=sc[:, 1:2], in_=sc[:, 0:1], func=AF.Ln, scale=1.0 / D, bias=EPS), reads=[r_sc], writes=[r_sc])
                        fw.op("act", lambda e: e.activation(out=sc[:, 2:3], in_=sc[:, 1:2], func=AF.Exp, scale=-0.5), reads=[r_sc], writes=[r_sc])
                        fw.op("dve", lambda e: e.scalar_tensor_tensor(out=xt[:], in0=xt[:], scalar=sc[:, 2:3], in1=gfin[:], op0=ALU.mult, op1=ALU.mult),
                              reads=[r_xt, r_sc, r_gfin], writes=[r_xt])
                        fw.dma("sp", out_d[rows, :], xt[:], reads=[r_xt])
                        if th == 0 and t8 == 3:
                            wl_next = c1_pre(1)
                    fw.barrier()
        fw.final_wait("sp")
    return nc


def _tile_w(W):
    K, n = W.shape
    return np.ascontiguousarray(W.reshape(K // 128, 128, n).transpose(1, 0, 2))


def prep_shared(inp):
    f = np.float32
    w_in = np.asarray(inp["w_in"], f)[0]
    ar = np.arange
    blocks = []
    for h in range(4):
        blocks.append(np.concatenate([O_QK + h * 256 + ar(256), O_QK + 1024 + h * 256 + ar(256)]))
    for b in range(2):
        blocks.append(O_V + b * 512 + ar(512))
    for h in range(4):
        blocks.append(np.concatenate([O_O + h * 256 + ar(256), O_Z + h * 256 + ar(256)]))
    blocks.append(O_CQ + ar(512))
    blocks.append(O_CKV + ar(512))
    for b in range(2):
        blocks.append(O_AZ + b * 512 + ar(512))
    for b in range(2):
        blocks.append(O_CQC + b * 512 + ar(512))
    for b in range(2):
        blocks.append(O_CZ + b * 512 + ar(512))
    sh = {}
    sh["wA"] = np.stack([_tile_w(w_in[:, c]) for c in blocks])
    swap = np.concatenate([32 + ar(32), ar(32)])
    sh["wKR"] = _tile_w(w_in[:, np.concatenate([O_KR + ar(64), O_KR + swap])])
    sh["wIF"] = _tile_w(w_in[:, np.concatenate([O_I + ar(4), O_F + ar(4)])])
    w_uq = np.asarray(inp["w_uq"], f)[0]
    uq = []
    for pr in range(4):
        cols = []
        for hh in (2 * pr, 2 * pr + 1):
            cols += [hh * 192 + ar(128), hh * 192 + 128 + ar(64), hh * 192 + 128 + swap]
        uq.append(_tile_w(w_uq[:, np.concatenate(cols)]))
    sh["wUQ"] = np.stack(uq)
    w_ukv = np.asarray(inp["w_ukv"], f)[0]
    sh["wUK"] = np.stack([_tile_w(w_ukv[:, np.concatenate([hh * 256 + ar(128) for hh in range(4 * b, 4 * b + 4)])]) for b in range(2)])
    sh["wUV"] = np.stack([_tile_w(w_ukv[:, np.concatenate([hh * 256 + 128 + ar(128) for hh in range(4 * b, 4 * b + 4)])]) for b in range(2)])
    w_mem = np.asarray(inp["w_mem_kv"], f)[0]
    sh["wMEM"] = np.stack([_tile_w(w_mem[:, b * 512:(b + 1) * 512]) for b in range(4)])
    wbm = np.asarray(inp["w_br_m"], f)[0]
    wba = np.asarray(inp["w_br_a"], f)[0]
    wbc = np.asarray(inp["w_br_c"], f)[0]
    sh["wG"] = np.stack([_tile_w(w_in[:, np.concatenate([O_G + b * 2048 + c * 128 + ar(128) for b in range(3)])]) for c in range(16)])
    sh["wBR"] = np.stack([_tile_w(np.concatenate([w[:, c * 128:(c + 1) * 128] for w in (wbm, wba, wbc)], axis=1)) for c in range(16)])
    w_out = np.asarray(inp["w_out"], f)[0]
    sh["wOUT"] = np.stack([_tile_w(w_out[:, b * 512:(b + 1) * 512]) for b in range(4)])
    bc = lambda v, n=128: np.ascontiguousarray(np.broadcast_to(np.asarray(v, f).reshape(1, -1), (n, np.asarray(v).size)))
    sh["g_norm"] = bc(inp["norm"][0])
    sh["g_final"] = bc(inp["final_norm"])
    sh["g_mem"] = bc(inp["mem_norm"][0])
    sh["g_cq"] = bc(inp["cq_norm"][0])
    sh["g_ckv"] = bc(inp["ckv_norm"][0])
    sh["g_mh"] = bc(inp["mh_norm"][0])
    conv_w = np.asarray(inp["conv_w"], f)[0]
    conv_b = np.asarray(inp["conv_b"], f)[0]
    cw = np.zeros((128, 16, 4), f)
    cb = np.zeros((128, 16), f)
    for h in range(4):
        for ci in range(4):
            base = (0 if ci < 2 else 1024) + h * 256 + (ci % 2) * 128
            cw[:, 4 * h + ci, :] = conv_w[:, base:base + 128].T
            cb[:, 4 * h + ci] = conv_b[base:base + 128]
    sh["convw"] = cw
    sh["convb"] = cb
    sh["bif"] = bc(np.concatenate([np.asarray(inp["b_igate"], f)[0], np.asarray(inp["b_fgate"], f)[0]]))
    inv_freq = (np.float32(10000.0) ** (-(np.arange(0, 64, 2, dtype=np.float32)) / np.float32(64))).astype(f)
    sh["invf"] = np.concatenate([inv_freq, inv_freq]).reshape(64, 1).astype(f)
    sh["sgn"] = np.concatenate([-np.ones(32, f), np.ones(32, f)]).reshape(64, 1)
    sh["ident"] = np.eye(128, dtype=f)
    sh["mask"] = np.triu(np.ones((128, 128), f))
    return sh


def prep_core(inp, b):
    return {
        "x": np.ascontiguousarray(np.asarray(inp["x"], np.float32)[b]),
        "mem": np.ascontiguousarray(np.asarray(inp["mem"], np.float32)[b]),
        "pos64": np.ascontiguousarray(np.broadcast_to(np.asarray(inp["positions"], np.int32)[b].reshape(1, S), (64, S))),
    }


def kernel(**inputs):
    sh = prep_shared(inputs)
    nc = build()
    in_maps = []
    for b in range(8):
        m = dict(sh)
        m.update(prep_core(inputs, b))
        in_maps.append(m)
    res = run_bass_kernel_spmd(nc, in_maps, core_ids=list(range(8)))
    return np.stack([np.asarray(r["out"], np.float32) for r in res.results], axis=0)
```

```python
import math
import contextlib
import numpy as np
import concourse.bass as bass
import concourse.mybir as mybir
from concourse.bass_utils import run_bass_kernel_spmd

F32 = mybir.dt.float32
BF16 = mybir.dt.bfloat16
I32 = mybir.dt.int32
AF = mybir.ActivationFunctionType
ALU = mybir.AluOpType

S = 2048
D = 2048
NT = 16
EPS = 1e-6
O_QK, O_V, O_O, O_Z, O_I, O_F, O_CQ, O_CKV, O_KR, O_AZ, O_CQC, O_CZ, O_G = (
    0, 2048, 3072, 4096, 5120, 5124, 5128, 5640, 6152, 6216, 7240, 8264, 9288)
LN16 = math.log(16.0)
PI = math.pi
PI_SAFE = 3.1415925
C1 = 6.28125
C2 = 2 * math.pi - 6.28125


class Res:
    __slots__ = ("w", "r")

    def __init__(self):
        self.w = None
        self.r = {}


class FW:
    NS = 6

    def __init__(self, nc, es):
        self.nc = nc
        self.eng = {"pe": nc.tensor, "act": nc.scalar, "dve": nc.vector, "pool": nc.gpsimd, "sp": nc.sync}
        self.sem = {k: es.enter_context(nc.semaphore("s_" + k)) for k in self.eng}
        self.cnt = {k: 0 for k in self.eng}
        self.seen = {k: {} for k in self.eng}
        self.dq = {}
        for q in ("sp", "pool", "act"):
            sems = [es.enter_context(nc.semaphore("d_%s%d" % (q, i))) for i in range(self.NS)]
            self.dq[q] = {"sems": sems, "vals": [0] * self.NS, "i": 0}

    def _semh(self, key):
        if isinstance(key, str):
            return self.sem[key]
        return self.dq[key[0]]["sems"][key[1]]

    def _wait(self, e, ev):
        if ev is None:
            return
        key, val = ev
        if e == "pe" and key == "pe":
            return
        if self.seen[e].get(key, 0) >= val:
            return
        self.seen[e][key] = val
        self.eng[e].wait_ge(self._semh(key), val)

    def _deps(self, e, reads, writes):
        for r in reads:
            self._wait(e, r.w)
        for w in writes:
            self._wait(e, w.w)
            for ev in list(w.r.values()):
                self._wait(e, ev)

    def _mark(self, me, reads, writes):
        for r in reads:
            r.r[me[0]] = me
        for w in writes:
            w.w = me
            w.r = {}

    def op(self, e, fn, reads=(), writes=()):
        self._deps(e, reads, writes)
        ins = fn(self.eng[e])
        self.cnt[e] += 1
        ins.then_inc(self.sem[e], 1)
        self._mark((e, self.cnt[e]), reads, writes)

    def dma(self, q, out, in_, reads=(), writes=(), **kw):
        self._deps(q, reads, writes)
        d = self.dq[q]
        i = d["i"]
        d["i"] = (i + 1) % self.NS
        key = (q, i)
        if d["vals"][i]:
            self._wait(q, (key, d["vals"][i]))
        ins = self.eng[q].dma_start(out=out, in_=in_, **kw)
        d["vals"][i] += 16
        ins.then_inc(d["sems"][i], 16)
        self._mark((key, d["vals"][i]), reads, writes)

    def all_events(self):
        evs = [(k, self.cnt[k]) for k in self.eng if self.cnt[k]]
        for q, d in self.dq.items():
            for i in range(self.NS):
                if d["vals"][i]:
                    evs.append(((q, i), d["vals"][i]))
        return evs

    def barrier(self):
        evs = self.all_events()
        for e in self.eng:
            for ev in evs:
                if ev[0] != e:
                    self._wait(e, ev)

    def final_wait(self, e="sp"):
        for ev in self.all_events():
            if ev[0] != e:
                self._wait(e, ev)


def build(debug=False, upto=99):
    nc = bass.Bass("TRN2", target_bir_lowering=False)

    def din(name, shape, dt=F32):
        return nc.dram_tensor(name, list(shape), dt, kind="ExternalInput").ap()

    skind = "ExternalOutput" if debug else "Internal"

    def dscr(name, shape, dt=BF16):
        return nc.dram_tensor(name, list(shape), dt, kind=skind).ap()

    x_d = din("x", [S, D])
    mem_d = din("mem", [256, D])
    pos_d = din("pos64", [64, S], I32)
    invf_d = din("invf", [64, 1])
    sgn_d = din("sgn", [64, 1])
    ident_d = din("ident", [128, 128])
    mask_d = din("mask", [128, 128])
    gn_d = din("g_norm", [128, D])
    gf_d = din("g_final", [128, D])
    gm_d = din("g_mem", [128, D])
    gcq_d = din("g_cq", [128, 512])
    gckv_d = din("g_ckv", [128, 512])
    gmh_d = din("g_mh", [128, 1024])
    convw_d = din("convw", [128, 16, 4])
    convb_d = din("convb", [128, 16])
    bif_d = din("bif", [128, 8])
    wA_d = din("wA", [18, 128, 16, 512])
    wKR_d = din("wKR", [128, 16, 128])
    wIF_d = din("wIF", [128, 16, 8])
    wUQ_d = din("wUQ", [4, 128, 4, 512])
    wUK_d = din("wUK", [2, 128, 4, 512])
    wUV_d = din("wUV", [2, 128, 4, 512])
    wMEM_d = din("wMEM", [4, 128, 16, 512])
    wG_d = din("wG", [16, 128, 16, 384])
    wBR_d = din("wBR", [16, 128, 8, 384])
    wOUT_d = din("wOUT", [4, 128, 16, 512])
    out_d = nc.dram_tensor("out", [S, D], F32, kind="ExternalOutput").ap()

    HT_d = dscr("HT", [D, S])
    MQT_d = dscr("MQT", [1024, S])
    MKT_d = dscr("MKT", [1024, S])
    MV_d = dscr("MV", [S, 1024])
    MG_d = dscr("MG", [S, 1024])
    KRT_d = dscr("KRT", [64, S])
    AZ_d = dscr("AZ", [S, 1024])
    CQT_d = dscr("CQT", [1024, S])
    CZ_d = dscr("CZ", [S, 1024])
    QNT_d = dscr("QNT", [1024, S])
    QRT_d = dscr("QRT", [512, S])
    KNT_d = dscr("KNT", [1024, S])
    VA_d = dscr("VA", [S, 1024])
    BH_d = dscr("BH", [64, 128], F32)
    HMT_d = dscr("HMT", [1024, S])
    HAT_d = dscr("HAT", [1024, S])
    HCT_d = dscr("HCT", [1024, S])
    GIF_d = dscr("GIF", [128, 128], F32)

    with contextlib.ExitStack() as es:
        fw = FW(nc, es)
        banks = [(es.enter_context(nc.psum_tensor("bk%d" % i, [128, 512], F32)), Res()) for i in range(8)]
        bi = [0]

        def nb():
            i = bi[0]
            bi[0] = (i + 1) % 8
            return banks[i][0], banks[i][1]

        bhi = [0]

        def nbhi():
            i = 4 + bhi[0]
            bhi[0] = (bhi[0] + 1) % 4
            return banks[i][0], banks[i][1]

        def b16(bk, k):
            return bk[:].bitcast(BF16).rearrange("p (k n) -> p k n", k=k)

        nsb = [0]

        def sbuf(stack, name, shape, dt):
            nsb[0] += 1
            return stack.enter_context(nc.sbuf_tensor("sb%d_%s" % (nsb[0], name), list(shape), dt))

        R0 = Res()

        identb = sbuf(es, "identb", [128, 128], BF16)
        identf = sbuf(es, "identf", [128, 128], F32)
        maskb = sbuf(es, "maskb", [128, 128], BF16)
        maskf = sbuf(es, "maskf", [128, 128], F32)
        onesf = sbuf(es, "onesf", [128, 128], F32)
        GI = sbuf(es, "GI", [128, 4, 16], F32)
        GF = sbuf(es, "GF", [128, 4, 16], F32)
        fw.dma("pool", identb[:], ident_d, writes=[R0])
        fw.dma("sp", identf[:], ident_d, writes=[R0])
        fw.dma("pool", maskb[:], mask_d, writes=[R0])
        fw.dma("sp", maskf[:], mask_d, writes=[R0])
        fw.op("dve", lambda e: e.memset(onesf[:], 1.0), writes=[R0])
        fw.barrier()

        def rmsnorm_T(st2, src_d, ntiles, g_d, dstT, hooks=None):
            xts = [(sbuf(st2, "rn_xt%d" % i, [128, D], F32), Res()) for i in range(4)]
            xbs = [(sbuf(st2, "rn_xb%d" % i, [128, D], BF16), Res()) for i in range(3)]
            junk = sbuf(st2, "rn_junk", [128, D], BF16)
            r_junk = Res()
            gbc = sbuf(st2, "rn_g", [128, D], F32)
            r_g = Res()
            stt = sbuf(st2, "rn_st", [128, 3 * ntiles], F32)
            r_sts = [Res() for _ in range(ntiles)]
            r_dst = Res()
            fw.dma("sp", gbc[:], g_d, writes=[r_g])
            def stats_part(tt):
                xt, r_xt = xts[tt % 4]
                r_st = r_sts[tt]
                c = 3 * tt
                fw.dma("sp", xt[:], src_d[tt * 128:(tt + 1) * 128, :], writes=[r_xt])
                fw.op("act", lambda e: e.activation(out=junk[:], in_=xt[:], func=AF.Square, accum_out=stt[:, c:c + 1]),
                      reads=[r_xt], writes=[r_junk, r_st])
                fw.op("act", lambda e: e.activation(out=stt[:, c + 1:c + 2], in_=stt[:, c:c + 1], func=AF.Ln, scale=1.0 / D, bias=EPS),
                      reads=[r_st], writes=[r_st])
                fw.op("act", lambda e: e.activation(out=stt[:, c + 2:c + 3], in_=stt[:, c + 1:c + 2], func=AF.Exp, scale=-0.5),
                      reads=[r_st], writes=[r_st])

            def write_part(tt):
                xt, r_xt = xts[tt % 4]
                xb, r_xb = xbs[tt % 3]
                r_st = r_sts[tt]
                c = 3 * tt
                fw.op("dve", lambda e: e.scalar_tensor_tensor(out=xb[:], in0=xt[:], scalar=stt[:, c + 2:c + 3], in1=gbc[:],
                                                              op0=ALU.mult, op1=ALU.mult),
                      reads=[r_xt, r_st, r_g], writes=[r_xb])

            def tr_part(tt):
                xb, r_xb = xbs[tt % 3]
                for half in range(2):
                    bk, r_bk = nb()
                    bkb = b16(bk, 8)
                    for k in range(8):
                        kk = half * 8 + k
                        fw.op("pe", lambda e: e.transpose(out=bkb[:, k, :], in_=xb[:, kk * 128:(kk + 1) * 128], identity=identb[:]),
                              reads=[r_xb], writes=[r_bk])
                    if half == 0:
                        fw.op("act", lambda e: e.copy(out=dstT[:, half * 8:(half + 1) * 8, tt * 128:(tt + 1) * 128], in_=bkb),
                              reads=[r_bk], writes=[r_dst])
                    else:
                        fw.op("dve", lambda e: e.tensor_copy(out=dstT[:, half * 8:(half + 1) * 8, tt * 128:(tt + 1) * 128], in_=bkb),
                              reads=[r_bk], writes=[r_dst])

            stats_part(0)
            if ntiles > 1:
                stats_part(1)
            write_part(0)
            for tt in range(ntiles):
                if tt + 2 < ntiles:
                    stats_part(tt + 2)
                    if hooks and (tt + 2) in hooks:
                        hooks[tt + 2](xts[(tt + 2) % 4][1])
                if tt + 1 < ntiles:
                    write_part(tt + 1)
                tr_part(tt)
            return r_dst

        class WPool:
            def __init__(self, stack, name, nslots, elems):
                self.t = [sbuf(stack, "%s%d" % (name, i), [128, elems], BF16) for i in range(nslots)]
                self.r = [Res() for _ in range(nslots)]
                self.i = 0

            def load(self, dram_ap, KC, ncols):
                i = self.i
                self.i = (i + 1) % len(self.t)
                view = self.t[i][:, 0:KC * ncols].rearrange("p (k n) -> p k n", k=KC)
                tot = KC * ncols
                bsz = max(d for d in range(1, 1025) if tot % d == 0)
                fw.dma("pool", self.t[i][:, 0:tot].rearrange("p (a b) -> p a b", b=bsz),
                       dram_ap.rearrange("p k n -> p (k n)").rearrange("p (a b) -> p a b", b=bsz), writes=[self.r[i]])
                return view, self.r[i]

        def proj_fm(wv, wr, chunks, src, r_src, KC, ntok, evac):
            ntb = (ntok + 511) // 512
            for ci, (c0, M) in enumerate(chunks):
                bks = [nb() for _ in range(ntb)]
                for kc in range(KC):
                    for tb in range(ntb):
                        n = min(512, ntok - tb * 512)
                        bk, r_bk = bks[tb]
                        fw.op("pe", lambda e: e.matmul(bk[0:M, 0:n], lhsT=wv[:, kc, c0:c0 + M], rhs=src[:, kc, tb * 512:tb * 512 + n],
                                                       start=(kc == 0), stop=(kc == KC - 1)),
                              reads=[wr, r_src], writes=[r_bk])
                for tb in range(ntb):
                    n = min(512, ntok - tb * 512)
                    evac(ci, tb, bks[tb][0][0:M, 0:n], bks[tb][1])

        def proj_tm(wv, wr, ncols, src, r_src, KC, ntiles, evac):
            for tt in range(ntiles):
                bk, r_bk = nb()
                for kc in range(KC):
                    fw.op("pe", lambda e: e.matmul(bk[:, 0:ncols], lhsT=src[:, kc, tt * 128:(tt + 1) * 128], rhs=wv[:, kc, 0:ncols],
                                                   start=(kc == 0), stop=(kc == KC - 1)),
                          reads=[wr, r_src], writes=[r_bk])
                evac(tt, bk[:, 0:ncols], r_bk)

        with contextlib.ExitStack() as sA:
            cosT = sbuf(sA, "cosT", [64, S], F32)
            sinS = sbuf(sA, "sinS", [64, S], F32)
            hT = sbuf(sA, "hT", [128, 16, S], BF16)
            wpA = WPool(sA, "wpA", 3, 16 * 512)
            preA = [wpA.load(wA_d[i], 16, 512) for i in range(3)]

            def pre_hook(lo, hi):
                def f(r_x):
                    fw._wait("pool", r_x.w)
                    for i in range(lo, hi):
                        preA.append(wpA.load(wA_d[i], 16, 512))
                return f
            if True:
                with contextlib.ExitStack() as s1:
                    rmsnorm_T(s1, x_d, NT, gn_d, hT)
                    fw.barrier()
            if upto >= 2:
              with contextlib.ExitStack() as s2:
                wp = wpA
                cqnT = sbuf(s2, "cqnT", [128, 4, S], BF16)
                ckvnT = sbuf(s2, "ckvnT", [128, 4, S], BF16)
                pre = sbuf(s2, "pre", [128, S + 4], F32)
                r_pre = Res()
                ybuf = sbuf(s2, "ybuf", [128, S], F32)
                r_y = Res()
                obs = [(sbuf(s2, "ob%d" % i, [128, S], BF16), Res()) for i in range(2)]
                t32s = [(sbuf(s2, "t32_%d" % i, [128, 512], F32), Res()) for i in range(3)]
                t16s = [(sbuf(s2, "t16_%d" % i, [128, 512], BF16), Res()) for i in range(4)]
                rot_i = {}

                def rot(lst):
                    i = rot_i.get(id(lst), 0)
                    rot_i[id(lst)] = (i + 1) % len(lst)
                    return lst[i]

                stA = sbuf(s2, "stA", [128, 16], F32)
                r_st = Res()
                cw = sbuf(s2, "cw", [128, 16, 4], F32)
                cb = sbuf(s2, "cb", [128, 16], F32)
                bif = sbuf(s2, "bif", [128, 8], F32)
                gcq = sbuf(s2, "gcq", [128, 512], F32)
                gckv = sbuf(s2, "gckv", [128, 512], F32)
                gmhA = sbuf(s2, "gmhA", [128, 1024], F32)
                r_k = Res()
                r_g = Res()
                r_cn = Res()
                fw.dma("sp", cw[:], convw_d, writes=[r_k])
                fw.dma("sp", cb[:], convb_d, writes=[r_k])
                fw.dma("sp", bif[:], bif_d, writes=[r_k])
                fw.dma("sp", gcq[:], gcq_d, writes=[r_k])
                fw.dma("sp", gckv[:], gckv_d, writes=[r_k])
                fw.dma("sp", gmhA[:], gmh_d, writes=[r_k])
                fw.op("dve", lambda e: e.memset(pre[:, 0:3], 0.0), writes=[r_pre])
                r_HT = Res()
                for k in range(16):
                    fw.dma("sp", HT_d[k * 128:(k + 1) * 128, :], hT[:, k, :], reads=[R0], writes=[r_HT])
                cur = {}

                for h in range(4):
                    wv, wr = preA[h] if h < 3 else wp.load(wA_d[h], 16, 512)

                    def ev_qk(ci, tb, ps, r_ps, h=h):
                        fw.op("act", lambda e: e.copy(out=pre[:, 3 + tb * 512:3 + (tb + 1) * 512], in_=ps), reads=[r_ps], writes=[r_pre])
                        if tb == 3:
                            j = 4 * h + ci
                            fw.op("dve", lambda e: e.tensor_scalar(out=ybuf[:], in0=pre[:, 3:3 + S], scalar1=cw[:, j, 3:4], scalar2=cb[:, j:j + 1],
                                                                   op0=ALU.mult, op1=ALU.add), reads=[r_pre, r_k], writes=[r_y])
                            for tap in (2, 1, 0):
                                fw.op("dve", lambda e: e.scalar_tensor_tensor(out=ybuf[:], in0=pre[:, tap:tap + S], scalar=cw[:, j, tap:tap + 1], in1=ybuf[:],
                                                                              op0=ALU.mult, op1=ALU.add), reads=[r_pre, r_y, r_k], writes=[r_y])
                            ob, r_ob = rot(obs)
                            fw.op("act", lambda e: e.activation(out=ob[:], in_=ybuf[:], func=AF.Silu), reads=[r_y], writes=[r_ob])
                            dst = MQT_d if ci < 2 else MKT_d
                            r0 = h * 256 + (ci % 2) * 128
                            fw.dma("sp", dst[r0:r0 + 128, :], ob[:], reads=[r_ob])

                    proj_fm(wv, wr, [(0, 128), (128, 128), (256, 128), (384, 128)], hT, R0, 16, S, ev_qk)

                QW = 512
                cst = sbuf(s2, "cst", [64, 2], F32)
                fw.dma("sp", cst[:, 0:1], invf_d, writes=[r_k])
                fw.dma("sp", cst[:, 1:2], sgn_d, writes=[r_k])
                posi = pre[0:64, 0:QW].bitcast(I32)
                ang = pre[0:64, QW:2 * QW]
                tq = pre[0:64, 2 * QW:3 * QW]
                ki = pre[0:64, 3 * QW:4 * QW].bitcast(I32)
                rr = ybuf[0:64, 0:QW]
                mm = ybuf[0:64, QW:2 * QW]
                r_rope = Res()

                def rope_quarter(qq):
                    qc = slice(qq * QW, (qq + 1) * QW)
                    RW = [r_pre, r_y]
                    V = lambda f: fw.op("dve", f, reads=[r_k], writes=RW)

                    def wrap(buf):
                        V(lambda e: e.tensor_scalar(out=mm, in0=buf, scalar1=PI, scalar2=-2 * PI, op0=ALU.is_gt, op1=ALU.mult))
                        V(lambda e: e.tensor_tensor(out=buf, in0=mm, in1=buf, op=ALU.add))
                        V(lambda e: e.tensor_scalar(out=mm, in0=buf, scalar1=-PI, scalar2=2 * PI, op0=ALU.is_lt, op1=ALU.mult))
                        V(lambda e: e.tensor_tensor(out=buf, in0=mm, in1=buf, op=ALU.add))
                        V(lambda e: e.tensor_scalar(out=buf, in0=buf, scalar1=PI_SAFE, scalar2=-PI_SAFE, op0=ALU.min, op1=ALU.max))

                    fw.dma("sp", posi, pos_d[:, qc], writes=RW)
                    V(lambda e: e.tensor_copy(out=ang, in_=posi))
                    V(lambda e: e.tensor_scalar(out=ang, in0=ang, scalar1=cst[:, 0:1], scalar2=None, op0=ALU.mult))
                    V(lambda e: e.tensor_scalar(out=tq, in0=ang, scalar1=1.0 / (2 * PI), scalar2=None, op0=ALU.mult))
                    V(lambda e: e.tensor_copy(out=ki, in_=tq))
                    V(lambda e: e.tensor_copy(out=tq, in_=ki))
                    V(lambda e: e.scalar_tensor_tensor(out=rr, in0=tq, scalar=-C1, in1=ang, op0=ALU.mult, op1=ALU.add))
                    V(lambda e: e.scalar_tensor_tensor(out=rr, in0=tq, scalar=-C2, in1=rr, op0=ALU.mult, op1=ALU.add))
                    wrap(rr)
                    fw.op("act", lambda e: e.activation(out=sinS[:, qc], in_=rr, func=AF.Sin), reads=RW, writes=[r_rope])
                    fw.op("dve", lambda e: e.tensor_scalar(out=sinS[:, qc], in0=sinS[:, qc], scalar1=cst[:, 1:2], scalar2=None, op0=ALU.mult),
                          reads=[r_k], writes=[r_rope])
                    V(lambda e: e.tensor_scalar(out=ang, in0=rr, scalar1=PI / 2, scalar2=None, op0=ALU.add))
                    wrap(ang)
                    fw.op("act", lambda e: e.activation(out=cosT[:, qc], in_=ang, func=AF.Sin), reads=RW, writes=[r_rope])

                for blk in range(2):
                    wv, wr = wp.load(wA_d[4 + blk], 16, 512)

                    def ev_v(tt, ps, r_ps, blk=blk):
                        t16, r16 = rot(t16s)
                        fw.op("act", lambda e: e.copy(out=t16[:], in_=ps), reads=[r_ps], writes=[r16])
                        fw.dma("sp", MV_d[tt * 128:(tt + 1) * 128, blk * 512:(blk + 1) * 512], t16[:], reads=[r16])
                        if tt in (2, 9):
                            rope_quarter(2 * blk + (0 if tt == 2 else 1))

                    proj_tm(wv, wr, 512, hT, R0, 16, NT, ev_v)

                for h in range(4):
                    wv, wr = wp.load(wA_d[6 + h], 16, 512)

                    def ev_oz(tt, ps, r_ps, h=h):
                        t32, r32 = rot(t32s)
                        fw.op("act", lambda e: e.activation(out=t32[:], in_=ps, func=AF.Sigmoid), reads=[r_ps], writes=[r32])
                        t16, r16 = rot(t16s)
                        fw.op("dve", lambda e: e.tensor_tensor(out=t32[:, 0:256], in0=t32[:, 0:256], in1=t32[:, 256:512], op=ALU.mult),
                              reads=[r32], writes=[r32])
                        fw.op("dve", lambda e: e.tensor_tensor(out=t32[:, 0:256], in0=ps[:, 256:512], in1=t32[:, 0:256], op=ALU.mult),
                              reads=[r32, r_ps], writes=[r32])
                        fw.op("dve", lambda e: e.tensor_tensor(out=t16[:, 0:256], in0=t32[:, 0:256], in1=gmhA[:, h * 256:(h + 1) * 256], op=ALU.mult),
                              reads=[r32, r_k], writes=[r16])
                        fw.dma("sp", MG_d[tt * 128:(tt + 1) * 128, h * 256:(h + 1) * 256], t16[:, 0:256], reads=[r16])

                    proj_tm(wv, wr, 512, hT, R0, 16, NT, ev_oz)

                wv, wr = wp.load(wIF_d, 16, 8)

                def ev_if(tt, ps, r_ps):
                    fw.op("dve", lambda e: e.tensor_tensor(out=stA[:, 0:8], in0=ps, in1=bif[:], op=ALU.add), reads=[r_ps, r_k], writes=[r_st])
                    fw.op("dve", lambda e: e.tensor_copy(out=GI[:, :, tt], in_=stA[:, 0:4]), reads=[r_st], writes=[r_g])
                    fw.op("act", lambda e: e.activation(out=stA[:, 8:12], in_=stA[:, 4:8], func=AF.Exp, scale=-1.0), reads=[r_st], writes=[r_st])
                    fw.op("act", lambda e: e.activation(out=stA[:, 12:16], in_=stA[:, 8:12], func=AF.Ln, bias=1.0), reads=[r_st], writes=[r_st])
                    fw.op("dve", lambda e: e.tensor_scalar(out=GF[:, :, tt], in0=stA[:, 12:16], scalar1=-1.0, scalar2=None, op0=ALU.mult),
                          reads=[r_st], writes=[r_g])

                proj_tm(wv, wr, 8, hT, R0, 16, NT, ev_if)

                for which, (gb, dstT) in enumerate([(gcq, cqnT), (gckv, ckvnT)]):
                    wv, wr = wp.load(wA_d[10 + which], 16, 512)

                    def ev_c(tt, ps, r_ps, gb=gb, dstT=dstT):
                        tj, rj = rot(t16s)
                        fw.op("act", lambda e: e.activation(out=tj[:], in_=ps, func=AF.Square, accum_out=stA[:, 0:1]), reads=[r_ps], writes=[rj, r_st])
                        fw.op("act", lambda e: e.activation(out=stA[:, 1:2], in_=stA[:, 0:1], func=AF.Ln, scale=1.0 / 512, bias=EPS), reads=[r_st], writes=[r_st])
                        fw.op("act", lambda e: e.activation(out=stA[:, 2:3], in_=stA[:, 1:2], func=AF.Exp, scale=-0.5), reads=[r_st], writes=[r_st])
                        t16, r16 = rot(t16s)
                        fw.op("dve", lambda e: e.scalar_tensor_tensor(out=t16[:], in0=ps, scalar=stA[:, 2:3], in1=gb[:], op0=ALU.mult, op1=ALU.mult),
                              reads=[r_ps, r_st, r_k], writes=[r16])
                        if cur.get("ctr") is not None:
                            cur["ctr"]()

                        def tr(t16=t16, r16=r16, tt=tt, dstT=dstT):
                            bk, r_bk = nb()
                            bkb = b16(bk, 8)
                            for k in range(4):
                                fw.op("pe", lambda e: e.transpose(out=bkb[:, k, :], in_=t16[:, k * 128:(k + 1) * 128], identity=identb[:]),
                                      reads=[r16], writes=[r_bk])
                            fw.op("act", lambda e: e.copy(out=dstT[:, 0:4, tt * 128:(tt + 1) * 128], in_=bkb[:, 0:4, :]), reads=[r_bk], writes=[r_cn])

                        cur["ctr"] = tr

                    proj_tm(wv, wr, 512, hT, R0, 16, NT, ev_c)
                    cur["ctr"]()
                    cur["ctr"] = None

                def rope_ev(kind, tb, ps, r_ps, dst_rows):
                    cols = slice(tb * 512, (tb + 1) * 512)
                    if kind == 1:
                        fw.op("dve", lambda e: e.tensor_tensor(out=ybuf[0:64, cols], in0=ps, in1=cosT[:, cols], op=ALU.mult), reads=[r_ps, r_rope], writes=[r_y])
                    else:
                        if tb == 0:
                            cur["rope"] = rot(obs)
                        ob, r_ob = cur["rope"]
                        t32, r32 = rot(t32s)
                        fw.op("dve", lambda e: e.tensor_tensor(out=t32[0:64, :], in0=ps, in1=sinS[:, cols], op=ALU.mult), reads=[r_ps, r_rope], writes=[r32])
                        fw.op("dve", lambda e: e.tensor_tensor(out=ob[0:64, cols], in0=ybuf[0:64, cols], in1=t32[0:64, :], op=ALU.add),
                              reads=[r_y, r32], writes=[r_ob])
                        if tb == 3:
                            fw.dma("sp", dst_rows, ob[0:64, :], reads=[r_ob])

                def copy_ev(tb, ps, r_ps, dst_rows, alt):
                    cols = slice(tb * 512, (tb + 1) * 512)
                    if tb == 0:
                        cur["cp"] = rot(obs)
                    ob, r_ob = cur["cp"]
                    if alt % 2 == 0:
                        fw.op("act", lambda e: e.copy(out=ob[:, cols], in_=ps), reads=[r_ps], writes=[r_ob])
                    else:
                        fw.op("dve", lambda e: e.tensor_copy(out=ob[:, cols], in_=ps), reads=[r_ps], writes=[r_ob])
                    if tb == 3:
                        fw.dma("sp", dst_rows, ob[:], reads=[r_ob])

                wv, wr = wp.load(wKR_d, 16, 128)
                proj_fm(wv, wr, [(0, 64), (64, 64)], hT, R0, 16, S,
                        lambda ci, tb, ps, r_ps: rope_ev(ci + 1, tb, ps, r_ps, KRT_d[0:64, :]))

                def silu_ev(dst_d, blk):
                    def ev(tt, ps, r_ps):
                        t16, r16 = rot(t16s)
                        fw.op("act", lambda e: e.activation(out=t16[:], in_=ps, func=AF.Silu), reads=[r_ps], writes=[r16])
                        fw.dma("sp", dst_d[tt * 128:(tt + 1) * 128, blk * 512:(blk + 1) * 512], t16[:], reads=[r16])
                    return ev

                for blk in range(2):
                    wv, wr = wp.load(wA_d[12 + blk], 16, 512)
                    proj_tm(wv, wr, 512, hT, R0, 16, NT, silu_ev(AZ_d, blk))
                for blk in range(2):
                    wv, wr = wp.load(wA_d[14 + blk], 16, 512)
                    proj_fm(wv, wr, [(0, 128), (128, 128), (256, 128), (384, 128)], hT, R0, 16, S,
                            lambda ci, tb, ps, r_ps, blk=blk: copy_ev(tb, ps, r_ps, CQT_d[(blk * 4 + ci) * 128:(blk * 4 + ci + 1) * 128, :], tb))
                for blk in range(2):
                    wv, wr = wp.load(wA_d[16 + blk], 16, 512)
                    proj_tm(wv, wr, 512, hT, R0, 16, NT, silu_ev(CZ_d, blk))

                for pr in range(4):
                    wv, wr = wp.load(wUQ_d[pr], 4, 512)

                    def ev_q(ci, tb, ps, r_ps, pr=pr):
                        hh = 2 * pr + ci // 3
                        kind = ci % 3
                        if kind == 0:
                            copy_ev(tb, ps, r_ps, QNT_d[hh * 128:(hh + 1) * 128, :], tb)
                        else:
                            rope_ev(kind, tb, ps, r_ps, QRT_d[hh * 64:(hh + 1) * 64, :])

                    proj_fm(wv, wr, [(0, 128), (128, 64), (192, 64), (256, 128), (384, 64), (448, 64)], cqnT, r_cn, 4, S, ev_q)
                for blk in range(2):
                    wv, wr = wp.load(wUK_d[blk], 4, 512)
                    proj_fm(wv, wr, [(0, 128), (128, 128), (256, 128), (384, 128)], ckvnT, r_cn, 4, S,
                            lambda ci, tb, ps, r_ps, blk=blk: copy_ev(tb, ps, r_ps, KNT_d[(blk * 4 + ci) * 128:(blk * 4 + ci + 1) * 128, :], tb))
                for blk in range(2):
                    wv, wr = wp.load(wUV_d[blk], 4, 512)

                    def ev_va(tt, ps, r_ps, blk=blk):
                        t16, r16 = rot(t16s)
                        fw.op("act", lambda e: e.copy(out=t16[:], in_=ps), reads=[r_ps], writes=[r16])
                        fw.dma("sp", VA_d[tt * 128:(tt + 1) * 128, blk * 512:(blk + 1) * 512], t16[:], reads=[r16])

                    proj_tm(wv, wr, 512, ckvnT, r_cn, 4, NT, ev_va)
                if debug:
                    fw.dma("sp", GIF_d[:, 0:64], GI[:].rearrange("p h t -> p (h t)"), reads=[r_g])
                    fw.dma("sp", GIF_d[:, 64:128], GF[:].rearrange("p h t -> p (h t)"), reads=[r_g])
                fw.barrier()
            fw.barrier()
        rot_i = {}

        def rot(lst):
            i = rot_i.get(id(lst), 0)
            rot_i[id(lst)] = (i + 1) % len(lst)
            return lst[i]

        def mk(stack, name, n, shape, dt):
            return [(sbuf(stack, "%s%d" % (name, i), shape, dt), Res()) for i in range(n)]

        if upto >= 3:
          with contextlib.ExitStack() as sC:
            memnT = sbuf(sC, "memnT", [128, 16, 256], BF16)
            kmT = sbuf(sC, "kmT", [128, 8, 256], BF16)
            vmx = sbuf(sC, "vmx", [128, 2, 4, 260], BF16)
            if True:
                s5 = sC
                wp = WPool(sC, "wpM", 2, 16 * 512)
                cqs = mk(s5, "cq", 2, [128, 2, S], BF16)
                czs = mk(s5, "cz", 2, [128, 16, 256], BF16)
                hcTs = mk(s5, "hcT", 2, [128, 2, S], BF16)
                Ps = mk(s5, "Pc", 4, [128, 512], BF16)
                hct = mk(s5, "hct", 3, [128, 256], BF16)
                stats = mk(s5, "stC", 4, [128, 8], F32)
                r_km = Res()
                r_vm = Res()
                wm = [wp.load(wMEM_d[0], 16, 512), wp.load(wMEM_d[1], 16, 512)]

                def c_loads(h):
                    fw.dma("sp", cqs[h % 2][0][:], CQT_d[h * 256:(h + 1) * 256, :].rearrange("(k p) t -> p k t", p=128), writes=[cqs[h % 2][1]])
                    fw.dma("sp", czs[h % 2][0][:], CZ_d[:, h * 256:(h + 1) * 256].rearrange("(t p) c -> p t c", p=128), writes=[czs[h % 2][1]])

                def c_scores(h, qb):
                    cq, r_cq = cqs[h % 2]
                    Pm = []
                    for mt in range(2):
                        bk, r_bk = nbhi()
                        for kc in range(2):
                            fw.op("pe", lambda e: e.matmul(bk[:, 0:512], lhsT=kmT[:, 2 * h + kc, mt * 128:(mt + 1) * 128],
                                                           rhs=cq[:, kc, qb * 512:(qb + 1) * 512], start=(kc == 0), stop=(kc == 1)),
                                  reads=[r_cq, r_km], writes=[r_bk])
                        P, r_P = rot(Ps)
                        fw.op("act", lambda e: e.activation(out=P[:], in_=bk[:, 0:512], func=AF.Exp, scale=1.0 / 16), reads=[r_bk], writes=[r_P])
                        Pm.append((P, r_P))
                    return Pm

                def c_tr(pend):
                    hc_, r_hc, hcT, r_hcT, tt = pend
                    bt, r_bt = nbhi()
                    btb = b16(bt, 8)
                    for kc in range(2):
                        fw.op("pe", lambda e: e.transpose(out=btb[:, kc, :], in_=hc_[:, kc * 128:(kc + 1) * 128], identity=identb[:]),
                              reads=[r_hc], writes=[r_bt])
                    fw.op("act", lambda e: e.copy(out=hcT[:, 0:2, tt * 128:(tt + 1) * 128], in_=btb[:, 0:2, :]), reads=[r_bt], writes=[r_hcT])

                steps = [(h, qb) for h in range(4) for qb in range(4)]
                r_memn = rmsnorm_T(sC, mem_d, 2, gm_d, memnT)
                c_loads(0)
                c_loads(1)
                fw.op("dve", lambda e: e.memset(vmx[:, :, :, 256:257], 1.0), writes=[r_vm])
                for blk in range(2):
                    wv, wr = wm[blk]
                    proj_fm(wv, wr, [(0, 128), (128, 128), (256, 128), (384, 128)], memnT, r_memn, 16, 256,
                            lambda ci, tb, ps, r_ps, blk=blk: fw.op("act", lambda e: e.copy(out=kmT[:, blk * 4 + ci, :], in_=ps), reads=[r_ps], writes=[r_km]))
                    if blk == 0:
                        wm.append(wp.load(wMEM_d[2], 16, 512))
                wm.append(wp.load(wMEM_d[3], 16, 512))
                for blk in range(2):
                    wv, wr = wm[2 + blk]
                    proj_tm(wv, wr, 512, memnT, r_memn, 16, 2,
                            lambda tt, ps, r_ps, blk=blk: fw.op("dve", lambda e: e.tensor_copy(out=vmx[:, tt, 2 * blk:2 * blk + 2, 0:256],
                                                                                             in_=ps.rearrange("p (h c) -> p h c", h=2)),
                                                               reads=[r_ps], writes=[r_vm]))
                nxtP = c_scores(0, 0)
                pend = None
                for si, (h, qb) in enumerate(steps):
                    Pm = nxtP
                    cz, r_cz = czs[h % 2]
                    hcT, r_hcT = hcTs[h % 2]
                    if si + 1 < len(steps):
                        nxtP = c_scores(*steps[si + 1])
                    for q4 in range(4):
                        tt = qb * 4 + q4
                        bo, r_bo = banks[q4]
                        for mt in range(2):
                            fw.op("pe", lambda e: e.matmul(bo[:, 0:257], lhsT=Pm[mt][0][:, q4 * 128:(q4 + 1) * 128], rhs=vmx[:, mt, h, 0:257],
                                                           start=(mt == 0), stop=(mt == 1)),
                                  reads=[Pm[mt][1], r_vm], writes=[r_bo])
                        if pend is not None:
                            c_tr(pend)
                        sc, r_sc = rot(stats)
                        fw.op("dve", lambda e: e.reciprocal(out=sc[:, 0:1], in_=bo[:, 256:257]), reads=[r_bo], writes=[r_sc])
                        hc_, r_hc = rot(hct)
                        fw.op("dve", lambda e: e.scalar_tensor_tensor(out=hc_[:], in0=bo[:, 0:256], scalar=sc[:, 0:1], in1=cz[:, tt, :],
                                                                      op0=ALU.mult, op1=ALU.mult),
                              reads=[r_bo, r_sc, r_cz], writes=[r_hc])
                        pend = (hc_, r_hc, hcT, r_hcT, tt)
                    if qb == 3:
                        c_tr(pend)
                        pend = None
                        fw.dma("sp", HCT_d[h * 256:(h + 1) * 256, :].rearrange("(k p) t -> p k t", p=128), hcT[:], reads=[r_hcT])
                        if h + 2 < 4:
                            c_loads(h + 2)
                fw.barrier()

        if upto >= 4:
          with contextlib.ExitStack() as s6:
            kr = sbuf(s6, "kr", [128, S], BF16)
            r_kr = Res()
            qns = mk(s6, "qn", 2, [128, S], BF16)
            qrs = mk(s6, "qr", 2, [128, S], BF16)
            kns = mk(s6, "kn", 2, [128, S], BF16)
            vxs = mk(s6, "vx", 2, [128, 16, 132], BF16)
            azs = mk(s6, "az", 2, [128, 16, 128], BF16)
            haTs = mk(s6, "haT", 2, [128, S], BF16)
            Ps = mk(s6, "Pa", 3, [128, 512], BF16)
            hat = mk(s6, "hat", 8, [128, 128], BF16)
            stats = mk(s6, "stB", 4, [128, 8], F32)
            fw.op("dve", lambda e: e.memset(kr[64:128, :], 0.0), writes=[r_kr])
            fw.dma("sp", kr[0:64, :], KRT_d[0:64, :], writes=[r_kr])
            for vx, r_vx in vxs:
                fw.op("dve", lambda e: e.memset(vx[:, :, 128:129], 1.0), writes=[r_vx])
            for qr, r_qr in qrs:
                fw.op("dve", lambda e: e.memset(qr[64:128, :], 0.0), writes=[r_qr])
            SC = 1.0 / math.sqrt(192.0)

            def mla_loads(h):
                fw.dma("sp", qns[h % 2][0][:], QNT_d[h * 128:(h + 1) * 128, :], writes=[qns[h % 2][1]])
                fw.dma("sp", qrs[h % 2][0][0:64, :], QRT_d[h * 64:(h + 1) * 64, :], writes=[qrs[h % 2][1]])
                fw.dma("sp", kns[h % 2][0][:], KNT_d[h * 128:(h + 1) * 128, :], writes=[kns[h % 2][1]])
                fw.dma("sp", vxs[h % 2][0][:, :, 0:128], VA_d[:, h * 128:(h + 1) * 128].rearrange("(t p) c -> p t c", p=128), writes=[vxs[h % 2][1]])
                fw.dma("sp", azs[h % 2][0][:], AZ_d[:, h * 128:(h + 1) * 128].rearrange("(t p) c -> p t c", p=128), writes=[azs[h % 2][1]])

            mla_loads(0)
            pend_ev = [None]
            for h in range(8):
                qn, r_qn = qns[h % 2]
                qr, r_qr = qrs[h % 2]
                kn, r_kn = kns[h % 2]
                vx, r_vx = vxs[h % 2]
                az, r_az = azs[h % 2]
                haT, r_haT = haTs[h % 2]
                if h + 1 < 8:
                    mla_loads(h + 1)
                for qb in range(4):
                    bos = [banks[i] for i in range(4)]
                    nkt = 4 * qb + 4

                    def emit_S(kt, qb=qb):
                        q_lo = max(kt, 4 * qb)
                        ncols = (4 * qb + 4 - q_lo) * 128
                        q0 = q_lo * 128
                        kc_ = slice(kt * 128, (kt + 1) * 128)
                        bs, r_bs = nbhi()
                        fw.op("pe", lambda e: e.matmul(bs[:, 0:ncols], lhsT=kn[:, kc_], rhs=qn[:, q0:q0 + ncols], start=True, stop=False),
                              reads=[r_kn, r_qn], writes=[r_bs])
                        fw.op("pe", lambda e: e.matmul(bs[:, 0:ncols], lhsT=kr[:, kc_], rhs=qr[:, q0:q0 + ncols], start=False, stop=True),
                              reads=[r_kr, r_qr], writes=[r_bs])
                        P, r_P = rot(Ps)
                        fw.op("act", lambda e: e.activation(out=P[:, 0:ncols], in_=bs[:, 0:ncols], func=AF.Exp, scale=SC), reads=[r_bs], writes=[r_P])
                        if kt >= 4 * qb:
                            fw.op("dve", lambda e: e.tensor_tensor(out=P[:, 0:128], in0=P[:, 0:128], in1=maskb[:], op=ALU.mult), reads=[r_P], writes=[r_P])
                        return P, r_P, q_lo

                    cur_s = emit_S(0)
                    if pend_ev[0] is not None:
                        pend_ev[0]()
                        pend_ev[0] = None
                    for kt in range(nkt):
                        nxt_s = emit_S(kt + 1) if kt + 1 < nkt else None
                        P, r_P, q_lo = cur_s
                        for qt in range(q_lo, 4 * qb + 4):
                            j = qt - q_lo
                            bo, r_bo = bos[qt - 4 * qb]
                            fw.op("pe", lambda e: e.matmul(bo[:, 0:129], lhsT=P[:, j * 128:(j + 1) * 128], rhs=vx[:, kt, 0:129],
                                                           start=(kt == 0), stop=(kt == qt)),
                                  reads=[r_P, r_vx], writes=[r_bo])
                        cur_s = nxt_s
                    has = []
                    for q4 in range(4):
                        qt = 4 * qb + q4
                        bo, r_bo = bos[q4]
                        sc, r_sc = rot(stats)
                        fw.op("dve", lambda e: e.reciprocal(out=sc[:, 0:1], in_=bo[:, 128:129]), reads=[r_bo], writes=[r_sc])
                        ha_, r_ha = rot(hat)
                        fw.op("dve", lambda e: e.scalar_tensor_tensor(out=ha_[:], in0=bo[:, 0:128], scalar=sc[:, 0:1], in1=az[:, qt, :],
                                                                      op0=ALU.mult, op1=ALU.mult),
                              reads=[r_bo, r_sc, r_az], writes=[r_ha])
                        has.append((ha_, r_ha, qt))

                    def ev_tail(has=has, haT=haT, r_haT=r_haT, h=h, last=(qb == 3)):
                        bt, r_bt = nbhi()
                        btb = b16(bt, 8)
                        for q4, (ha_, r_ha, qt) in enumerate(has):
                            fw.op("pe", lambda e: e.transpose(out=btb[:, q4, :], in_=ha_[:], identity=identb[:]), reads=[r_ha], writes=[r_bt])
                        q0 = has[0][2] * 128
                        fw.op("act", lambda e: e.copy(out=haT[:, q0:q0 + 512].rearrange("p (a b) -> p a b", a=4), in_=btb[:, 0:4, :]), reads=[r_bt], writes=[r_haT])
                        if last:
                            fw.dma("sp", HAT_d[h * 128:(h + 1) * 128, :], haT[:], reads=[r_haT])

                    pend_ev[0] = ev_tail
            pend_ev[0]()
            fw.barrier()

        hTh = sbuf(es, "hTh", [128, 16, 1024], BF16)
        r_in = Res()
        if upto >= 5:
          with contextlib.ExitStack() as s7:
            Bt = sbuf(s7, "Bt", [128, 64], F32)
            Gt = sbuf(s7, "Gt", [128, 64], F32)
            At = sbuf(s7, "At", [128, 64], F32)
            WS = sbuf(s7, "WS", [128, 64], F32)
            tmpg = sbuf(s7, "tmpg", [128, 64], F32)
            btT = sbuf(s7, "btT", [64, 128], F32)
            Mneg = sbuf(s7, "Mneg", [128, S], F32)
            r_gp = Res()
            r_bh = Res()
            for cc in range(16):
                fw.op("pool", lambda e: e.tensor_scalar(out=Mneg[:, cc * 128:(cc + 1) * 128], in0=maskf[:], scalar1=-1.0, scalar2=30000.0, op0=ALU.add, op1=ALU.mult),
                      writes=[r_gp])
            GFv = GF[:].rearrange("p h t -> p (h t)")
            GIv = GI[:].rearrange("p h t -> p (h t)")
            bk, r_bk = nb()
            fw.op("pe", lambda e: e.matmul(bk[:, 0:64], lhsT=maskf[:], rhs=GFv, start=True, stop=True), reads=[R0], writes=[r_bk])
            fw.op("act", lambda e: e.copy(out=Bt[:], in_=bk[:, 0:64]), reads=[r_bk], writes=[r_gp])
            bk2, r_bk2 = nb()
            fw.op("pe", lambda e: e.matmul(bk2[:, 0:64], lhsT=onesf[:], rhs=GFv, start=True, stop=True), reads=[R0], writes=[r_bk2])
            fw.op("dve", lambda e: e.tensor_copy(out=Gt[:], in_=bk2[:, 0:64]), reads=[r_bk2], writes=[r_gp])
            fw.op("dve", lambda e: e.scalar_tensor_tensor(out=At[:], in0=GIv, scalar=-LN16, in1=Bt[:], op0=ALU.add, op1=ALU.subtract),
                  reads=[r_gp], writes=[r_gp])
            fw.op("dve", lambda e: e.tensor_tensor(out=tmpg[:], in0=Gt[:], in1=At[:], op=ALU.add), reads=[r_gp], writes=[r_gp])
            fw.op("act", lambda e: e.activation(out=WS[:], in_=tmpg[:], func=AF.Exp), reads=[r_gp], writes=[r_gp])
            bk3, r_bk3 = nb()
            fw.op("pe", lambda e: e.transpose(out=bk3[0:64, 0:128], in_=Bt[:], identity=identf[:]), reads=[r_gp], writes=[r_bk3])
            fw.op("act", lambda e: e.copy(out=btT[:], in_=bk3[0:64, 0:128]), reads=[r_bk3], writes=[r_gp])
            fw.dma("sp", BH_d, btT[:], reads=[r_gp], writes=[r_bh])
            BHv = BH_d.rearrange("(h t) c -> h (t c)", h=4)

            NH = 2
            qTs = mk(s7, "mqT", NH, [128, 2, S], BF16)
            kTs = mk(s7, "mkT", NH, [128, 2, S], BF16)
            vxs = mk(s7, "mvx", NH, [128, 16, 260], BF16)
            mgs = mk(s7, "mmg", NH, [128, 16, 256], BF16)
            Brs = mk(s7, "Brow", NH, [128, S], F32)
            EBs = mk(s7, "EBrow", NH, [128, S], F32)
            BMs = mk(s7, "BrM", NH, [128, S], F32)
            hmTs = mk(s7, "hmT", NH, [128, 2, S], BF16)
            Cfs = mk(s7, "Cf", NH, [128, 2, 260], F32)
            Cb3 = [mk(s7, "Cb%d_" % i, 3, [128, 2, 260], BF16) for i in range(NH)]
            qeb2 = [mk(s7, "qeb%d_" % i, 3, [128, 2, 128], BF16) for i in range(NH)]
            DTs = mk(s7, "DT", NH, [128, 128], F32)
            PT2 = [mk(s7, "PT%d_" % i, 3, [128, 128], BF16) for i in range(NH)]
            kws = mk(s7, "kw", NH, [128, 2, 128], BF16)
            hmt2 = [mk(s7, "hmt%d_" % i, 2, [128, 256], BF16) for i in range(NH)]
            jks = mk(s7, "jk", NH, [128, 256], BF16)
            stats = mk(s7, "stM", 2, [128, 2, 8], F32)
            for vx, r_vx in vxs:
                fw.op("dve", lambda e: e.memset(vx[:, :, 256:257], 1.0), writes=[r_vx])
            for hg in ((0, 1), (2, 3)):
                if hg == (2, 3):
                    fw.dma("sp", hTh[:], HT_d[:, 0:1024].rearrange("(k p) t -> p k t", p=128), writes=[r_in])
                for i, h in enumerate(hg):
                    fw.dma("sp", qTs[i][0][:], MQT_d[h * 256:(h + 1) * 256, :].rearrange("(k p) t -> p k t", p=128), writes=[qTs[i][1]])
                    fw.dma("sp", kTs[i][0][:], MKT_d[h * 256:(h + 1) * 256, :].rearrange("(k p) t -> p k t", p=128), writes=[kTs[i][1]])
                    fw.dma("sp", vxs[i][0][:, :, 0:256], MV_d[:, h * 256:(h + 1) * 256].rearrange("(t p) c -> p t c", p=128), writes=[vxs[i][1]])
                    fw.dma("sp", mgs[i][0][:], MG_d[:, h * 256:(h + 1) * 256].rearrange("(t p) c -> p t c", p=128), writes=[mgs[i][1]])
                    fw.dma("sp", Brs[i][0][:].rearrange("p (o t) -> p o t", o=1), BHv[h:h + 1, :].partition_broadcast(128),
                           reads=[r_bh], writes=[Brs[i][1]])
                    fw.op("act", lambda e: e.activation(out=EBs[i][0][:], in_=Brs[i][0][:], func=AF.Exp), reads=[Brs[i][1]], writes=[EBs[i][1]])
                    fw.op("dve", lambda e: e.tensor_tensor(out=BMs[i][0][:], in0=Brs[i][0][:], in1=Mneg[:], op=ALU.add),
                          reads=[Brs[i][1], r_gp], writes=[BMs[i][1]])
                def P1a(c, hg=hg):
                    cols = slice(c * 128, (c + 1) * 128)
                    for i, h in enumerate(hg):
                        qT, r_qT = qTs[i]
                        kT, r_kT = kTs[i]
                        Br, r_Br = BMs[i]
                        EB, r_EB = EBs[i]
                        qeb, r_qeb = qeb2[i][c % 3]
                        DT, r_DT = DTs[i]
                        PT, r_PT = PT2[i][c % 3]
                        kw, r_kw = kws[i]
                        col = h * 16 + c
                        if c > 0:
                            for k in range(2):
                                fw.op("pool", lambda e: e.tensor_tensor(out=qeb[:, k, :], in0=qT[:, k, cols], in1=EB[:, cols], op=ALU.mult),
                                      reads=[r_qT, r_EB], writes=[r_qeb])
                        bs, r_bs = nb()
                        for k in range(2):
                            fw.op("pe", lambda e: e.matmul(bs[:, 0:128], lhsT=kT[:, k, cols], rhs=qT[:, k, cols], start=(k == 0), stop=(k == 1)),
                                  reads=[r_kT, r_qT], writes=[r_bs])
                        fw.op("act", lambda e: e.activation(out=DT[:], in_=Br[:, cols], func=AF.Exp, bias=At[:, col:col + 1], scale=1.0),
                              reads=[r_Br, r_gp], writes=[r_DT])
                        fw.op("dve", lambda e: e.tensor_tensor(out=PT[:], in0=bs[:, 0:128], in1=DT[:], op=ALU.mult), reads=[r_bs, r_DT], writes=[r_PT])
                        if c < 15:
                            bkt, r_bkt = nb()
                            bktb = b16(bkt, 8)
                            for k in range(2):
                                fw.op("pe", lambda e: e.transpose(out=bktb[:, k, :], in_=kT[:, k, cols], identity=identb[:]), reads=[r_kT], writes=[r_bkt])
                            fw.op("dve", lambda e: e.tensor_scalar(out=kw[:], in0=bktb[:, 0:2, :], scalar1=WS[:, col:col + 1], scalar2=None, op0=ALU.mult),
                                  reads=[r_bkt, r_gp], writes=[r_kw])

                def P1b(c, hg=hg):
                    if c >= 15:
                        return
                    for i, h in enumerate(hg):
                        vx, r_vx = vxs[i]
                        EB, r_EB = EBs[i]
                        Cf, r_Cf = Cfs[i]
                        Cb, r_Cb = Cb3[i][c % 3]
                        kw, r_kw = kws[i]
                        for kc in range(2):
                            bu, r_bu = nb()
                            fw.op("pe", lambda e: e.matmul(bu[:, 0:257], lhsT=kw[:, kc, :], rhs=vx[:, c, 0:257], start=True, stop=True),
                                  reads=[r_kw, r_vx], writes=[r_bu])
                            if c == 0:
                                fw.op("dve", lambda e: e.tensor_copy(out=Cf[:, kc, 0:257], in_=bu[:, 0:257]), reads=[r_bu], writes=[r_Cf])
                            else:
                                fw.op("dve", lambda e: e.scalar_tensor_tensor(out=Cf[:, kc, 0:257], in0=Cf[:, kc, 0:257],
                                                                              scalar=EB[:, c * 128 + 127:c * 128 + 128], in1=bu[:, 0:257],
                                                                              op0=ALU.mult, op1=ALU.add),
                                      reads=[r_bu, r_Cf, r_EB], writes=[r_Cf])
                        fw.op("act", lambda e: e.copy(out=Cb[:, :, 0:257], in_=Cf[:, :, 0:257]), reads=[r_Cf], writes=[r_Cb])

                def P2a(c, hg=hg):
                    sc, r_sc = stats[c % 2]
                    bos = []
                    for i, h in enumerate(hg):
                        vx, r_vx = vxs[i]
                        qeb, r_qeb = qeb2[i][c % 3]
                        PT, r_PT = PT2[i][c % 3]
                        jk, r_jk = jks[i]
                        bo, r_bo = nb()
                        bos.append((bo, r_bo))
                        if c > 0:
                            Cb, r_Cb = Cb3[i][(c - 1) % 3]
                            for k in range(2):
                                fw.op("pe", lambda e: e.matmul(bo[:, 0:257], lhsT=qeb[:, k, :], rhs=Cb[:, k, 0:257], start=(k == 0), stop=False),
                                      reads=[r_qeb, r_Cb], writes=[r_bo])
                        fw.op("pe", lambda e: e.matmul(bo[:, 0:257], lhsT=PT[:], rhs=vx[:, c, 0:257], start=(c == 0), stop=True),
                              reads=[r_PT, r_vx], writes=[r_bo])
                        fw.op("act", lambda e: e.activation(out=sc[:, i, 0:1], in_=bo[:, 256:257], func=AF.Square), reads=[r_bo], writes=[r_sc])
                        fw.op("act", lambda e: e.activation(out=jk[:], in_=bo[:, 0:256], func=AF.Square, accum_out=sc[:, i, 1:2]), reads=[r_bo], writes=[r_jk, r_sc])
                    return bos

                def P2b(c, bos, hg=hg):
                    sc, r_sc = stats[c % 2]
                    fw.op("dve", lambda e: e.tensor_scalar(out=sc[:, :, 2], in0=sc[:, :, 0], scalar1=1.0, scalar2=EPS, op0=ALU.max, op1=ALU.mult),
                          reads=[r_sc], writes=[r_sc])
                    fw.op("dve", lambda e: e.scalar_tensor_tensor(out=sc[:, :, 3], in0=sc[:, :, 1], scalar=1.0 / 256, in1=sc[:, :, 2], op0=ALU.mult, op1=ALU.add),
                          reads=[r_sc], writes=[r_sc])
                    fw.op("act", lambda e: e.activation(out=sc[:, :, 4], in_=sc[:, :, 3], func=AF.Ln), reads=[r_sc], writes=[r_sc])
                    fw.op("act", lambda e: e.activation(out=sc[:, :, 5], in_=sc[:, :, 4], func=AF.Exp, scale=-0.5), reads=[r_sc], writes=[r_sc])
                    for i, h in enumerate(hg):
                        mg, r_mg = mgs[i]
                        hmt, r_hmt = hmt2[i][c % 2]
                        bo, r_bo = bos[i]
                        fw.op("dve", lambda e: e.scalar_tensor_tensor(out=hmt[:], in0=bo[:, 0:256], scalar=sc[:, i, 5:6], in1=mg[:, c, :],
                                                                      op0=ALU.mult, op1=ALU.mult),
                              reads=[r_bo, r_sc, r_mg], writes=[r_hmt])

                def P2tail(c, hg=hg):
                    cols = slice(c * 128, (c + 1) * 128)
                    for i, h in enumerate(hg):
                        hmT, r_hmT = hmTs[i]
                        hmt, r_hmt = hmt2[i][c % 2]
                        bt, r_bt = nb()
                        btb = b16(bt, 8)
                        for kc in range(2):
                            fw.op("pe", lambda e: e.transpose(out=btb[:, kc, :], in_=hmt[:, kc * 128:(kc + 1) * 128], identity=identb[:]),
                                  reads=[r_hmt], writes=[r_bt])
                        fw.op("act", lambda e: e.copy(out=hmT[:, 0:2, cols], in_=btb[:, 0:2, :]), reads=[r_bt], writes=[r_hmT])

                P1a(0)
                P1b(0)
                P1a(1)
                P1b(1)
                for c in range(16):
                    if c + 2 < 16:
                        P1a(c + 2)
                    bos_c = P2a(c)
                    if c > 0:
                        P2tail(c - 1)
                    P2b(c, bos_c)
                    if c + 2 < 16:
                        P1b(c + 2)
                P2tail(15)
                for i, h in enumerate(hg):
                    fw.dma("sp", HMT_d[h * 256:(h + 1) * 256, :].rearrange("(k p) t -> p k t", p=128), hmTs[i][0][:], reads=[hmTs[i][1]])
            fw.barrier()

        if upto >= 6:
          with contextlib.ExitStack() as sD:
            gfin = sbuf(sD, "gfin", [128, D], F32)
            mergedT = sbuf(sD, "mergedT", [128, 16, 1024], BF16)
            wpg = WPool(sD, "wpg", 2, 16 * 384)
            wpb = WPool(sD, "wpb", 2, 8 * 384)
            wpoa = WPool(sD, "wpoa", 2, 16 * 512)
            r_gfin = Res()

            def c1_pre(th):
                if th > 0:
                    fw.dma("sp", hTh[:], HT_d[:, th * 1024:(th + 1) * 1024].rearrange("(k p) t -> p k t", p=128), writes=[r_in])
                return {0: (wpg.load(wG_d[0], 16, 384), wpb.load(wBR_d[0], 8, 384))}

            wl_next = c1_pre(0)
            fw.dma("sp", gfin[:], gf_d, writes=[r_gfin])
            for th in range(2):
                t0 = th * 1024
                with contextlib.ExitStack() as c1:
                    hbs = [sbuf(c1, "hb%d" % b, [128, 8, 1024], BF16) for b in range(3)]
                    r_hb = [Res() for _ in range(3)]
                    sgs = mk(c1, "sg", 2, [128, 512], F32)
                    tmps = mk(c1, "tmpc", 2, [128, 512], F32)
                    acc = sbuf(c1, "acc", [128, 1024], F32)
                    r_acc = Res()
                    r_mT = Res()
                    wl = wl_next
                    for b, src in enumerate((HMT_d, HAT_d, HCT_d)):
                        fw.dma("sp", hbs[b][:], src[:, t0:t0 + 1024].rearrange("(k p) t -> p k t", p=128), writes=[r_hb[b]])
                    for c in range(16):
                        (wgv, wgr), (wbv, wbr) = wl.pop(c)
                        for b in range(3):
                            if b == 1 and c + 1 < 16:
                                wl[c + 1] = (wpg.load(wG_d[c + 1], 16, 384), wpb.load(wBR_d[c + 1], 8, 384))
                            if b == 2 and c == 13:
                                wos_pre = [wpoa.load(wOUT_d[n4], 16, 512) for n4 in range(2)]
                            gb = [nb(), nb()]
                            rb = [nb(), nb()]
                            for kc in range(16):
                                for tb in range(2):
                                    fw.op("pe", lambda e: e.matmul(gb[tb][0][:, 0:512], lhsT=wgv[:, kc, b * 128:(b + 1) * 128],
                                                                   rhs=hTh[:, kc, tb * 512:(tb + 1) * 512], start=(kc == 0), stop=(kc == 15)),
                                          reads=[wgr, r_in], writes=[gb[tb][1]])
                            for kc in range(8):
                                for tb in range(2):
                                    fw.op("pe", lambda e: e.matmul(rb[tb][0][:, 0:512], lhsT=wbv[:, kc, b * 128:(b + 1) * 128],
                                                                   rhs=hbs[b][:, kc, tb * 512:(tb + 1) * 512], start=(kc == 0), stop=(kc == 7)),
                                          reads=[wbr, r_hb[b]], writes=[rb[tb][1]])
                            for tb in range(2):
                                tc_ = slice(tb * 512, (tb + 1) * 512)
                                sg, r_sg = rot(sgs)
                                fw.op("act", lambda e: e.activation(out=sg[:], in_=gb[tb][0][:, 0:512], func=AF.Sigmoid), reads=[gb[tb][1]], writes=[r_sg])
                                if b == 0:
                                    fw.op("dve", lambda e: e.tensor_tensor(out=acc[:, tc_], in0=sg[:], in1=rb[tb][0][:, 0:512], op=ALU.mult),
                                          reads=[r_sg, rb[tb][1]], writes=[r_acc])
                                else:
                                    tmp, r_tmp = rot(tmps)
                                    fw.op("dve", lambda e: e.tensor_tensor(out=tmp[:], in0=sg[:], in1=rb[tb][0][:, 0:512], op=ALU.mult),
                                          reads=[r_sg, rb[tb][1]], writes=[r_tmp])
                                    if b == 1:
                                        fw.op("pool", lambda e: e.tensor_tensor(out=acc[:, tc_], in0=acc[:, tc_], in1=tmp[:], op=ALU.add),
                                              reads=[r_tmp, r_acc], writes=[r_acc])
                                    else:
                                        fw.op("pool", lambda e: e.tensor_tensor(out=mergedT[:, c, tc_], in0=acc[:, tc_], in1=tmp[:], op=ALU.add),
                                              reads=[r_tmp, r_acc], writes=[r_mT])
                    fw.barrier()
                with contextlib.ExitStack() as c2:
                    wpo = WPool(c2, "wpo", 2, 16 * 512)
                    wos = wos_pre + [wpo.load(wOUT_d[n4], 16, 512) for n4 in range(2, 4)]
                    xts = mk(c2, "xo", 3, [128, D], F32)
                    fw.dma("sp", xts[0][0][:], x_d[t0:t0 + 128, :], writes=[xts[0][1]])
                    jk = sbuf(c2, "jko", [128, D], BF16)
                    r_jk = Res()
                    stats = mk(c2, "stO", 2, [128, 4], F32)
                    for t8 in range(8):
                        xt, r_xt = xts[t8 % 3]
                        sc, r_sc = rot(stats)
                        rows = slice(t0 + t8 * 128, t0 + (t8 + 1) * 128)
                        if t8 + 1 < 8:
                            fw.dma("sp", xts[(t8 + 1) % 3][0][:], x_d[t0 + (t8 + 1) * 128:t0 + (t8 + 2) * 128, :], writes=[xts[(t8 + 1) % 3][1]])
                        for n4 in range(4):
                            wov, wor = wos[n4]
                            bk, r_bk = nb()
                            for kc in range(16):
                                fw.op("pe", lambda e: e.matmul(bk[:, 0:512], lhsT=mergedT[:, kc, t8 * 128:(t8 + 1) * 128], rhs=wov[:, kc, 0:512],
                                                               start=(kc == 0), stop=(kc == 15)),
                                      reads=[wor], writes=[r_bk])
                            fw.op("dve", lambda e: e.tensor_tensor(out=xt[:, n4 * 512:(n4 + 1) * 512], in0=bk[:, 0:512], in1=xt[:, n4 * 512:(n4 + 1) * 512], op=ALU.add),
                                  reads=[r_bk, r_xt], writes=[r_xt])
                        fw.op("act", lambda e: e.activation(out=jk[:], in_=xt[:], func=AF.Square, accum_out=sc[:, 0:1]), reads=[r_xt], writes=[r_jk, r_sc])
                        fw.op("act", lambda e: e.activation(out=sc[:, 1:2], in_=sc[:, 0:1], func=AF.Ln, scale=1.0 / D, bias=EPS), reads=[r_sc], writes=[r_sc])
                        fw.op("act", lambda e: e.activation(out=sc[:, 2:3], in_=sc[:, 1:2], func=AF.Exp, scale=-0.5), reads=[r_sc], writes=[r_sc])
                        fw.op("dve", lambda e: e.scalar_tensor_tensor(out=xt[:], in0=xt[:], scalar=sc[:, 2:3], in1=gfin[:], op0=ALU.mult, op1=ALU.mult),
                              reads=[r_xt, r_sc, r_gfin], writes=[r_xt])
                        fw.dma("sp", out_d[rows, :], xt[:], reads=[r_xt])
                        if th == 0 and t8 == 3:
                            wl_next = c1_pre(1)
                    fw.barrier()
        fw.final_wait("sp")
    return nc


def _tile_w(W):
    K, n = W.shape
    return np.ascontiguousarray(W.reshape(K // 128, 128, n).transpose(1, 0, 2))


def prep_shared(inp):
    f = np.float32
    w_in = np.asarray(inp["w_in"], f)[0]
    ar = np.arange
    blocks = []
    for h in range(4):
        blocks.append(np.concatenate([O_QK + h * 256 + ar(256), O_QK + 1024 + h * 256 + ar(256)]))
    for b in range(2):
        blocks.append(O_V + b * 512 + ar(512))
    for h in range(4):
        blocks.append(np.concatenate([O_O + h * 256 + ar(256), O_Z + h * 256 + ar(256)]))
    blocks.append(O_CQ + ar(512))
    blocks.append(O_CKV + ar(512))
    for b in range(2):
        blocks.append(O_AZ + b * 512 + ar(512))
    for b in range(2):
        blocks.append(O_CQC + b * 512 + ar(512))
    for b in range(2):
        blocks.append(O_CZ + b * 512 + ar(512))
    sh = {}
    sh["wA"] = np.stack([_tile_w(w_in[:, c]) for c in blocks])
    swap = np.concatenate([32 + ar(32), ar(32)])
    sh["wKR"] = _tile_w(w_in[:, np.concatenate([O_KR + ar(64), O_KR + swap])])
    sh["wIF"] = _tile_w(w_in[:, np.concatenate([O_I + ar(4), O_F + ar(4)])])
    w_uq = np.asarray(inp["w_uq"], f)[0]
    uq = []
    for pr in range(4):
        cols = []
        for hh in (2 * pr, 2 * pr + 1):
            cols += [hh * 192 + ar(128), hh * 192 + 128 + ar(64), hh * 192 + 128 + swap]
        uq.append(_tile_w(w_uq[:, np.concatenate(cols)]))
    sh["wUQ"] = np.stack(uq)
    w_ukv = np.asarray(inp["w_ukv"], f)[0]
    sh["wUK"] = np.stack([_tile_w(w_ukv[:, np.concatenate([hh * 256 + ar(128) for hh in range(4 * b, 4 * b + 4)])]) for b in range(2)])
    sh["wUV"] = np.stack([_tile_w(w_ukv[:, np.concatenate([hh * 256 + 128 + ar(128) for hh in range(4 * b, 4 * b + 4)])]) for b in range(2)])
    w_mem = np.asarray(inp["w_mem_kv"], f)[0]
    sh["wMEM"] = np.stack([_tile_w(w_mem[:, b * 512:(b + 1) * 512]) for b in range(4)])
    wbm = np.asarray(inp["w_br_m"], f)[0]
    wba = np.asarray(inp["w_br_a"], f)[0]
    wbc = np.asarray(inp["w_br_c"], f)[0]
    sh["wG"] = np.stack([_tile_w(w_in[:, np.concatenate([O_G + b * 2048 + c * 128 + ar(128) for b in range(3)])]) for c in range(16)])
    sh["wBR"] = np.stack([_tile_w(np.concatenate([w[:, c * 128:(c + 1) * 128] for w in (wbm, wba, wbc)], axis=1)) for c in range(16)])
    w_out = np.asarray(inp["w_out"], f)[0]
    sh["wOUT"] = np.stack([_tile_w(w_out[:, b * 512:(b + 1) * 512]) for b in range(4)])
    bc = lambda v, n=128: np.ascontiguousarray(np.broadcast_to(np.asarray(v, f).reshape(1, -1), (n, np.asarray(v).size)))
    sh["g_norm"] = bc(inp["norm"][0])
    sh["g_final"] = bc(inp["final_norm"])
    sh["g_mem"] = bc(inp["mem_norm"][0])
    sh["g_cq"] = bc(inp["cq_norm"][0])
    sh["g_ckv"] = bc(inp["ckv_norm"][0])
    sh["g_mh"] = bc(inp["mh_norm"][0])
    conv_w = np.asarray(inp["conv_w"], f)[0]
    conv_b = np.asarray(inp["conv_b"], f)[0]
    cw = np.zeros((128, 16, 4), f)
    cb = np.zeros((128, 16), f)
    for h in range(4):
        for ci in range(4):
            base = (0 if ci < 2 else 1024) + h * 256 + (ci % 2) * 128
            cw[:, 4 * h + ci, :] = conv_w[:, base:base + 128].T
            cb[:, 4 * h + ci] = conv_b[base:base + 128]
    sh["convw"] = cw
    sh["convb"] = cb
    sh["bif"] = bc(np.concatenate([np.asarray(inp["b_igate"], f)[0], np.asarray(inp["b_fgate"], f)[0]]))
    inv_freq = (np.float32(10000.0) ** (-(np.arange(0, 64, 2, dtype=np.float32)) / np.float32(64))).astype(f)
    sh["invf"] = np.concatenate([inv_freq, inv_freq]).reshape(64, 1).astype(f)
    sh["sgn"] = np.concatenate([-np.ones(32, f), np.ones(32, f)]).reshape(64, 1)
    sh["ident"] = np.eye(128, dtype=f)
    sh["mask"] = np.triu(np.ones((128, 128), f))
    return sh


def prep_core(inp, b):
    return {
        "x": np.ascontiguousarray(np.asarray(inp["x"], np.float32)[b]),
        "mem": np.ascontiguousarray(np.asarray(inp["mem"], np.float32)[b]),
        "pos64": np.ascontiguousarray(np.broadcast_to(np.asarray(inp["positions"], np.int32)[b].reshape(1, S), (64, S))),
    }


def kernel(**inputs):
    sh = prep_shared(inputs)
    nc = build()
    in_maps = []
    for b in range(8):
        m = dict(sh)
        m.update(prep_core(inputs, b))
        in_maps.append(m)
    res = run_bass_kernel_spmd(nc, in_maps, core_ids=list(range(8)))
    return np.stack([np.asarray(r["out"], np.float32) for r in res.results], axis=0)
```
